# Optimizing a Trainium2 kernel written in Bass

```python
import jax, jax.numpy as jnp
from jax import lax
import numpy as np

D_MODEL = 1024
BATCH = 8
SEQ = 8192
DEPTH = 1
DEC_BATCH = 16
DEC_SEQ = 32
PAST_LEN = 2048

CHUNK = 64
N_HEADS = 8
HEAD_DIM = 128
N_KV_HEADS = 2
ATTN_DIM = N_HEADS * HEAD_DIM
KV_DIM = N_KV_HEADS * HEAD_DIM
IDX_HEADS = 8
IDX_DIM = 64
TOPK_MAX = 256
CONV_DIM = D_MODEL
CONV_WIDTH = 31
Q_BLOCK = 128
EPS = 1e-6
IN_SIZES = (ATTN_DIM, KV_DIM, KV_DIM, ATTN_DIM, IDX_HEADS * IDX_DIM, IDX_HEADS, IDX_DIM,
            2 * CONV_DIM, CONV_DIM, 2 * D_MODEL)
N_IN = 2 * ATTN_DIM + 2 * KV_DIM + IDX_HEADS * IDX_DIM + IDX_HEADS + IDX_DIM + 3 * CONV_DIM + 2 * D_MODEL

kernel_name = "dsa_conformer_griffin_stream_step"


def rms_norm(x, g):
    xf = x.astype(jnp.float32)
    y = xf * lax.rsqrt(jnp.mean(xf * xf, axis=-1, keepdims=True) + EPS)
    return (y * g.astype(jnp.float32)).astype(x.dtype)


def layer_norm(x, g, b):
    xf = x.astype(jnp.float32)
    mu = jnp.mean(xf, axis=-1, keepdims=True)
    var = jnp.mean(jnp.square(xf - mu), axis=-1, keepdims=True)
    y = (xf - mu) * lax.rsqrt(var + EPS) * g.astype(jnp.float32) + b.astype(jnp.float32)
    return y.astype(x.dtype)


def split_in(z):
    offsets = np.cumsum(IN_SIZES)[:-1].tolist()
    return jnp.split(z, offsets, axis=-1)


def sparse_attend(q, iq, iw, qpos, k, v, ik, topk):
    B, Q = q.shape[0], q.shape[1]
    L = k.shape[1]
    kpos = jnp.arange(L)
    qchunk = qpos // CHUNK
    adm = (kpos // CHUNK)[None, :] <= qchunk[:, None]
    dots = jnp.einsum('bqhd,bsd->bqhs', iq.astype(jnp.float32), ik.astype(jnp.float32)) * (IDX_DIM ** -0.5)
    w = iw.astype(jnp.float32) * (IDX_HEADS ** -0.5)
    score = jnp.einsum('bqh,bqhs->bqs', w, jax.nn.relu(dots))
    score = jnp.where(adm[None], score, -jnp.inf)
    _, idx = lax.top_k(score, topk)
    valid = (idx // CHUNK) <= qchunk[None, :, None]
    gather = jax.vmap(lambda rows, ids: rows[ids])
    ksel = gather(k, idx)
    vsel = gather(v, idx)
    qg = q.reshape(B, Q, N_KV_HEADS, N_HEADS // N_KV_HEADS, HEAD_DIM)
    s = jnp.einsum('bqgrd,bqkgd->bqgrk', qg.astype(jnp.float32), ksel.astype(jnp.float32)) * (HEAD_DIM ** -0.5)
    s = jnp.where(valid[:, :, None, None, :], s, -jnp.inf)
    p = jax.nn.softmax(s, axis=-1)
    o = jnp.einsum('bqgrk,bqkgd->bqgrd', p.astype(v.dtype), vsel)
    return o.reshape(B, Q, ATTN_DIM)


def attend_blocks(q, iq, iw, qpos, k, v, ik, topk):
    B, T = q.shape[0], q.shape[1]
    if T > Q_BLOCK and T % Q_BLOCK == 0:
        nb = T // Q_BLOCK

        def to_blocks(a):
            return jnp.moveaxis(a.reshape(a.shape[0], nb, Q_BLOCK, *a.shape[2:]), 1, 0)

        xs = (to_blocks(q), to_blocks(iq), to_blocks(iw), qpos.reshape(nb, Q_BLOCK))
        out = lax.map(lambda blk: sparse_attend(blk[0], blk[1], blk[2], blk[3], k, v, ik, topk), xs)
        return jnp.moveaxis(out, 0, 1).reshape(B, T, ATTN_DIM)
    return sparse_attend(q, iq, iw, qpos, k, v, ik, topk)


def mixer_layer(x, c, past_k, past_v, past_ik, conv_state,
                w_ada, b_ada, g_norm, w_in, w_a, w_dw, b_dw, g_ln, b_ln, w_b, w_out):
    B, T, _ = x.shape
    P = past_k.shape[1]
    mod = jax.nn.silu(c) @ w_ada + b_ada
    shift, scale, gate = jnp.split(mod, 3, axis=-1)
    h = rms_norm(x, g_norm) * (1 + scale[:, None, :]) + shift[:, None, :]
    z = h @ w_in
    q, k, v, gate_a, iq, iw, ik, glu, gate_b, merge = split_in(z)
    q = q.reshape(B, T, N_HEADS, HEAD_DIM)
    k = k.reshape(B, T, N_KV_HEADS, HEAD_DIM)
    v = v.reshape(B, T, N_KV_HEADS, HEAD_DIM)
    iq = iq.reshape(B, T, IDX_HEADS, IDX_DIM)
    k_all = jnp.concatenate([past_k, k], axis=1)
    v_all = jnp.concatenate([past_v, v], axis=1)
    ik_all = jnp.concatenate([past_ik, ik], axis=1)
    L = P + T
    topk = min(TOPK_MAX, L // 4)
    qpos = P + jnp.arange(T)
    o_a = attend_blocks(q, iq, iw, qpos, k_all, v_all, ik_all, topk)
    y_a = (o_a * jax.nn.silu(gate_a)) @ w_a
    ga, gb = jnp.split(glu, 2, axis=-1)
    u = ga * jax.nn.sigmoid(gb)
    u_ext = jnp.concatenate([conv_state, u], axis=1)
    dw = lax.conv_general_dilated(u_ext, w_dw[:, None, :], (1,), 'VALID',
                                  dimension_numbers=('NWC', 'WIO', 'NWC'),
                                  feature_group_count=CONV_DIM) + b_dw
    n = layer_norm(dw, g_ln, b_ln)
    y_b = (jax.nn.silu(n) * jax.nn.silu(gate_b)) @ w_b
    m_a, m_b = jnp.split(merge, 2, axis=-1)
    mixed = jax.nn.sigmoid(m_a) * y_a + jax.nn.sigmoid(m_b) * y_b
    x = x + gate[:, None, :] * (mixed @ w_out)
    new_conv = u_ext[:, -(CONV_WIDTH - 1):]
    return x, k, v, ik, new_conv


def setup_inputs(seed: int = 0) -> dict:
    key = jax.random.key(seed)
    ks = jax.random.split(key, 20)
    nrm = jax.random.normal
    f32 = jnp.float32
    sd = D_MODEL ** -0.5
    return {
        "x_prompt": nrm(ks[0], (BATCH, SEQ, D_MODEL), f32),
        "x_sample": nrm(ks[1], (DEC_BATCH, DEC_SEQ, D_MODEL), f32),
        "cache_k": nrm(ks[2], (DEPTH, DEC_BATCH, PAST_LEN, N_KV_HEADS, HEAD_DIM), f32),
        "cache_v": nrm(ks[3], (DEPTH, DEC_BATCH, PAST_LEN, N_KV_HEADS, HEAD_DIM), f32),
        "cache_idx_k": nrm(ks[4], (DEPTH, DEC_BATCH, PAST_LEN, IDX_DIM), f32),
        "state_conv": 0.5 * nrm(ks[5], (DEPTH, DEC_BATCH, CONV_WIDTH - 1, CONV_DIM), f32),
        "c_prompt": nrm(ks[6], (BATCH, D_MODEL), f32),
        "c_sample": nrm(ks[7], (DEC_BATCH, D_MODEL), f32),
        "w_ada": 0.1 * sd * nrm(ks[8], (DEPTH, D_MODEL, 3 * D_MODEL), f32),
        "b_ada": 0.01 * nrm(ks[9], (DEPTH, 3 * D_MODEL), f32),
        "g_norm": 1.0 + 0.01 * nrm(ks[10], (DEPTH, D_MODEL), f32),
        "w_in": sd * nrm(ks[11], (DEPTH, D_MODEL, N_IN), f32),
        "w_a": (ATTN_DIM ** -0.5) * nrm(ks[12], (DEPTH, ATTN_DIM, D_MODEL), f32),
        "w_dw": (CONV_WIDTH ** -0.5) * nrm(ks[13], (DEPTH, CONV_WIDTH, CONV_DIM), f32),
        "b_dw": 0.01 * nrm(ks[14], (DEPTH, CONV_DIM), f32),
        "g_ln": 1.0 + 0.01 * nrm(ks[15], (DEPTH, CONV_DIM), f32),
        "b_ln": 0.01 * nrm(ks[16], (DEPTH, CONV_DIM), f32),
        "w_b": (CONV_DIM ** -0.5) * nrm(ks[17], (DEPTH, CONV_DIM, D_MODEL), f32),
        "w_out": sd * nrm(ks[18], (DEPTH, D_MODEL, D_MODEL), f32),
        "g_final": 1.0 + 0.01 * nrm(ks[19], (D_MODEL,), f32),
    }


def reference(x_prompt, x_sample, cache_k, cache_v, cache_idx_k, state_conv, c_prompt, c_sample,
              w_ada, b_ada, g_norm, w_in, w_a, w_dw, b_dw, g_ln, b_ln, w_b, w_out, g_final):
    xp, xs = x_prompt, x_sample
    bp = xp.shape[0]
    kp_l, vp_l, ikp_l, cp_l, ks_l, vs_l, iks_l, cs_l = [], [], [], [], [], [], [], []
    for l in range(DEPTH):
        lw = (w_ada[l], b_ada[l], g_norm[l], w_in[l], w_a[l], w_dw[l], b_dw[l], g_ln[l], b_ln[l], w_b[l], w_out[l])
        empty_kv = jnp.zeros((bp, 0, N_KV_HEADS, HEAD_DIM), xp.dtype)
        empty_ik = jnp.zeros((bp, 0, IDX_DIM), xp.dtype)
        zero_conv = jnp.zeros((bp, CONV_WIDTH - 1, CONV_DIM), xp.dtype)
        xp, kp, vp, ikp, cp = mixer_layer(xp, c_prompt, empty_kv, empty_kv, empty_ik, zero_conv, *lw)
        xs, k_s, v_s, ik_s, c_s = mixer_layer(xs, c_sample, cache_k[l], cache_v[l], cache_idx_k[l], state_conv[l], *lw)
        kp_l.append(kp); vp_l.append(vp); ikp_l.append(ikp); cp_l.append(cp)
        ks_l.append(k_s); vs_l.append(v_s); iks_l.append(ik_s); cs_l.append(c_s)
    y_prompt = rms_norm(xp, g_final)
    y_sample = rms_norm(xs, g_final)
    return (y_prompt, y_sample,
            jnp.stack(kp_l), jnp.stack(vp_l), jnp.stack(ikp_l), jnp.stack(cp_l),
            jnp.stack(ks_l), jnp.stack(vs_l), jnp.stack(iks_l), jnp.stack(cs_l))
```

```python
import numpy as np
from contextlib import ExitStack
import concourse.bass as bass
import concourse.mybir as mybir
from concourse.bass_utils import run_bass_kernel_spmd

F32 = mybir.dt.float32
BF16 = mybir.dt.bfloat16
AF = mybir.ActivationFunctionType
ALU = mybir.AluOpType
AX = mybir.AxisListType

D = 1024
NIN = 8264
CHUNK = 64
NITER = 22
import os
ACT_FRAC = float(os.environ.get("ACT_FRAC", "0.0"))
NEG = -1.0e30
MASKV = -30000.0

O_Q, O_K, O_V, O_GA, O_IQ, O_IW, O_IK, O_GLU, O_GB, O_M = 0, 1024, 1280, 1536, 2560, 3072, 3080, 3144, 5192, 6216
FM_KINDS = ["q", "ga", "ua", "ub", "gb", "ma", "mb"]
FM_OFF = {"q": O_Q, "ga": O_GA, "ua": O_GLU, "ub": O_GLU + 1024, "gb": O_GB, "ma": O_M, "mb": O_M + 1024}


class Sem:
    def __init__(self, k, name):
        self.h = k.stack0.enter_context(k.nc.semaphore(name))
        self.v = 0


class Buf:
    def __init__(self, t, disjoint=False):
        self.t = t
        self.w = {}
        self.r = {}
        self.disjoint = disjoint

    def __getitem__(self, idx):
        return self.t[idx]


def _merge(d, tok):
    if tok is None:
        return
    s, v = tok
    if id(s) not in d or d[id(s)][1] < v:
        d[id(s)] = (s, v)


class Eng:
    def __init__(self, k, e, name, same=True):
        self.k = k
        self.e = e
        self.name = name
        self.sem = Sem(k, "e_" + name)
        self.seen = {}
        self.same = same
        self.pending = []

    def wait_tok(self, s, v):
        if s is self.sem and not self.same:
            return
        if self.seen.get(id(s), 0) < v:
            self.e.wait_ge(s.h, v)
            self.seen[id(s)] = v

    def wait_bufs(self, reads, writes):
        for b in reads:
            for s, v in b.w.values():
                self.wait_tok(s, v)
        for b in writes:
            if b.disjoint:
                continue
            for s, v in b.w.values():
                self.wait_tok(s, v)
            for s, v in b.r.values():
                self.wait_tok(s, v)


class K:
    def __init__(self, nc):
        self.nc = nc
        self.stack0 = ExitStack()
        self.pe = Eng(self, nc.tensor, "pe", same=False)
        self.act = Eng(self, nc.scalar, "act")
        self.dve = Eng(self, nc.vector, "dve")
        self.pool = Eng(self, nc.gpsimd, "pool")
        self.sp = Eng(self, nc.sync, "sp")
        self.engs = [self.pe, self.act, self.dve, self.pool, self.sp]
        self.dsems = [Sem(self, "d%d" % i) for i in range(24)]
        self.di = 0
        self.nins = 0

    def _post(self, E, tok, reads, writes):
        for b in writes:
            if not b.disjoint:
                b.w = {}
                b.r = {}
            _merge(b.w, tok)
        for b in reads:
            _merge(b.r, tok)

    def do(self, E, fn, reads, writes, *a, sig=True, **kw):
        E.wait_bufs(reads, writes)
        ins = getattr(E.e, fn)(*a, **kw)
        self.nins += 1
        if sig:
            E.sem.v += 1
            ins.then_inc(E.sem.h, 1)
            tok = (E.sem, E.sem.v)
            for (rr, ww) in E.pending:
                self._post(E, tok, rr, ww)
            E.pending = []
            self._post(E, tok, reads, writes)
            return tok
        E.pending.append((reads, writes))
        return None

    def dma(self, out, in_, reads, writes, E=None, **kw):
        E = E or self.sp
        E.wait_bufs(reads, writes)
        s = self.dsems[self.di % len(self.dsems)]
        self.di += 1
        if s.v > 0:
            E.wait_tok(s, s.v)
        s.v += 16
        E.e.dma_start(out=out, in_=in_, **kw).then_inc(s.h, 16)
        self.nins += 1
        tok = (s, s.v)
        for b in writes:
            _merge(b.w, tok)
        for b in reads:
            _merge(b.r, tok)
        return tok

    def barrier(self):
        sems = [e.sem for e in self.engs] + self.dsems
        for E in [self.pe, self.act, self.dve, self.pool, self.sp]:
            for s in sems:
                if s.v > 0 and s is not E.sem:
                    E.wait_tok(s, s.v)

    def final_wait(self):
        for s in self.dsems:
            if s.v > 0:
                self.sp.wait_tok(s, s.v)
        for e in self.engs:
            if e is not self.sp and e.sem.v > 0:
                self.sp.wait_tok(e.sem, e.sem.v)


class Cfg:
    def __init__(self, TP=8192, PAST=2048, TS=32, NS=2, topk_max=256, debug=False):
        self.TP, self.PAST, self.TS, self.NS = TP, PAST, TS, NS
        self.debug = debug
        self.NTOK = TP + NS * TS
        self.topk_p = min(topk_max, TP // 4)
        self.topk_s = min(topk_max, (PAST + TS) // 4)
        self.LS = PAST + TS
        self.LMAX = max(TP, ((self.LS + 127) // 128) * 128)
        self.NKT = self.LMAX // 128


def build(cfg):
    TP, NTOK, TS, NS, PAST = cfg.TP, cfg.NTOK, cfg.TS, cfg.NS, cfg.PAST
    SROWS = NS * TS
    nc = bass.Bass("TRN2", target_bir_lowering=False)
    k = K(nc)

    def din(name, shape, dt=F32):
        return nc.dram_tensor(name, list(shape), dt, kind="ExternalInput").ap()

    def dout(name, shape, dt=F32):
        return nc.dram_tensor(name, list(shape), dt, kind="ExternalOutput").ap()

    def dscr(name, shape, dt=BF16):
        return Buf(nc.dram_tensor(name, list(shape), dt, kind="Internal").ap(), disjoint=True)

    x_all = din("x_all", [NTOK, D])
    cT_d = din("cT", [128, 8, 3])
    wada_d = din("wada", [128, 8, 3072])
    bada_d = din("bada", [3, 3072])
    gn_d = din("gn_bc", [128, D])
    wfm_d = din("wfm", [56, 128, 8 * 128])
    wiq_d = din("wiq", [8, 128, 8 * 64])
    wkv_d = din("wkv", [128, 8 * 584])
    wa_d = din("wa", [128, 8 * D])
    wb_d = din("wb", [128, 8 * D])
    wo_d = din("wo", [128, 8 * D])
    wdw_d = din("wdwT", [128, 8 * 31])
    vec_d = din("vecT", [128, 3 * 8])
    gf_d = din("gf_bc", [128, D])
    ck_d = din("ck", [NS, PAST, 256])
    cv_d = din("cv", [NS, PAST, 256])
    cik_d = din("cik", [NS, PAST, 64])
    sconv_d = din("sconvT", [NS, 128, 8 * 30])
    const_d = din("consts", [128, 128 + 192 + 512 + 128])

    y_all = Buf(disjoint=True, t=dout("y_all", [NTOK, D]))
    k_all = Buf(disjoint=True, t=dout("k_all", [NTOK, 256]))
    v_all = Buf(disjoint=True, t=dout("v_all", [NTOK, 256]))
    ik_all = Buf(disjoint=True, t=dout("ik_all", [NTOK, 64]))
    conv_o = Buf(disjoint=True, t=dout("conv_o", [1 + NS, 30, D]))

    S = {kind: dscr("s_" + kind, [8, 128, NTOK]) for kind in FM_KINDS}
    if cfg.debug:
        S["og"] = Buf(nc.dram_tensor("s_og", [8, 128, NTOK], BF16, kind="ExternalOutput").ap(), disjoint=True)
    else:
        S["og"] = dscr("s_og", [8, 128, NTOK])
    S_iq = dscr("s_iq", [8, 64, NTOK])
    S_iw = dscr("s_iw", [NTOK, 8], F32)

    def fm_view(b, t0, n):
        return b.t.rearrange("h d t -> d h t")[:, :, t0:t0 + n]

    P0 = ExitStack()

    uniq = [0]

    def sb(st, name, shape, dt):
        uniq[0] += 1
        return Buf(st.enter_context(nc.sbuf_tensor("%s_u%d" % (name, uniq[0]), list(shape), dt)))

    def ps(st, name, shape, dt):
        uniq[0] += 1
        return Buf(st.enter_context(nc.psum_tensor("%s_u%d" % (name, uniq[0]), list(shape), dt)))

    identf = sb(P0, "identf", [128, 128], F32)
    identb = sb(P0, "identb", [128, 128], BF16)
    G_p = Buf(nc.dram_tensor("g_p", [128, D], F32, kind="Internal").ap(), disjoint=True)
    G_s = [Buf(nc.dram_tensor("g_s%d" % i, [TS, D], F32, kind="Internal").ap(), disjoint=True) for i in range(NS)]
    epsT = sb(P0, "epsT", [128, 1], F32)
    onesb = sb(P0, "onesb", [128, 128], BF16)
    I4p = sb(P0, "I4p", [128, 512], BF16)
    I4s = sb(P0, "I4s", [32, 128], BF16)
    PC = ExitStack()
    cst = sb(PC, "cst", [128, 128 + 192 + 512 + 128], F32)

    k.dma(cst[:], const_d, [], [cst])
    k.do(k.dve, "tensor_copy", [cst], [identf], out=identf[:], in_=cst[:, 0:128])
    k.do(k.dve, "tensor_copy", [cst], [identb], out=identb[:], in_=cst[:, 0:128])
    k.do(k.dve, "memset", [], [epsT], epsT[:], 1e-6)
    k.do(k.dve, "memset", [], [onesb], onesb[:], 1.0)
    k.do(k.dve, "tensor_copy", [cst], [I4p], out=I4p[:], in_=cst[:, 320:832])
    k.do(k.dve, "tensor_copy", [cst], [I4s], out=I4s[:], in_=cst[0:32, 832:960])
    sel = lambda a, b_: cst[0:3, 128 + a:128 + b_]

    P1 = ExitStack()
    hT = sb(P1, "hT", [128, 8, NTOK], BF16)
    P1a = ExitStack()
    AB_p = sb(P1a, "AB_p", [128, 2 * D], F32)
    AB_s = sb(P1a, "AB_s", [SROWS, 2 * D], F32)
    with ExitStack() as st:
        cT = sb(st, "cT_t", [128, 8, 3], F32)
        sc = sb(st, "sc_t", [128, 8, 3], F32)
        modrows = sb(st, "modrows", [3, 3072], F32)
        gn = sb(st, "gn", [128, D], F32)
        wat = [sb(st, "wat%d" % i, [128, 8, 256], F32) for i in range(2)]
        pm = ps(st, "pm", [128, 512], F32)
        pbp = [ps(st, "pbp%d" % i, [128, 512], F32) for i in range(2)]
        k.dma(cT[:], cT_d, [], [cT])
        k.dma(modrows[:], bada_d, [], [modrows])
        k.dma(gn[:], gn_d, [], [gn])
        k.do(k.act, "activation", [cT], [sc], out=sc[:], in_=cT[:], func=AF.Silu)
        wview = wada_d
        for i in range(12):
            w = wat[i % 2]
            k.dma(w[:], wview[:, :, i * 256:(i + 1) * 256], [], [w])
            for kc in range(8):
                k.do(k.pe, "matmul", [sc, w], [pm], pm[0:3, 0:256], sc[:, kc, :], w[:, kc, :], start=(kc == 0), stop=(kc == 7), sig=(kc == 7))
            k.do(k.dve, "tensor_tensor", [pm, modrows], [modrows], out=modrows[:, i * 256:(i + 1) * 256], in0=pm[0:3, 0:256], in1=modrows[:, i * 256:(i + 1) * 256], op=ALU.add)
        gstg = [sb(st, "gstg%d" % i, [128, 512], F32) for i in range(2)]
        targets = [((0, 128), 128, AB_p, G_p), ((128, 128 + SROWS), SROWS, AB_s, None)]
        for si in range(NS):
            targets.append(((128 + si * 32, 128 + si * 32 + TS), TS, None, G_s[si]))
        bi = 0
        for (sc0, sc1), M, AB, gt in targets:
            for part in range(3):
                if part < 2 and AB is None:
                    continue
                if part == 2 and gt is None:
                    continue
                for half in range(2):
                    pb = pbp[bi % 2]
                    bi += 1
                    k.do(k.pe, "matmul", [cst, modrows], [pb], pb[0:M, :], sel(sc0, sc1), modrows[:, part * 1024 + half * 512: part * 1024 + half * 512 + 512], start=True, stop=True)
                    cs = slice(half * 512, half * 512 + 512)
                    if part == 1:
                        k.do(k.dve, "scalar_tensor_tensor", [pb, gn], [AB], out=AB[0:M, cs], in0=pb[0:M, :], scalar=1.0, in1=gn[0:M, cs], op0=ALU.add, op1=ALU.mult)
                    elif part == 0:
                        k.do(k.act, "copy", [pb], [AB], out=AB[0:M, D + half * 512: D + half * 512 + 512], in_=pb[0:M, :])
                    else:
                        g_ = gstg[bi % 2]
                        k.do(k.act, "copy", [pb], [g_], out=g_[0:M, :], in_=pb[0:M, :])
                        k.dma(gt.t[0:M, cs], g_[0:M, :], [g_], [gt])
        k.barrier()

    NT = (NTOK + 127) // 128
    with ExitStack() as st:
        xb = [sb(st, "xb%d" % i, [128, D], F32) for i in range(3)]
        h1 = [sb(st, "h1_%d" % i, [128, D], F32) for i in range(2)]
        hb = [sb(st, "hb%d" % i, [128, D], BF16) for i in range(2)]
        junk = sb(st, "junk1", [128, D], F32)
        stats = [sb(st, "stats%d" % i, [128, 4], F32) for i in range(3)]
        pT = [ps(st, "pT%d" % i, [128, 8, 128], BF16) for i in range(2)]
        for tt in range(NT):
            t0 = tt * 128
            rows = min(128, NTOK - t0)
            AB = AB_p if t0 < TP else AB_s
            x = xb[tt % 3]
            s_ = stats[tt % 3]
            k.dma(x[0:rows, :], x_all[t0:t0 + rows, :], [], [x])
            k.do(k.act, "activation", [x], [junk, s_], out=junk[0:rows, :], in_=x[0:rows, :], func=AF.Square, accum_out=s_[0:rows, 0:1])
            k.do(k.act, "activation", [s_, epsT], [s_], out=s_[0:rows, 1:2], in_=s_[0:rows, 0:1], func=AF.Sqrt, scale=1.0 / D, bias=epsT[0:rows, 0:1])
            k.do(k.dve, "reciprocal", [s_], [s_], out=s_[0:rows, 2:3], in_=s_[0:rows, 1:2])
            h_ = h1[tt % 2]
            k.do(k.dve, "scalar_tensor_tensor", [x, s_, AB], [h_], out=h_[0:rows, :], in0=x[0:rows, :], scalar=s_[0:rows, 2:3], in1=AB[0:rows, 0:D], op0=ALU.mult, op1=ALU.mult)
            hb_ = hb[tt % 2]
            k.do(k.dve, "tensor_tensor", [h_, AB], [hb_], out=hb_[0:rows, :], in0=h_[0:rows, :], in1=AB[0:rows, D:2 * D], op=ALU.add)
            p = pT[tt % 2]
            for c in range(8):
                k.do(k.pe, "transpose", [hb_, identb], [p], p[:, c, 0:rows], hb_[0:rows, c * 128:(c + 1) * 128], identb[0:rows, 0:rows], sig=(c == 7))
            k.do(k.act, "copy", [p], [hT], out=hT[:, :, t0:t0 + rows], in_=p[:, :, 0:rows])
        k.barrier()
    P1a.close()

    groups = [(g0, min(512, NTOK - g0)) for g0 in range(0, NTOK, 512)]
    with ExitStack() as st:
        wkvb = sb(st, "wkvb", [128, 8, 584], BF16)
        kvst = [sb(st, "kvst%d" % i, [128, 584], F32) for i in range(2)]
        wst = [sb(st, "wst%d" % i, [128, 8 * 128], F32) for i in range(2)]
        wbf = [sb(st, "wbf%d" % i, [128, 8, 128], BF16) for i in range(3)]
        stg = [sb(st, "stg%d" % i, [128, 512], BF16) for i in range(3)]
        sig_ = [sb(st, "sig%d" % i, [128, 512], F32) for i in range(2)]
        cstg = [sb(st, "cstg%d" % i, [64, 256], F32) for i in range(2)]
        pz = [ps(st, "pz%d" % i, [128, 512], F32) for i in range(4)]
        pk0 = ps(st, "pk0", [128, 512], F32)
        pk1 = ps(st, "pk1", [128, 512], F32)
        pc = ps(st, "pcv", [128, 512], F32)
        for kc in range(8):
            b_ = kvst[kc % 2]
            k.dma(b_[:], wkv_d[:, kc * 584:(kc + 1) * 584], [], [b_])
            k.do(k.pool, "tensor_copy", [b_], [wkvb], out=wkvb[:, kc, :], in_=b_[:])
        for tt in range(NT):
            t0 = tt * 128
            rows = min(128, NTOK - t0)
            for kc in range(8):
                k.do(k.pe, "matmul", [hT, wkvb], [pk0], pk0[0:rows, :], hT[:, kc, t0:t0 + rows], wkvb[:, kc, 0:512], start=(kc == 0), stop=(kc == 7), sig=False)
            for kc in range(8):
                k.do(k.pe, "matmul", [hT, wkvb], [pk1], pk1[0:rows, 0:72], hT[:, kc, t0:t0 + rows], wkvb[:, kc, 512:584], start=(kc == 0), stop=(kc == 7), sig=(kc == 7))
            o_ = kvst[tt % 2]
            k.do(k.act, "copy", [pk0], [o_], out=o_[0:rows, 0:512], in_=pk0[0:rows, :])
            k.do(k.dve, "tensor_copy", [pk1], [o_], out=o_[0:rows, 512:584], in_=pk1[0:rows, 0:72])
            k.dma(k_all.t[t0:t0 + rows, :], o_[0:rows, 0:256], [o_], [k_all])
            k.dma(v_all.t[t0:t0 + rows, :], o_[0:rows, 256:512], [o_], [v_all])
            k.dma(ik_all.t[t0:t0 + rows, :], o_[0:rows, 512:576], [o_], [ik_all])
            k.dma(S_iw.t[t0:t0 + rows, :], o_[0:rows, 576:584], [o_], [S_iw])

        wi = [0]
        zi = [0]
        si_ = [0]

        def load_w(src_ap, width):
            i = wi[0]
            wi[0] += 1
            ws = wst[i % 2]
            wb_ = wbf[i % 3]
            k.dma(ws[:, 0:8 * width], src_ap, [], [ws])
            k.do(k.pool, "tensor_copy", [ws], [wb_], out=wb_[:, :, 0:width], in_=ws[:, 0:8 * width].rearrange("p (a b) -> p a b", b=width))
            return wb_

        def proj(wb_, width, g0, n):
            p = pz[zi[0] % 4]
            zi[0] += 1
            for kc in range(8):
                k.do(k.pe, "matmul", [hT, wb_], [p], p[0:width, 0:n], wb_[:, kc, 0:width], hT[:, kc, g0:g0 + n], start=(kc == 0), stop=(kc == 7), sig=(kc == 7))
            return p

        def single(kind, c, func, scale=1.0):
            wb_ = load_w(wfm_d[FM_KINDS.index(kind) * 8 + c], 128)
            for (g0, n) in groups:
                p = proj(wb_, 128, g0, n)
                o_ = stg[si_[0] % 3]
                si_[0] += 1
                k.do(k.act, "activation", [p], [o_], out=o_[:, 0:n], in_=p[:, 0:n], func=func, scale=scale)
                k.dma(S[kind].t[c, :, g0:g0 + n], o_[:, 0:n], [o_], [S[kind]])

        for c in range(8):
            single("q", c, AF.Copy, scale=128.0 ** -0.5)
        for h in range(8):
            wb_ = load_w(wiq_d[h], 64)
            for (g0, n) in groups:
                p = proj(wb_, 64, g0, n)
                o_ = stg[si_[0] % 3]
                si_[0] += 1
                k.do(k.dve, "tensor_copy", [p], [o_], out=o_[0:64, 0:n], in_=p[0:64, 0:n])
                k.dma(S_iq.t[h, :, g0:g0 + n], o_[0:64, 0:n], [o_], [S_iq])
        crow_sets = [(TP - 32, 32, [(0, 2, 32)]), (TP, SROWS, [(1 + s, TS * s + 2, TS * s + TS) for s in range(NS)])]
        ci = 0
        for c in range(8):
            wA = load_w(wfm_d[FM_KINDS.index("ua") * 8 + c], 128)
            wB = load_w(wfm_d[FM_KINDS.index("ub") * 8 + c], 128)
            for (g0, n) in groups:
                pA = proj(wA, 128, g0, n)
                pB = proj(wB, 128, g0, n)
                sg = sig_[si_[0] % 2]
                o_ = stg[si_[0] % 3]
                si_[0] += 1
                k.do(k.act, "activation", [pB], [sg], out=sg[:, 0:n], in_=pB[:, 0:n], func=AF.Sigmoid)
                k.do(k.dve, "tensor_tensor", [pA, sg], [o_], out=o_[:, 0:n], in0=pA[:, 0:n], in1=sg[:, 0:n], op=ALU.mult)
                k.dma(S["ua"].t[c, :, g0:g0 + n], o_[:, 0:n], [o_], [S["ua"]])
            for (r0, M, outs) in crow_sets:
                for kc in range(8):
                    k.do(k.pe, "matmul", [hT, wA], [pc], pc[0:M, 0:128], hT[:, kc, r0:r0 + M], wA[:, kc, :], start=(kc == 0), stop=(kc == 7), sig=False)
                for kc in range(8):
                    k.do(k.pe, "matmul", [hT, wB], [pc], pc[0:M, 128:256], hT[:, kc, r0:r0 + M], wB[:, kc, :], start=(kc == 0), stop=(kc == 7), sig=(kc == 7))
                cs_ = cstg[ci % 2]
                ci += 1
                k.do(k.act, "activation", [pc], [cs_], out=cs_[0:M, 128:256], in_=pc[0:M, 128:256], func=AF.Sigmoid)
                k.do(k.dve, "tensor_tensor", [pc, cs_], [cs_], out=cs_[0:M, 0:128], in0=pc[0:M, 0:128], in1=cs_[0:M, 128:256], op=ALU.mult)
                for (job, ra, rb) in outs:
                    k.dma(conv_o.t[job, :, c * 128:(c + 1) * 128], cs_[ra:rb, 0:128], [cs_], [conv_o])
        for c in range(8):
            single("ga", c, AF.Silu)
        for c in range(8):
            single("gb", c, AF.Silu)
        for c in range(8):
            single("ma", c, AF.Sigmoid)
        for c in range(8):
            single("mb", c, AF.Sigmoid)
        k.barrier()
    P1.close()
    PC.close()

    LMAX, NKT = cfg.LMAX, cfg.NKT
    with ExitStack() as st:
        Vc = sb(st, "Vc", [128, NKT, 256], BF16)
        kT = sb(st, "kT", [128, 2, LMAX], BF16)
        ikT = sb(st, "ikT", [64, LMAX], BF16)
        score = [sb(st, "score%d" % i, [128, LMAX], F32) for i in range(2)]
        mbias = [sb(st, "mbias%d" % i, [128, LMAX], BF16) for i in range(2)]
        pw = sb(st, "pw", [128, NITER + 1], F32)
        for i in range(NITER + 1):
            k.do(k.dve, "memset", [], [pw], pw[:, i:i + 1], 2.0 ** -(i + 1))
        psD = psSc = psS = psO = psN = psT = psI = None
        W = {}

        def build_cache(srcs):
            kst, vst, ist, kbb, ibb = W["kst"], W["vst"], W["ist"], W["kbb"], W["ibb"]
            kpos = 0
            bi = 0
            for (kap, vap, iap, rows, nt, deps) in srcs:
                a = bi % 2
                bi += 1
                ks_, vs_, is_, kb_, ib_ = kst[a], vst[a], ist[a], kbb[a], ibb[a]
                if nt > 1 or rows == 128:
                    k.dma(ks_[:, 0:nt, :], kap.rearrange("(a p) c -> p a c", p=128), deps, [ks_])
                    k.dma(vs_[:, 0:nt, :], vap.rearrange("(a p) c -> p a c", p=128), deps, [vs_])
                    k.dma(is_[:, 0:nt, :], iap.rearrange("(a p) c -> p a c", p=128), deps, [is_])
                else:
                    k.dma(ks_[0:rows, 0, :], kap, deps, [ks_])
                    k.dma(vs_[0:rows, 0, :], vap, deps, [vs_])
                    k.dma(is_[0:rows, 0, :], iap, deps, [is_])
                kt0 = kpos // 128
                k.do(k.pool, "tensor_copy", [ks_], [kb_], out=kb_[0:rows, 0:nt, :], in_=ks_[0:rows, 0:nt, :])
                k.do(k.dve, "tensor_copy", [vs_], [Vc], out=Vc[0:rows, kt0:kt0 + nt, :], in_=vs_[0:rows, 0:nt, :])
                k.do(k.pool, "tensor_copy", [is_], [ib_], out=ib_[0:rows, 0:nt, :], in_=is_[0:rows, 0:nt, :])
                for a_ in range(nt):
                    for g in range(2):
                        k.do(k.pe, "transpose", [kb_, identb], [psT], psT[:, g, a_ * 128:a_ * 128 + rows], kb_[0:rows, a_, g * 128:(g + 1) * 128], identb[0:rows, 0:rows], sig=False)
                    k.do(k.pe, "transpose", [ib_, identb], [psI], psI[0:64, a_ * 128:a_ * 128 + rows], ib_[0:rows, a_, :], identb[0:rows, 0:rows], sig=(a_ == nt - 1))
                n = (nt - 1) * 128 + rows
                k.do(k.act, "copy", [psT], [kT], out=kT[:, :, kpos:kpos + n], in_=psT[:, :, 0:n])
                k.do(k.dve, "tensor_copy", [psI], [ikT], out=ikT[:, kpos:kpos + n], in_=psI[0:64, 0:n])
                kpos += n
            return kpos

        ctr = {"d": 0, "r": 0, "s": 0, "p": 0}

        class QB_:
            def __init__(self, bi, tok0, QB, LK, topk, diag, I4):
                self.bi, self.tok0, self.QB, self.LK, self.topk, self.diag, self.I4 = bi, tok0, QB, LK, topk, diag, I4
                self.q_, self.ga_, self.og_ = W["qT"][bi % 2], W["gaT"][bi % 2], W["ogT"][0]
                self.iq_, self.iw_, self.wd_ = W["iqT"][bi % 2], W["iwt"][bi % 2], W["Wd"][bi % 2]
                self.sc, self.mb = score[bi % 2], mbias[bi % 2]
                self.bs, self.Hh, self.LO, self.mid = W["bs"][bi % 2], W["Hh"][bi % 2], W["LO"][bi % 2], W["mid"][bi % 2]
                self.jD, self.jA, self.SA, self.TA = W["jD"][bi % 2], W["jA"][bi % 2], W["SA"][bi % 2], W["TA"][bi % 2]

            def early_loads(self):
                QB, tok0 = self.QB, self.tok0
                k.dma(self.iq_[:, :, 0:QB], S_iq.t.rearrange("h d t -> d h t")[:, :, tok0:tok0 + QB], [S_iq], [self.iq_])
                k.dma(self.iw_[0:QB, :], S_iw.t[tok0:tok0 + QB, :], [S_iw], [self.iw_])
                for h in range(8):
                    k.do(k.pool, "tensor_scalar", [identf, self.iw_], [self.wd_], out=self.wd_[0:QB, h, 0:QB], in0=identf[0:QB, 0:QB], scalar1=self.iw_[0:QB, h:h + 1], scalar2=None, op0=ALU.mult, sig=(h == 7))

            def late_loads(self):
                QB, tok0 = self.QB, self.tok0
                k.dma(self.q_[:, :, 0:QB], fm_view(S["q"], tok0, QB), [S["q"]], [self.q_])
                k.dma(self.ga_[:, :, 0:QB], fm_view(S["ga"], tok0, QB), [S["ga"]], [self.ga_])

            def gen_I(self):
                QB, LK, iq_, wd_, sc = self.QB, self.LK, self.iq_, self.wd_, self.sc
                Rb = W["Rb"]
                nkt = (LK + 511) // 512
                for kt in range(nkt):
                    k0 = kt * 512
                    wk = min(512, LK - k0)

                    def dmm(h):
                        p = psD[ctr["d"] % 3]
                        ctr["d"] += 1
                        k.do(k.pe, "matmul", [iq_, ikT], [p], p[0:QB, 0:wk], iq_[:, h, 0:QB], ikT[:, k0:k0 + wk], start=True, stop=True)
                        return p
                    pq = [dmm(0), dmm(1)]
                    rq = []
                    for h in range(8):
                        p = pq.pop(0)
                        r_ = Rb[ctr["r"] % 3]
                        ctr["r"] += 1
                        k.do(k.act, "activation", [p], [r_], out=r_[0:QB, 0:wk], in_=p[0:QB, 0:wk], func=AF.Relu)
                        if h + 2 < 8:
                            pq.append(dmm(h + 2))
                        k.do(k.pe, "matmul", [wd_, r_], [psSc], psSc[0:QB, 0:wk], wd_[0:QB, h, 0:QB], r_[0:QB, 0:wk], start=(h == 0), stop=(h == 7), sig=(h == 7))
                        if h < 7:
                            yield
                    k.do(k.act, "copy", [psSc], [sc], out=sc[0:QB, k0:k0 + wk], in_=psSc[0:QB, 0:wk])
                    yield

            def gen_T(self):
                QB, LK, topk, sc, mb = self.QB, self.LK, self.topk, self.sc, self.mb
                bs, Hh, LO, mid = self.bs, self.Hh, self.LO, self.mid
                if LK > topk:
                    k.do(k.dve, "tensor_reduce", [sc], [bs], out=bs[0:QB, 0:1], in_=sc[0:QB, 0:LK], axis=AX.X, op=ALU.min)
                    k.do(k.dve, "tensor_reduce", [sc], [bs], out=bs[0:QB, 1:2], in_=sc[0:QB, 0:LK], axis=AX.X, op=ALU.max)
                    yield
                if self.diag:
                    k.do(k.dve, "memset", [], [sc], sc[0:64, LK - 64:LK], NEG)
                if LK > topk:
                    jD, jA, SA, MID, U, TA = self.jD, self.jA, self.SA, self.mid, self.LO, self.TA
                    k.do(k.dve, "tensor_tensor", [bs], [bs], out=bs[0:QB, 2:3], in0=bs[0:QB, 1:2], in1=bs[0:QB, 0:1], op=ALU.subtract)
                    k.do(k.dve, "tensor_scalar", [pw, bs], [Hh], out=Hh[0:QB, :], in0=pw[0:QB, :], scalar1=bs[0:QB, 2:3], scalar2=None, op0=ALU.mult)
                    k.do(k.dve, "tensor_tensor", [bs, Hh], [MID], out=MID[0:QB, 0:1], in0=bs[0:QB, 0:1], in1=Hh[0:QB, 0:1], op=ALU.add)
                    j0 = jD[0:QB, 0:1]
                    k1 = LK
                    if LK >= 1024:
                        k1 = max(128, int(round(LK * (1.0 - ACT_FRAC) / 128.0)) * 128)
                    nA = LK - k1
                    jb = bass.AP(j0.tensor, j0.offset, [list(j0.ap[0]), [0, k1]])
                    thrc = float(topk) - 0.5 * nA
                    if nA > 0:
                        a0 = jA[0:QB, 0:1]
                        jbA = bass.AP(a0.tensor, a0.offset, [list(a0.ap[0]), [0, nA]])
                        k.do(k.dve, "memset", [], [SA], SA[0:QB, :], 0.0)
                        k.do(k.dve, "memset", [], [TA], TA[0:QB, NITER:NITER + 1], thrc)
                    for it in range(NITER):
                        if nA > 0:
                            k.do(k.act, "activation", [sc, MID], [jA, SA], out=jbA, in_=sc[0:QB, k1:LK], func=AF.Sign, scale=-1.0, bias=MID[0:QB, it:it + 1], accum_out=SA[0:QB, it:it + 1])
                            k.do(k.act, "activation", [SA, TA], [TA], out=TA[0:QB, it:it + 1], in_=SA[0:QB, it:it + 1], func=AF.Identity, scale=0.5, bias=TA[0:QB, NITER:NITER + 1])
                        k.do(k.dve, "tensor_scalar", [sc, MID], [jD, bs], out=jb, in0=sc[0:QB, 0:k1], scalar1=MID[0:QB, it:it + 1], scalar2=None, op0=ALU.is_ge, op1=ALU.add, accum_out=bs[0:QB, 3:4])
                        if nA > 0:
                            k.do(k.dve, "tensor_scalar", [bs, TA, Hh], [U], out=U[0:QB, it:it + 1], in0=bs[0:QB, 3:4], scalar1=TA[0:QB, it:it + 1], scalar2=Hh[0:QB, it:it + 1], op0=ALU.is_ge, op1=ALU.mult)
                        else:
                            k.do(k.dve, "tensor_scalar", [bs, Hh], [U], out=U[0:QB, it:it + 1], in0=bs[0:QB, 3:4], scalar1=thrc, scalar2=Hh[0:QB, it:it + 1], op0=ALU.is_ge, op1=ALU.mult)
                        k.do(k.dve, "tensor_scalar", [U, MID, Hh], [MID], out=MID[0:QB, it + 1:it + 2], in0=U[0:QB, it:it + 1], scalar1=MID[0:QB, it:it + 1], scalar2=Hh[0:QB, it + 1:it + 2], op0=ALU.add, op1=ALU.subtract)
                        yield
                    k.do(k.dve, "tensor_tensor", [MID, Hh], [U], out=U[0:QB, NITER:NITER + 1], in0=MID[0:QB, NITER:NITER + 1], in1=Hh[0:QB, NITER:NITER + 1], op=ALU.subtract)
                    thr = U[0:QB, NITER:NITER + 1]
                    k.do(k.dve, "tensor_scalar", [sc, U], [mb], out=mb[0:QB, 0:LK], in0=sc[0:QB, 0:LK], scalar1=thr, scalar2=MASKV, op0=ALU.is_lt, op1=ALU.mult)
                else:
                    k.do(k.dve, "tensor_scalar", [sc], [mb], out=mb[0:QB, 0:LK], in0=sc[0:QB, 0:LK], scalar1=-1.0e29, scalar2=MASKV, op0=ALU.is_lt, op1=ALU.mult)
                yield

            def gen_A(self):
                QB, LK, q_, ga_, og_, mb, I4 = self.QB, self.LK, self.q_, self.ga_, self.og_, self.mb, self.I4
                PTb, rden = W["PTb"], W["rden"]
                N4 = 4 * QB
                nj = (LK + 127) // 128
                for g in range(2):
                    def qk(j):
                        wk = min(128, LK - j * 128)
                        p = psS[ctr["s"] % 2]
                        ctr["s"] += 1
                        k.do(k.pe, "matmul", [kT, q_], [p], p[0:wk, 0:N4], kT[:, g, j * 128:j * 128 + wk], q_[:, 4 * g:4 * g + 4, 0:QB], start=True, stop=False, sig=False)
                        k.do(k.pe, "matmul", [mb, I4], [p], p[0:wk, 0:N4], mb[0:QB, j * 128:j * 128 + wk], I4[0:QB, 0:N4], start=False, stop=True)
                        return p, wk
                    pend = qk(0)
                    for j in range(nj):
                        p, wk = pend
                        if j + 1 < nj:
                            pend = qk(j + 1)
                        pt = PTb[ctr["p"] % 3]
                        ctr["p"] += 1
                        k.do(k.act, "activation", [p], [pt], out=pt[0:wk, 0:N4], in_=p[0:wk, 0:N4], func=AF.Exp)
                        k.do(k.pe, "matmul", [Vc, pt], [psO], psO[:, 0:N4], Vc[0:wk, j, g * 128:(g + 1) * 128], pt[0:wk, 0:N4], start=(j == 0), stop=(j == nj - 1), sig=False)
                        k.do(k.pe, "matmul", [onesb, pt], [psN], psN[:, 0:N4], onesb[0:wk, :], pt[0:wk, 0:N4], start=(j == 0), stop=(j == nj - 1), sig=True)
                        yield
                    k.do(k.dve, "reciprocal", [psN], [rden], out=rden[:, 0:N4], in_=psN[:, 0:N4])
                    k.do(k.dve, "tensor_tensor", [psO, rden], [rden], out=rden[:, 0:N4], in0=psO[:, 0:N4], in1=rden[:, 0:N4], op=ALU.mult)
                    k.do(k.pool, "tensor_tensor", [rden, ga_], [og_], out=og_[:, 4 * g:4 * g + 4, 0:QB], in0=rden[:, 0:N4].rearrange("p (a b) -> p a b", b=QB), in1=ga_[:, 4 * g:4 * g + 4, 0:QB], op=ALU.mult)
                    yield
                k.dma(fm_view(S["og"], self.tok0, QB), og_[:, :, 0:QB], [og_], [S["og"]])

        def run(gen):
            for _ in gen:
                pass

        def interleave(gens):
            items = []
            for g in gens:
                steps = []
                items.append((g, steps))
            live = [g for g in gens]
            while live:
                for g in list(live):
                    try:
                        next(g)
                    except StopIteration:
                        live.remove(g)

        def interleave_n(gl):
            st_ = [[g, n, 0, False] for (g, n) in gl]
            while True:
                live = [e for e in st_ if not e[3]]
                if not live:
                    break
                e = min(live, key=lambda e: e[2] / max(e[1], 1))
                try:
                    next(e[0])
                    e[2] += 1
                except StopIteration:
                    e[3] = True

        def interleave_ratio(gA, nA, gT, nT):
            ia = it = 0
            doneA = doneT = False
            while not (doneA and doneT):
                fa = ia / max(nA, 1)
                ft = it / max(nT, 1)
                pick_T = (not doneT) and (doneA or ft <= fa)
                if pick_T:
                    try:
                        next(gT)
                        it += 1
                    except StopIteration:
                        doneT = True
                else:
                    try:
                        next(gA)
                        ia += 1
                    except StopIteration:
                        doneA = True

        jobs = []
        srcs = []
        for t0 in range(0, TP, 256):
            nt = min(2, (TP - t0) // 128)
            srcs.append((k_all.t[t0:t0 + nt * 128, :], v_all.t[t0:t0 + nt * 128, :], ik_all.t[t0:t0 + nt * 128, :], 128, nt, [k_all, v_all, ik_all]))
        jobs.append(("p", srcs))
        for s in range(NS):
            srcs = []
            for t0 in range(0, PAST, 256):
                nt = min(2, (PAST - t0) // 128)
                srcs.append((ck_d[s, t0:t0 + nt * 128, :], cv_d[s, t0:t0 + nt * 128, :], cik_d[s, t0:t0 + nt * 128, :], 128, nt, []))
            a0 = TP + s * TS
            srcs.append((k_all.t[a0:a0 + TS, :], v_all.t[a0:a0 + TS, :], ik_all.t[a0:a0 + TS, :], TS, 1, [k_all, v_all, ik_all]))
            jobs.append(("s%d" % s, srcs))
        gbi = 0
        for ji, (jn, srcs) in enumerate(jobs):
            with ExitStack() as st2:
                psT = ps(st2, "psT_" + jn, [128, 2, 512], BF16)
                psI = ps(st2, "psI_" + jn, [128, 512], BF16)
                W["kst"] = [sb(st2, "kst%d" % i, [128, 2, 256], F32) for i in range(2)]
                W["vst"] = [sb(st2, "vst%d" % i, [128, 2, 256], F32) for i in range(2)]
                W["ist"] = [sb(st2, "ist%d" % i, [128, 2, 64], F32) for i in range(2)]
                W["kbb"] = [sb(st2, "kbb%d" % i, [128, 2, 256], BF16) for i in range(2)]
                W["ibb"] = [sb(st2, "ibb%d" % i, [128, 2, 64], BF16) for i in range(2)]
                build_cache(srcs)
                k.barrier()
            with ExitStack() as st2:
                psD = [ps(st2, "psD%d_%s" % (i, jn), [128, 512], F32) for i in range(3)]
                psSc = ps(st2, "psSc_" + jn, [128, 512], F32)
                psS = [ps(st2, "psS%d_%s" % (i, jn), [128, 512], F32) for i in range(2)]
                psO = ps(st2, "psO_" + jn, [128, 512], F32)
                psN = ps(st2, "psN_" + jn, [128, 512], F32)
                W["qT"] = [sb(st2, "qT%d" % i, [128, 8, 128], BF16) for i in range(2)]
                W["gaT"] = [sb(st2, "gaT%d" % i, [128, 8, 128], BF16) for i in range(2)]
                W["ogT"] = [sb(st2, "ogT%d" % i, [128, 8, 128], BF16) for i in range(1)]
                W["iqT"] = [sb(st2, "iqT%d" % i, [64, 8, 128], BF16) for i in range(2)]
                W["iwt"] = [sb(st2, "iwt%d" % i, [128, 8], F32) for i in range(2)]
                W["Wd"] = [sb(st2, "Wd%d" % i, [128, 8, 128], BF16) for i in range(2)]
                W["Rb"] = [sb(st2, "Rb%d" % i, [128, 512], BF16) for i in range(3)]
                W["PTb"] = [sb(st2, "PTb%d" % i, [128, 512], BF16) for i in range(3)]
                W["rden"] = sb(st2, "rden", [128, 512], F32)
                W["bs"] = [sb(st2, "bs%d" % i, [128, 8], F32) for i in range(2)]
                W["Hh"] = [sb(st2, "Hh%d" % i, [128, NITER + 1], F32) for i in range(2)]
                W["LO"] = [sb(st2, "LO%d" % i, [128, NITER + 1], F32) for i in range(2)]
                W["mid"] = [sb(st2, "mid%d" % i, [128, NITER + 2], F32) for i in range(2)]
                W["TA"] = [sb(st2, "TA%d" % i, [128, NITER + 1], F32) for i in range(2)]
                W["jD"] = [sb(st2, "jD%d" % i, [128, 2], F32) for i in range(2)]
                W["jA"] = [sb(st2, "jA%d" % i, [128, 2], F32) for i in range(2)]
                W["SA"] = [sb(st2, "SA%d" % i, [128, NITER], F32) for i in range(2)]
                if ji == 0:
                    NB = TP // 128
                    blocks = [QB_(b, b * 128, 128, (b + 1) * 128, cfg.topk_p, True, I4p) for b in range(NB)]
                else:
                    blocks = [QB_(0, TP + (ji - 1) * TS, TS, cfg.LS, cfg.topk_s, False, I4s)]
                NB = len(blocks)
                blocks[0].early_loads()
                blocks[0].late_loads()
                run(blocks[0].gen_I())
                for b in range(NB):
                    gl = []
                    if b + 1 < NB:
                        blocks[b + 1].early_loads()
                    nT = (NITER + 2) if blocks[b].LK > blocks[b].topk else 1
                    gl.append((blocks[b].gen_T(), nT))
                    if b >= 1:
                        gl.append((blocks[b - 1].gen_A(), 2 * ((blocks[b - 1].LK + 127) // 128 + 1)))
                    if b + 1 < NB:
                        gl.append((blocks[b + 1].gen_I(), 8 * ((blocks[b + 1].LK + 511) // 512)))
                    interleave_n(gl)
                    if b + 1 < NB:
                        blocks[b + 1].late_loads()
                run(blocks[NB - 1].gen_A())
                k.barrier()
        k.barrier()

    TW3 = 512
    with ExitStack() as st:
        wab = sb(st, "wab", [128, 8, D], BF16)
        wbb = sb(st, "wbb", [128, 8, D], BF16)
        wob = sb(st, "wob", [128, 8, D], BF16)
        wdw = sb(st, "wdw", [128, 8, 31], F32)
        vec = sb(st, "vec", [128, 3, 8], F32)
        gfb = sb(st, "gfb", [128, D], F32)
        onesf = sb(st, "onesf", [128, 128], F32)
        gate_p = sb(st, "gate_p", [128, D], F32)
        gate_s = [sb(st, "gate_s%d" % i, [TS, D], F32) for i in range(NS)]
        k.dma(gate_p[:], G_p.t, [G_p], [gate_p])
        for i in range(NS):
            k.dma(gate_s[i][:], G_s[i].t, [G_s[i]], [gate_s[i]])
        k.do(k.dve, "memset", [], [onesf], onesf[:], 1.0 / D)
        k.dma(wdw[:], wdw_d.rearrange("p (a b) -> p a b", b=31), [], [wdw])
        k.dma(vec[:], vec_d.rearrange("p (a b) -> p a b", b=8), [], [vec])
        k.dma(gfb[:], gf_d, [], [gfb])
        with ExitStack() as st2:
            wl = [sb(st2, "wl%d" % i, [128, 2 * D], F32) for i in range(2)]
            li = 0
            for (src, dst) in [(wa_d, wab), (wb_d, wbb), (wo_d, wob)]:
                for q4 in range(4):
                    w_ = wl[li % 2]
                    li += 1
                    k.dma(w_[:], src[:, q4 * 2 * D:(q4 + 1) * 2 * D], [], [w_])
                    k.do(k.pool if li % 2 else k.dve, "tensor_copy", [w_], [dst], out=dst[:, 2 * q4:2 * q4 + 2, :], in_=w_[:].rearrange("p (a b) -> p a b", b=D))
            k.barrier()
        uext = [sb(st, "uext%d" % i, [128, 8, 30 + TW3], BF16) for i in range(2)]
        gbt = [sb(st, "gbt%d" % i, [128, 8, TW3], BF16) for i in range(1)]
        mat = [sb(st, "mat%d" % i, [128, 8, TW3], BF16) for i in range(1)]
        mbt = [sb(st, "mbt%d" % i, [128, 8, TW3], BF16) for i in range(1)]
        ogt = [sb(st, "ogt%d" % i, [128, 8, TW3], BF16) for i in range(1)]
        sstg = sb(st, "sstg", [128, 8, 30], F32)
        acc = [sb(st, "acc%d" % i, [128, TW3], F32) for i in range(8)]
        sq = [sb(st, "sq%d" % i, [128, TW3], F32) for i in range(2)]
        diag = [sb(st, "diag%d" % i, [128, 128], BF16) for i in range(6)]
        mean_sb = sb(st, "mean_sb", [128, TW3], F32)
        rstd_bc = sb(st, "rstd_bc", [128, TW3], F32)
        t1 = [sb(st, "t1_%d" % i, [128, TW3], F32) for i in range(2)]
        t2 = [sb(st, "t2_%d" % i, [128, TW3], F32) for i in range(2)]
        ybin = sb(st, "ybin", [128, 8, TW3], BF16)
        mixed = sb(st, "mixed", [128, 8, TW3], BF16)
        m1 = [sb(st, "m1_%d" % i, [128, TW3], F32) for i in range(2)]
        m2 = [sb(st, "m2_%d" % i, [128, TW3], F32) for i in range(2)]
        xt3 = [sb(st, "xt3_%d" % i, [128, D], F32) for i in range(2)]
        xn = [sb(st, "xn%d" % i, [128, D], F32) for i in range(2)]
        yt = [sb(st, "yt%d" % i, [128, D], F32) for i in range(2)]
        junk3 = sb(st, "junk3", [128, D], F32)
        st3 = [sb(st, "st3_%d" % i, [128, 4], F32) for i in range(2)]
        psM = ps(st, "psM", [128, 512], F32)
        psQ = ps(st, "psQ", [128, 512], F32)
        psA = [ps(st, "psA%d" % i, [128, 512], F32) for i in range(1)]
        psB = [ps(st, "psB%d" % i, [128, 512], F32) for i in range(1)]
        psC = [ps(st, "psC%d" % i, [128, 512], F32) for i in range(2)]
        psY = [ps(st, "psY%d" % i, [128, 512], F32) for i in range(2)]
        cnt = {"t": 0, "c": 0, "y": 0, "m": 0}

        def tile3(tok0, TW, first, sidx, gate):
            i = cnt["t"]
            cnt["t"] += 1
            ue, gb_, ma_, mb_, og_ = uext[i % 2], gbt[0], mat[0], mbt[0], ogt[0]
            if first:
                if sidx is None:
                    k.do(k.pool, "memset", [], [ue], ue[:, :, 0:30], 0.0)
                else:
                    k.dma(sstg[:], sconv_d[sidx].rearrange("p (a b) -> p a b", b=30), [], [sstg])
                    k.do(k.pool, "tensor_copy", [sstg], [ue], out=ue[:, :, 0:30], in_=sstg[:])
                k.dma(ue[:, :, 30:30 + TW], fm_view(S["ua"], tok0, TW), [S["ua"]], [ue])
            else:
                k.dma(ue[:, :, 0:30 + TW], fm_view(S["ua"], tok0 - 30, TW + 30), [S["ua"]], [ue])
            k.dma(gb_[:, :, 0:TW], fm_view(S["gb"], tok0, TW), [S["gb"]], [gb_])
            k.dma(ma_[:, :, 0:TW], fm_view(S["ma"], tok0, TW), [S["ma"]], [ma_])
            k.dma(mb_[:, :, 0:TW], fm_view(S["mb"], tok0, TW), [S["mb"]], [mb_])
            k.dma(og_[:, :, 0:TW], fm_view(S["og"], tok0, TW), [S["og"]], [og_])
            for c in range(8):
                pcv = psC[c % 2]
                for j in range(31):
                    dg = diag[cnt["c"] % 6]
                    cnt["c"] += 1
                    k.do(k.dve, "tensor_scalar", [identf, wdw], [dg], out=dg[:], in0=identf[:], scalar1=wdw[:, c, j:j + 1], scalar2=None, op0=ALU.mult)
                    k.do(k.pe, "matmul", [dg, ue], [pcv], pcv[:, 0:TW], dg[:], ue[:, c, j:j + TW], start=(j == 0), stop=(j == 30))
                k.do(k.act, "activation", [pcv, vec], [acc[c]], out=acc[c][:, 0:TW], in_=pcv[:, 0:TW], func=AF.Identity, bias=vec[:, 0, c:c + 1])
            for c in range(8):
                s_ = sq[c % 2]
                k.do(k.act, "activation", [acc[c]], [s_], out=s_[:, 0:TW], in_=acc[c][:, 0:TW], func=AF.Square)
                k.do(k.pe, "matmul", [onesf, acc[c]], [psM], psM[:, 0:TW], onesf[:], acc[c][:, 0:TW], start=(c == 0), stop=(c == 7), sig=False)
                k.do(k.pe, "matmul", [onesf, s_], [psQ], psQ[:, 0:TW], onesf[:], s_[:, 0:TW], start=(c == 0), stop=(c == 7), sig=True)
            k.do(k.act, "copy", [psM], [mean_sb], out=mean_sb[:, 0:TW], in_=psM[:, 0:TW])
            k.do(k.dve, "tensor_tensor", [mean_sb], [rstd_bc], out=rstd_bc[:, 0:TW], in0=mean_sb[:, 0:TW], in1=mean_sb[:, 0:TW], op=ALU.mult)
            k.do(k.dve, "tensor_tensor", [psQ, rstd_bc], [rstd_bc], out=rstd_bc[:, 0:TW], in0=psQ[:, 0:TW], in1=rstd_bc[:, 0:TW], op=ALU.subtract)
            k.do(k.act, "activation", [rstd_bc, epsT], [rstd_bc], out=rstd_bc[:, 0:TW], in_=rstd_bc[:, 0:TW], func=AF.Sqrt, bias=epsT[:, 0:1])
            k.do(k.dve, "reciprocal", [rstd_bc], [rstd_bc], out=rstd_bc[:, 0:TW], in_=rstd_bc[:, 0:TW])
            for c in range(8):
                a_, b_ = t1[c % 2], t2[c % 2]
                k.do(k.dve, "tensor_tensor", [acc[c], mean_sb], [a_], out=a_[:, 0:TW], in0=acc[c][:, 0:TW], in1=mean_sb[:, 0:TW], op=ALU.subtract)
                k.do(k.dve, "tensor_tensor", [a_, rstd_bc], [a_], out=a_[:, 0:TW], in0=a_[:, 0:TW], in1=rstd_bc[:, 0:TW], op=ALU.mult)
                k.do(k.act, "activation", [a_, vec], [b_], out=b_[:, 0:TW], in_=a_[:, 0:TW], func=AF.Silu, scale=vec[:, 1, c:c + 1], bias=vec[:, 2, c:c + 1])
                k.do(k.dve, "tensor_tensor", [b_, gb_], [ybin], out=ybin[:, c, 0:TW], in0=b_[:, 0:TW], in1=gb_[:, c, 0:TW], op=ALU.mult)
            for cc in range(8):
                pa, pb = psA[0], psB[0]
                for kc in range(8):
                    k.do(k.pe, "matmul", [wab, og_], [pa], pa[:, 0:TW], wab[:, kc, cc * 128:(cc + 1) * 128], og_[:, kc, 0:TW], start=(kc == 0), stop=(kc == 7), sig=(kc == 7))
                for kc in range(8):
                    k.do(k.pe, "matmul", [wbb, ybin], [pb], pb[:, 0:TW], wbb[:, kc, cc * 128:(cc + 1) * 128], ybin[:, kc, 0:TW], start=(kc == 0), stop=(kc == 7), sig=(kc == 7))
                a_, b_ = m1[cc % 2], m2[cc % 2]
                k.do(k.dve, "tensor_tensor", [pa, ma_], [a_], out=a_[:, 0:TW], in0=pa[:, 0:TW], in1=ma_[:, cc, 0:TW], op=ALU.mult)
                k.do(k.dve, "tensor_tensor", [pb, mb_], [b_], out=b_[:, 0:TW], in0=pb[:, 0:TW], in1=mb_[:, cc, 0:TW], op=ALU.mult)
                k.do(k.pool, "tensor_tensor", [a_, b_], [mixed], out=mixed[:, cc, 0:TW], in0=a_[:, 0:TW], in1=b_[:, 0:TW], op=ALU.add)
            for s0 in range(0, TW, 128):
                rows = min(128, TW - s0)
                yi = cnt["y"]
                cnt["y"] += 1
                x_, xn_, y_, s_ = xt3[yi % 2], xn[yi % 2], yt[yi % 2], st3[yi % 2]
                k.dma(x_[0:rows, :], x_all[tok0 + s0:tok0 + s0 + rows, :], [], [x_])
                for half in range(2):
                    py = psY[half]
                    for kc in range(8):
                        k.do(k.pe, "matmul", [mixed, wob], [py], py[0:rows, :], mixed[:, kc, s0:s0 + rows], wob[:, kc, half * 512:(half + 1) * 512], start=(kc == 0), stop=(kc == 7), sig=(kc == 7))
                    hs = slice(half * 512, half * 512 + 512)
                    k.do(k.dve, "tensor_tensor", [py, gate], [xn_], out=xn_[0:rows, hs], in0=py[0:rows, :], in1=gate[0:rows, hs], op=ALU.mult)
                k.do(k.pool, "tensor_tensor", [xn_, x_], [xn_], out=xn_[0:rows, :], in0=xn_[0:rows, :], in1=x_[0:rows, :], op=ALU.add)
                k.do(k.act, "activation", [xn_], [junk3, s_], out=junk3[0:rows, :], in_=xn_[0:rows, :], func=AF.Square, accum_out=s_[0:rows, 0:1])
                k.do(k.act, "activation", [s_, epsT], [s_], out=s_[0:rows, 1:2], in_=s_[0:rows, 0:1], func=AF.Sqrt, scale=1.0 / D, bias=epsT[0:rows, 0:1])
                k.do(k.dve, "reciprocal", [s_], [s_], out=s_[0:rows, 2:3], in_=s_[0:rows, 1:2])
                k.do(k.dve, "scalar_tensor_tensor", [xn_, s_, gfb], [y_], out=y_[0:rows, :], in0=xn_[0:rows, :], scalar=s_[0:rows, 2:3], in1=gfb[0:rows, :], op0=ALU.mult, op1=ALU.mult)
                k.dma(y_all.t[tok0 + s0:tok0 + s0 + rows, :], y_[0:rows, :], [y_], [y_all])

        for t0 in range(0, TP, TW3):
            tile3(t0, min(TW3, TP - t0), t0 == 0, None, gate_p)
        for s in range(NS):
            tile3(TP + s * TS, TS, True, s, gate_s[s])
        k.barrier()
    k.final_wait()
    P0.close()
    k.stack0.close()
    return nc, k


def _consts():
    c = np.zeros((128, 128 + 192 + 512 + 128), np.float32)
    c[:, 0:128] = np.eye(128, dtype=np.float32)
    c[0, 128:256] = 1.0
    c[1, 256:288] = 1.0
    c[2, 288:320] = 1.0
    for r in range(4):
        c[:, 320 + r * 128:320 + (r + 1) * 128] = np.eye(128, dtype=np.float32)
        c[0:32, 832 + r * 32:832 + (r + 1) * 32] = np.eye(32, dtype=np.float32)
    return c


def _kc_layout(w):
    n = w.shape[1]
    return np.ascontiguousarray(w.reshape(8, 128, n).transpose(1, 0, 2)).reshape(128, 8 * n)


def _fmT(v, n):
    return np.ascontiguousarray(v.reshape(n, 128).T)


def make_in_maps(cfg, x_prompt, x_sample, cache_k, cache_v, cache_idx_k, state_conv, c_prompt, c_sample,
                 w_ada, b_ada, g_norm, w_in, w_a, w_dw, b_dw, g_ln, b_ln, w_b, w_out, g_final, n_cores):
    f = lambda a: np.ascontiguousarray(np.asarray(a, dtype=np.float32))
    w_in0 = f(w_in[0])
    wada = _kc_layout(f(w_ada[0])).reshape(128, 8, 3072)
    bada = np.ascontiguousarray(np.broadcast_to(f(b_ada[0])[None, :], (3, 3072)))
    gn_bc = np.ascontiguousarray(np.broadcast_to(f(g_norm[0])[None, :], (128, D)))
    gf_bc = np.ascontiguousarray(np.broadcast_to(f(g_final)[None, :], (128, D)))
    wfm = np.stack([_kc_layout(w_in0[:, FM_OFF[kd] + c * 128: FM_OFF[kd] + (c + 1) * 128]) for kd in FM_KINDS for c in range(8)])
    wiq = np.stack([_kc_layout(w_in0[:, O_IQ + h * 64: O_IQ + (h + 1) * 64]) for h in range(8)])
    wkv = _kc_layout(np.concatenate([w_in0[:, O_K:O_K + 512], w_in0[:, O_IK:O_IK + 64], w_in0[:, O_IW:O_IW + 8]], axis=1))
    wa, wb, wo = _kc_layout(f(w_a[0])), _kc_layout(f(w_b[0])), _kc_layout(f(w_out[0]))
    wdwT = np.ascontiguousarray(f(w_dw[0]).T.reshape(8, 128, 31).transpose(1, 0, 2)).reshape(128, 8 * 31)
    vecT = np.concatenate([_fmT(f(b_dw[0]), 8), _fmT(f(g_ln[0]), 8), _fmT(f(b_ln[0]), 8)], axis=1)
    consts = _consts()
    NS, TS = cfg.NS, cfg.TS
    maps = []
    for i in range(n_cores):
        ss = slice(i * NS, (i + 1) * NS)
        xs = f(x_sample[ss]).reshape(NS * TS, D)
        x_all = np.concatenate([f(x_prompt[i]), xs], axis=0)
        cc = np.concatenate([f(c_prompt[i])[None], f(c_sample[ss])], axis=0)
        cT = np.ascontiguousarray(cc.reshape(3, 8, 128).transpose(2, 1, 0))
        sconv = f(state_conv[0][ss])
        sconvT = np.ascontiguousarray(sconv.reshape(NS, 30, 8, 128).transpose(0, 3, 2, 1)).reshape(NS, 128, 240)
        maps.append({
            "x_all": x_all, "cT": cT, "wada": wada, "bada": bada, "gn_bc": gn_bc, "wfm": wfm, "wiq": wiq, "wkv": wkv,
            "wa": wa, "wb": wb, "wo": wo, "wdwT": wdwT, "vecT": vecT, "gf_bc": gf_bc,
            "ck": f(cache_k[0][ss]).reshape(NS, cfg.PAST, 256), "cv": f(cache_v[0][ss]).reshape(NS, cfg.PAST, 256),
            "cik": f(cache_idx_k[0][ss]), "sconvT": sconvT, "consts": consts,
        })
    return maps


def assemble(cfg, results, n_cores):
    TP, NS, TS = cfg.TP, cfg.NS, cfg.TS
    g = lambda name: [np.asarray(r[name]) for r in results]
    y, kk, vv, ik, cv = g("y_all"), g("k_all"), g("v_all"), g("ik_all"), g("conv_o")
    y_p = np.stack([a[:TP] for a in y])
    y_s = np.concatenate([a[TP:].reshape(NS, TS, D) for a in y])
    k_p = np.stack([a[:TP].reshape(TP, 2, 128) for a in kk])[None]
    v_p = np.stack([a[:TP].reshape(TP, 2, 128) for a in vv])[None]
    ik_p = np.stack([a[:TP] for a in ik])[None]
    c_p = np.stack([a[0] for a in cv])[None]
    k_s = np.concatenate([a[TP:].reshape(NS, TS, 2, 128) for a in kk])[None]
    v_s = np.concatenate([a[TP:].reshape(NS, TS, 2, 128) for a in vv])[None]
    ik_s = np.concatenate([a[TP:].reshape(NS, TS, 64) for a in ik])[None]
    c_s = np.concatenate([a[1:] for a in cv])[None]
    return tuple(np.ascontiguousarray(a, dtype=np.float32) for a in (y_p, y_s, k_p, v_p, ik_p, c_p, k_s, v_s, ik_s, c_s))


def kernel(**inputs):
    cfg = Cfg()
    n = 8
    nc, _ = build(cfg)
    maps = make_in_maps(cfg, n_cores=n, **inputs)
    res = run_bass_kernel_spmd(nc, maps, core_ids=list(range(n)))
    return assemble(cfg, res.results, n)
```

```python
import numpy as np
from contextlib import ExitStack
import concourse.bass as bass
import concourse.mybir as mybir
from concourse.bass_utils import run_bass_kernel_spmd

F32 = mybir.dt.float32
BF16 = mybir.dt.bfloat16
AF = mybir.ActivationFunctionType
ALU = mybir.AluOpType
AX = mybir.AxisListType

D = 1024
NIN = 8264
CHUNK = 64
NITER = 22
import os
ACT_FRAC = float(os.environ.get("ACT_FRAC", "0.0"))
NEG = -1.0e30
MASKV = -30000.0

O_Q, O_K, O_V, O_GA, O_IQ, O_IW, O_IK, O_GLU, O_GB, O_M = 0, 1024, 1280, 1536, 2560, 3072, 3080, 3144, 5192, 6216
FM_KINDS = ["q", "ga", "ua", "ub", "gb", "ma", "mb"]
FM_OFF = {"q": O_Q, "ga": O_GA, "ua": O_GLU, "ub": O_GLU + 1024, "gb": O_GB, "ma": O_M, "mb": O_M + 1024}


class Sem:
    def __init__(self, k, name):
        self.h = k.stack0.enter_context(k.nc.semaphore(name))
        self.v = 0


class Buf:
    def __init__(self, t, disjoint=False):
        self.t = t
        self.w = {}
        self.r = {}
        self.disjoint = disjoint

    def __getitem__(self, idx):
        return self.t[idx]


def _merge(d, tok):
    if tok is None:
        return
    s, v = tok
    if id(s) not in d or d[id(s)][1] < v:
        d[id(s)] = (s, v)


class Eng:
    def __init__(self, k, e, name, same=True):
        self.k = k
        self.e = e
        self.name = name
        self.sem = Sem(k, "e_" + name)
        self.seen = {}
        self.same = same
        self.pending = []

    def wait_tok(self, s, v):
        if s is self.sem and not self.same:
            return
        if self.seen.get(id(s), 0) < v:
            self.e.wait_ge(s.h, v)
            self.seen[id(s)] = v

    def wait_bufs(self, reads, writes):
        for b in reads:
            for s, v in b.w.values():
                self.wait_tok(s, v)
        for b in writes:
            if b.disjoint:
                continue
            for s, v in b.w.values():
                self.wait_tok(s, v)
            for s, v in b.r.values():
                self.wait_tok(s, v)


class K:
    def __init__(self, nc):
        self.nc = nc
        self.stack0 = ExitStack()
        self.pe = Eng(self, nc.tensor, "pe", same=False)
        self.act = Eng(self, nc.scalar, "act")
        self.dve = Eng(self, nc.vector, "dve")
        self.pool = Eng(self, nc.gpsimd, "pool")
        self.sp = Eng(self, nc.sync, "sp")
        self.engs = [self.pe, self.act, self.dve, self.pool, self.sp]
        self.dsems = [Sem(self, "d%d" % i) for i in range(24)]
        self.di = 0
        self.nins = 0

    def _post(self, E, tok, reads, writes):
        for b in writes:
            if not b.disjoint:
                b.w = {}
                b.r = {}
            _merge(b.w, tok)
        for b in reads:
            _merge(b.r, tok)

    def do(self, E, fn, reads, writes, *a, sig=True, **kw):
        E.wait_bufs(reads, writes)
        ins = getattr(E.e, fn)(*a, **kw)
        self.nins += 1
        if sig:
            E.sem.v += 1
            ins.then_inc(E.sem.h, 1)
            tok = (E.sem, E.sem.v)
            for (rr, ww) in E.pending:
                self._post(E, tok, rr, ww)
            E.pending = []
            self._post(E, tok, reads, writes)
            return tok
        E.pending.append((reads, writes))
        return None

    def dma(self, out, in_, reads, writes, E=None, **kw):
        E = E or self.sp
        E.wait_bufs(reads, writes)
        s = self.dsems[self.di % len(self.dsems)]
        self.di += 1
        if s.v > 0:
            E.wait_tok(s, s.v)
        s.v += 16
        E.e.dma_start(out=out, in_=in_, **kw).then_inc(s.h, 16)
        self.nins += 1
        tok = (s, s.v)
        for b in writes:
            _merge(b.w, tok)
        for b in reads:
            _merge(b.r, tok)
        return tok

    def barrier(self):
        sems = [e.sem for e in self.engs] + self.dsems
        for E in [self.pe, self.act, self.dve, self.pool, self.sp]:
            for s in sems:
                if s.v > 0 and s is not E.sem:
                    E.wait_tok(s, s.v)

    def final_wait(self):
        for s in self.dsems:
            if s.v > 0:
                self.sp.wait_tok(s, s.v)
        for e in self.engs:
            if e is not self.sp and e.sem.v > 0:
                self.sp.wait_tok(e.sem, e.sem.v)


class Cfg:
    def __init__(self, TP=8192, PAST=2048, TS=32, NS=2, topk_max=256, debug=False):
        self.TP, self.PAST, self.TS, self.NS = TP, PAST, TS, NS
        self.debug = debug
        self.NTOK = TP + NS * TS
        self.topk_p = min(topk_max, TP // 4)
        self.topk_s = min(topk_max, (PAST + TS) // 4)
        self.LS = PAST + TS
        self.LMAX = max(TP, ((self.LS + 127) // 128) * 128)
        self.NKT = self.LMAX // 128


def build(cfg):
    TP, NTOK, TS, NS, PAST = cfg.TP, cfg.NTOK, cfg.TS, cfg.NS, cfg.PAST
    SROWS = NS * TS
    nc = bass.Bass("TRN2", target_bir_lowering=False)
    k = K(nc)

    def din(name, shape, dt=F32):
        return nc.dram_tensor(name, list(shape), dt, kind="ExternalInput").ap()

    def dout(name, shape, dt=F32):
        return nc.dram_tensor(name, list(shape), dt, kind="ExternalOutput").ap()

    def dscr(name, shape, dt=BF16):
        return Buf(nc.dram_tensor(name, list(shape), dt, kind="Internal").ap(), disjoint=True)

    x_all = din("x_all", [NTOK, D])
    cT_d = din("cT", [128, 8, 3])
    wada_d = din("wada", [128, 8, 3072])
    bada_d = din("bada", [3, 3072])
    gn_d = din("gn_bc", [128, D])
    wfm_d = din("wfm", [56, 128, 8 * 128])
    wiq_d = din("wiq", [8, 128, 8 * 64])
    wkv_d = din("wkv", [128, 8 * 584])
    wa_d = din("wa", [128, 8 * D])
    wb_d = din("wb", [128, 8 * D])
    wo_d = din("wo", [128, 8 * D])
    wdw_d = din("wdwT", [128, 8 * 31])
    vec_d = din("vecT", [128, 3 * 8])
    gf_d = din("gf_bc", [128, D])
    ck_d = din("ck", [NS, PAST, 256])
    cv_d = din("cv", [NS, PAST, 256])
    cik_d = din("cik", [NS, PAST, 64])
    sconv_d = din("sconvT", [NS, 128, 8 * 30])
    const_d = din("consts", [128, 128 + 192 + 512 + 128])

    y_all = Buf(disjoint=True, t=dout("y_all", [NTOK, D]))
    k_all = Buf(disjoint=True, t=dout("k_all", [NTOK, 256]))
    v_all = Buf(disjoint=True, t=dout("v_all", [NTOK, 256]))
    ik_all = Buf(disjoint=True, t=dout("ik_all", [NTOK, 64]))
    conv_o = Buf(disjoint=True, t=dout("conv_o", [1 + NS, 30, D]))

    S = {kind: dscr("s_" + kind, [8, 128, NTOK]) for kind in FM_KINDS}
    if cfg.debug:
        S["og"] = Buf(nc.dram_tensor("s_og", [8, 128, NTOK], BF16, kind="ExternalOutput").ap(), disjoint=True)
    else:
        S["og"] = dscr("s_og", [8, 128, NTOK])
    S_iq = dscr("s_iq", [8, 64, NTOK])
    S_iw = dscr("s_iw", [NTOK, 8], F32)

    def fm_view(b, t0, n):
        return b.t.rearrange("h d t -> d h t")[:, :, t0:t0 + n]

    P0 = ExitStack()

    uniq = [0]

    def sb(st, name, shape, dt):
        uniq[0] += 1
        return Buf(st.enter_context(nc.sbuf_tensor("%s_u%d" % (name, uniq[0]), list(shape), dt)))

    def ps(st, name, shape, dt):
        uniq[0] += 1
        return Buf(st.enter_context(nc.psum_tensor("%s_u%d" % (name, uniq[0]), list(shape), dt)))

    identf = sb(P0, "identf", [128, 128], F32)
    identb = sb(P0, "identb", [128, 128], BF16)
    G_p = Buf(nc.dram_tensor("g_p", [128, D], F32, kind="Internal").ap(), disjoint=True)
    G_s = [Buf(nc.dram_tensor("g_s%d" % i, [TS, D], F32, kind="Internal").ap(), disjoint=True) for i in range(NS)]
    epsT = sb(P0, "epsT", [128, 1], F32)
    onesb = sb(P0, "onesb", [128, 128], BF16)
    I4p = sb(P0, "I4p", [128, 512], BF16)
    I4s = sb(P0, "I4s", [32, 128], BF16)
    PC = ExitStack()
    cst = sb(PC, "cst", [128, 128 + 192 + 512 + 128], F32)

    k.dma(cst[:], const_d, [], [cst])
    k.do(k.dve, "tensor_copy", [cst], [identf], out=identf[:], in_=cst[:, 0:128])
    k.do(k.dve, "tensor_copy", [cst], [identb], out=identb[:], in_=cst[:, 0:128])
    k.do(k.dve, "memset", [], [epsT], epsT[:], 1e-6)
    k.do(k.dve, "memset", [], [onesb], onesb[:], 1.0)
    k.do(k.dve, "tensor_copy", [cst], [I4p], out=I4p[:], in_=cst[:, 320:832])
    k.do(k.dve, "tensor_copy", [cst], [I4s], out=I4s[:], in_=cst[0:32, 832:960])
    sel = lambda a, b_: cst[0:3, 128 + a:128 + b_]

    P1 = ExitStack()
    hT = sb(P1, "hT", [128, 8, NTOK], BF16)
    P1a = ExitStack()
    AB_p = sb(P1a, "AB_p", [128, 2 * D], F32)
    AB_s = sb(P1a, "AB_s", [SROWS, 2 * D], F32)
    with ExitStack() as st:
        cT = sb(st, "cT_t", [128, 8, 3], F32)
        sc = sb(st, "sc_t", [128, 8, 3], F32)
        modrows = sb(st, "modrows", [3, 3072], F32)
        gn = sb(st, "gn", [128, D], F32)
        wat = [sb(st, "wat%d" % i, [128, 8, 256], F32) for i in range(2)]
        pm = ps(st, "pm", [128, 512], F32)
        pbp = [ps(st, "pbp%d" % i, [128, 512], F32) for i in range(2)]
        k.dma(cT[:], cT_d, [], [cT])
        k.dma(modrows[:], bada_d, [], [modrows])
        k.dma(gn[:], gn_d, [], [gn])
        k.do(k.act, "activation", [cT], [sc], out=sc[:], in_=cT[:], func=AF.Silu)
        wview = wada_d
        for i in range(12):
            w = wat[i % 2]
            k.dma(w[:], wview[:, :, i * 256:(i + 1) * 256], [], [w])
            for kc in range(8):
                k.do(k.pe, "matmul", [sc, w], [pm], pm[0:3, 0:256], sc[:, kc, :], w[:, kc, :], start=(kc == 0), stop=(kc == 7), sig=(kc == 7))
            k.do(k.dve, "tensor_tensor", [pm, modrows], [modrows], out=modrows[:, i * 256:(i + 1) * 256], in0=pm[0:3, 0:256], in1=modrows[:, i * 256:(i + 1) * 256], op=ALU.add)
        gstg = [sb(st, "gstg%d" % i, [128, 512], F32) for i in range(2)]
        targets = [((0, 128), 128, AB_p, G_p), ((128, 128 + SROWS), SROWS, AB_s, None)]
        for si in range(NS):
            targets.append(((128 + si * 32, 128 + si * 32 + TS), TS, None, G_s[si]))
        bi = 0
        for (sc0, sc1), M, AB, gt in targets:
            for part in range(3):
                if part < 2 and AB is None:
                    continue
                if part == 2 and gt is None:
                    continue
                for half in range(2):
                    pb = pbp[bi % 2]
                    bi += 1
                    k.do(k.pe, "matmul", [cst, modrows], [pb], pb[0:M, :], sel(sc0, sc1), modrows[:, part * 1024 + half * 512: part * 1024 + half * 512 + 512], start=True, stop=True)
                    cs = slice(half * 512, half * 512 + 512)
                    if part == 1:
                        k.do(k.dve, "scalar_tensor_tensor", [pb, gn], [AB], out=AB[0:M, cs], in0=pb[0:M, :], scalar=1.0, in1=gn[0:M, cs], op0=ALU.add, op1=ALU.mult)
                    elif part == 0:
                        k.do(k.act, "copy", [pb], [AB], out=AB[0:M, D + half * 512: D + half * 512 + 512], in_=pb[0:M, :])
                    else:
                        g_ = gstg[bi % 2]
                        k.do(k.act, "copy", [pb], [g_], out=g_[0:M, :], in_=pb[0:M, :])
                        k.dma(gt.t[0:M, cs], g_[0:M, :], [g_], [gt])
        k.barrier()

    NT = (NTOK + 127) // 128
    with ExitStack() as st:
        xb = [sb(st, "xb%d" % i, [128, D], F32) for i in range(3)]
        h1 = [sb(st, "h1_%d" % i, [128, D], F32) for i in range(2)]
        hb = [sb(st, "hb%d" % i, [128, D], BF16) for i in range(2)]
        junk = sb(st, "junk1", [128, D], F32)
        stats = [sb(st, "stats%d" % i, [128, 4], F32) for i in range(3)]
        pT = [ps(st, "pT%d" % i, [128, 8, 128], BF16) for i in range(2)]
        for tt in range(NT):
            t0 = tt * 128
            rows = min(128, NTOK - t0)
            AB = AB_p if t0 < TP else AB_s
            x = xb[tt % 3]
            s_ = stats[tt % 3]
            k.dma(x[0:rows, :], x_all[t0:t0 + rows, :], [], [x])
            k.do(k.act, "activation", [x], [junk, s_], out=junk[0:rows, :], in_=x[0:rows, :], func=AF.Square, accum_out=s_[0:rows, 0:1])
            k.do(k.act, "activation", [s_, epsT], [s_], out=s_[0:rows, 1:2], in_=s_[0:rows, 0:1], func=AF.Sqrt, scale=1.0 / D, bias=epsT[0:rows, 0:1])
            k.do(k.dve, "reciprocal", [s_], [s_], out=s_[0:rows, 2:3], in_=s_[0:rows, 1:2])
            h_ = h1[tt % 2]
            k.do(k.dve, "scalar_tensor_tensor", [x, s_, AB], [h_], out=h_[0:rows, :], in0=x[0:rows, :], scalar=s_[0:rows, 2:3], in1=AB[0:rows, 0:D], op0=ALU.mult, op1=ALU.mult)
            hb_ = hb[tt % 2]
            k.do(k.dve, "tensor_tensor", [h_, AB], [hb_], out=hb_[0:rows, :], in0=h_[0:rows, :], in1=AB[0:rows, D:2 * D], op=ALU.add)
            p = pT[tt % 2]
            for c in range(8):
                k.do(k.pe, "transpose", [hb_, identb], [p], p[:, c, 0:rows], hb_[0:rows, c * 128:(c + 1) * 128], identb[0:rows, 0:rows], sig=(c == 7))
            k.do(k.act, "copy", [p], [hT], out=hT[:, :, t0:t0 + rows], in_=p[:, :, 0:rows])
        k.barrier()
    P1a.close()

    groups = [(g0, min(512, NTOK - g0)) for g0 in range(0, NTOK, 512)]
    with ExitStack() as st:
        wkvb = sb(st, "wkvb", [128, 8, 584], BF16)
        kvst = [sb(st, "kvst%d" % i, [128, 584], F32) for i in range(2)]
        wst = [sb(st, "wst%d" % i, [128, 8 * 128], F32) for i in range(2)]
        wbf = [sb(st, "wbf%d" % i, [128, 8, 128], BF16) for i in range(3)]
        stg = [sb(st, "stg%d" % i, [128, 512], BF16) for i in range(3)]
        sig_ = [sb(st, "sig%d" % i, [128, 512], F32) for i in range(2)]
        cstg = [sb(st, "cstg%d" % i, [64, 256], F32) for i in range(2)]
        pz = [ps(st, "pz%d" % i, [128, 512], F32) for i in range(4)]
        pk0 = ps(st, "pk0", [128, 512], F32)
        pk1 = ps(st, "pk1", [128, 512], F32)
        pc = ps(st, "pcv", [128, 512], F32)
        for kc in range(8):
            b_ = kvst[kc % 2]
            k.dma(b_[:], wkv_d[:, kc * 584:(kc + 1) * 584], [], [b_])
            k.do(k.pool, "tensor_copy", [b_], [wkvb], out=wkvb[:, kc, :], in_=b_[:])
        for tt in range(NT):
            t0 = tt * 128
            rows = min(128, NTOK - t0)
            for kc in range(8):
                k.do(k.pe, "matmul", [hT, wkvb], [pk0], pk0[0:rows, :], hT[:, kc, t0:t0 + rows], wkvb[:, kc, 0:512], start=(kc == 0), stop=(kc == 7), sig=False)
            for kc in range(8):
                k.do(k.pe, "matmul", [hT, wkvb], [pk1], pk1[0:rows, 0:72], hT[:, kc, t0:t0 + rows], wkvb[:, kc, 512:584], start=(kc == 0), stop=(kc == 7), sig=(kc == 7))
            o_ = kvst[tt % 2]
            k.do(k.act, "copy", [pk0], [o_], out=o_[0:rows, 0:512], in_=pk0[0:rows, :])
            k.do(k.dve, "tensor_copy", [pk1], [o_], out=o_[0:rows, 512:584], in_=pk1[0:rows, 0:72])
            k.dma(k_all.t[t0:t0 + rows, :], o_[0:rows, 0:256], [o_], [k_all])
            k.dma(v_all.t[t0:t0 + rows, :], o_[0:rows, 256:512], [o_], [v_all])
            k.dma(ik_all.t[t0:t0 + rows, :], o_[0:rows, 512:576], [o_], [ik_all])
            k.dma(S_iw.t[t0:t0 + rows, :], o_[0:rows, 576:584], [o_], [S_iw])

        wi = [0]
        zi = [0]
        si_ = [0]

        def load_w(src_ap, width):
            i = wi[0]
            wi[0] += 1
            ws = wst[i % 2]
            wb_ = wbf[i % 3]
            k.dma(ws[:, 0:8 * width], src_ap, [], [ws])
            k.do(k.pool, "tensor_copy", [ws], [wb_], out=wb_[:, :, 0:width], in_=ws[:, 0:8 * width].rearrange("p (a b) -> p a b", b=width))
            return wb_

        def proj(wb_, width, g0, n):
            p = pz[zi[0] % 4]
            zi[0] += 1
            for kc in range(8):
                k.do(k.pe, "matmul", [hT, wb_], [p], p[0:width, 0:n], wb_[:, kc, 0:width], hT[:, kc, g0:g0 + n], start=(kc == 0), stop=(kc == 7), sig=(kc == 7))
            return p

        def single(kind, c, func, scale=1.0):
            wb_ = load_w(wfm_d[FM_KINDS.index(kind) * 8 + c], 128)
            for (g0, n) in groups:
                p = proj(wb_, 128, g0, n)
                o_ = stg[si_[0] % 3]
                si_[0] += 1
                k.do(k.act, "activation", [p], [o_], out=o_[:, 0:n], in_=p[:, 0:n], func=func, scale=scale)
                k.dma(S[kind].t[c, :, g0:g0 + n], o_[:, 0:n], [o_], [S[kind]])

        for c in range(8):
            single("q", c, AF.Copy, scale=128.0 ** -0.5)
        for h in range(8):
            wb_ = load_w(wiq_d[h], 64)
            for (g0, n) in groups:
                p = proj(wb_, 64, g0, n)
                o_ = stg[si_[0] % 3]
                si_[0] += 1
                k.do(k.dve, "tensor_copy", [p], [o_], out=o_[0:64, 0:n], in_=p[0:64, 0:n])
                k.dma(S_iq.t[h, :, g0:g0 + n], o_[0:64, 0:n], [o_], [S_iq])
        crow_sets = [(TP - 32, 32, [(0, 2, 32)]), (TP, SROWS, [(1 + s, TS * s + 2, TS * s + TS) for s in range(NS)])]
        ci = 0
        for c in range(8):
            wA = load_w(wfm_d[FM_KINDS.index("ua") * 8 + c], 128)
            wB = load_w(wfm_d[FM_KINDS.index("ub") * 8 + c], 128)
            for (g0, n) in groups:
                pA = proj(wA, 128, g0, n)
                pB = proj(wB, 128, g0, n)
                sg = sig_[si_[0] % 2]
                o_ = stg[si_[0] % 3]
                si_[0] += 1
                k.do(k.act, "activation", [pB], [sg], out=sg[:, 0:n], in_=pB[:, 0:n], func=AF.Sigmoid)
                k.do(k.dve, "tensor_tensor", [pA, sg], [o_], out=o_[:, 0:n], in0=pA[:, 0:n], in1=sg[:, 0:n], op=ALU.mult)
                k.dma(S["ua"].t[c, :, g0:g0 + n], o_[:, 0:n], [o_], [S["ua"]])
            for (r0, M, outs) in crow_sets:
                for kc in range(8):
                    k.do(k.pe, "matmul", [hT, wA], [pc], pc[0:M, 0:128], hT[:, kc, r0:r0 + M], wA[:, kc, :], start=(kc == 0), stop=(kc == 7), sig=False)
                for kc in range(8):
                    k.do(k.pe, "matmul", [hT, wB], [pc], pc[0:M, 128:256], hT[:, kc, r0:r0 + M], wB[:, kc, :], start=(kc == 0), stop=(kc == 7), sig=(kc == 7))
                cs_ = cstg[ci % 2]
                ci += 1
                k.do(k.act, "activation", [pc], [cs_], out=cs_[0:M, 128:256], in_=pc[0:M, 128:256], func=AF.Sigmoid)
                k.do(k.dve, "tensor_tensor", [pc, cs_], [cs_], out=cs_[0:M, 0:128], in0=pc[0:M, 0:128], in1=cs_[0:M, 128:256], op=ALU.mult)
                for (job, ra, rb) in outs:
                    k.dma(conv_o.t[job, :, c * 128:(c + 1) * 128], cs_[ra:rb, 0:128], [cs_], [conv_o])
        for c in range(8):
            single("ga", c, AF.Silu)
        for c in range(8):
            single("gb", c, AF.Silu)
        for c in range(8):
            single("ma", c, AF.Sigmoid)
        for c in range(8):
            single("mb", c, AF.Sigmoid)
        k.barrier()
    P1.close()
    PC.close()

    LMAX, NKT = cfg.LMAX, cfg.NKT
    with ExitStack() as st:
        Vc = sb(st, "Vc", [128, NKT, 256], BF16)
        kT = sb(st, "kT", [128, 2, LMAX], BF16)
        ikT = sb(st, "ikT", [64, LMAX], BF16)
        score = [sb(st, "score%d" % i, [128, LMAX], F32) for i in range(2)]
        mbias = [sb(st, "mbias%d" % i, [128, LMAX], BF16) for i in range(2)]
        pw = sb(st, "pw", [128, NITER + 1], F32)
        for i in range(NITER + 1):
            k.do(k.dve, "memset", [], [pw], pw[:, i:i + 1], 2.0 ** -(i + 1))
        psD = psSc = psS = psO = psN = psT = psI = None
        W = {}

        def build_cache(srcs):
            kst, vst, ist, kbb, ibb = W["kst"], W["vst"], W["ist"], W["kbb"], W["ibb"]
            kpos = 0
            bi = 0
            for (kap, vap, iap, rows, nt, deps) in srcs:
                a = bi % 2
                bi += 1
                ks_, vs_, is_, kb_, ib_ = kst[a], vst[a], ist[a], kbb[a], ibb[a]
                if nt > 1 or rows == 128:
                    k.dma(ks_[:, 0:nt, :], kap.rearrange("(a p) c -> p a c", p=128), deps, [ks_])
                    k.dma(vs_[:, 0:nt, :], vap.rearrange("(a p) c -> p a c", p=128), deps, [vs_])
                    k.dma(is_[:, 0:nt, :], iap.rearrange("(a p) c -> p a c", p=128), deps, [is_])
                else:
                    k.dma(ks_[0:rows, 0, :], kap, deps, [ks_])
                    k.dma(vs_[0:rows, 0, :], vap, deps, [vs_])
                    k.dma(is_[0:rows, 0, :], iap, deps, [is_])
                kt0 = kpos // 128
                k.do(k.pool, "tensor_copy", [ks_], [kb_], out=kb_[0:rows, 0:nt, :], in_=ks_[0:rows, 0:nt, :])
                k.do(k.dve, "tensor_copy", [vs_], [Vc], out=Vc[0:rows, kt0:kt0 + nt, :], in_=vs_[0:rows, 0:nt, :])
                k.do(k.pool, "tensor_copy", [is_], [ib_], out=ib_[0:rows, 0:nt, :], in_=is_[0:rows, 0:nt, :])
                for a_ in range(nt):
                    for g in range(2):
                        k.do(k.pe, "transpose", [kb_, identb], [psT], psT[:, g, a_ * 128:a_ * 128 + rows], kb_[0:rows, a_, g * 128:(g + 1) * 128], identb[0:rows, 0:rows], sig=False)
                    k.do(k.pe, "transpose", [ib_, identb], [psI], psI[0:64, a_ * 128:a_ * 128 + rows], ib_[0:rows, a_, :], identb[0:rows, 0:rows], sig=(a_ == nt - 1))
                n = (nt - 1) * 128 + rows
                k.do(k.act, "copy", [psT], [kT], out=kT[:, :, kpos:kpos + n], in_=psT[:, :, 0:n])
                k.do(k.dve, "tensor_copy", [psI], [ikT], out=ikT[:, kpos:kpos + n], in_=psI[0:64, 0:n])
                kpos += n
            return kpos

        ctr = {"d": 0, "r": 0, "s": 0, "p": 0}

        class QB_:
            def __init__(self, bi, tok0, QB, LK, topk, diag, I4):
                self.bi, self.tok0, self.QB, self.LK, self.topk, self.diag, self.I4 = bi, tok0, QB, LK, topk, diag, I4
                self.q_, self.ga_, self.og_ = W["qT"][bi % 2], W["gaT"][bi % 2], W["ogT"][0]
                self.iq_, self.iw_, self.wd_ = W["iqT"][bi % 2], W["iwt"][bi % 2], W["Wd"][bi % 2]
                self.sc, self.mb = score[bi % 2], mbias[bi % 2]
                self.bs, self.Hh, self.LO, self.mid = W["bs"][bi % 2], W["Hh"][bi % 2], W["LO"][bi % 2], W["mid"][bi % 2]
                self.jD, self.jA, self.SA, self.TA = W["jD"][bi % 2], W["jA"][bi % 2], W["SA"][bi % 2], W["TA"][bi % 2]

            def early_loads(self):
                QB, tok0 = self.QB, self.tok0
                k.dma(self.iq_[:, :, 0:QB], S_iq.t.rearrange("h d t -> d h t")[:, :, tok0:tok0 + QB], [S_iq], [self.iq_])
                k.dma(self.iw_[0:QB, :], S_iw.t[tok0:tok0 + QB, :], [S_iw], [self.iw_])
                for h in range(8):
                    k.do(k.pool, "tensor_scalar", [identf, self.iw_], [self.wd_], out=self.wd_[0:QB, h, 0:QB], in0=identf[0:QB, 0:QB], scalar1=self.iw_[0:QB, h:h + 1], scalar2=None, op0=ALU.mult, sig=(h == 7))

            def late_loads(self):
                QB, tok0 = self.QB, self.tok0
                k.dma(self.q_[:, :, 0:QB], fm_view(S["q"], tok0, QB), [S["q"]], [self.q_])
                k.dma(self.ga_[:, :, 0:QB], fm_view(S["ga"], tok0, QB), [S["ga"]], [self.ga_])

            def gen_I(self):
                QB, LK, iq_, wd_, sc = self.QB, self.LK, self.iq_, self.wd_, self.sc
                Rb = W["Rb"]
                nkt = (LK + 511) // 512
                for kt in range(nkt):
                    k0 = kt * 512
                    wk = min(512, LK - k0)

                    def dmm(h):
                        p = psD[ctr["d"] % 3]
                        ctr["d"] += 1
                        k.do(k.pe, "matmul", [iq_, ikT], [p], p[0:QB, 0:wk], iq_[:, h, 0:QB], ikT[:, k0:k0 + wk], start=True, stop=True)
                        return p
                    pq = [dmm(0), dmm(1)]
                    rq = []
                    for h in range(8):
                        p = pq.pop(0)
                        r_ = Rb[ctr["r"] % 3]
                        ctr["r"] += 1
                        k.do(k.act, "activation", [p], [r_], out=r_[0:QB, 0:wk], in_=p[0:QB, 0:wk], func=AF.Relu)
                        if h + 2 < 8:
                            pq.append(dmm(h + 2))
                        k.do(k.pe, "matmul", [wd_, r_], [psSc], psSc[0:QB, 0:wk], wd_[0:QB, h, 0:QB], r_[0:QB, 0:wk], start=(h == 0), stop=(h == 7), sig=(h == 7))
                        if h < 7:
                            yield
                    k.do(k.act, "copy", [psSc], [sc], out=sc[0:QB, k0:k0 + wk], in_=psSc[0:QB, 0:wk])
                    yield

            def gen_T(self):
                QB, LK, topk, sc, mb = self.QB, self.LK, self.topk, self.sc, self.mb
                bs, Hh, LO, mid = self.bs, self.Hh, self.LO, self.mid
                NIT = NITER
                if LK > topk:
                    nseg = topk // 8
                    base = LK - 64 if self.diag else LK
                    seglen = base // nseg if nseg > 0 else 0
                    if topk % 8 == 0 and nseg * seglen == base and seglen >= 8:
                        NIT = NITER - 3
                        M8, T8 = W["M8"][self.bi % 2], W["T8"][self.bi % 2]
                        for i in range(nseg):
                            k.do(k.dve, "max", [sc], [M8], out=M8[0:QB, i * 8:(i + 1) * 8], in_=sc[0:QB, i * seglen:(i + 1) * seglen], sig=(i == nseg - 1))
                            if i % 8 == 7:
                                yield
                        k.do(k.dve, "tensor_reduce", [M8], [T8], out=T8[0:QB, 0:nseg], in_=M8[0:QB, 0:nseg * 8].rearrange("p (a b) -> p a b", b=8), axis=AX.X, op=ALU.min)
                        k.do(k.dve, "tensor_reduce", [T8], [bs], out=bs[0:QB, 0:1], in_=T8[0:QB, 0:nseg], axis=AX.X, op=ALU.min)
                        k.do(k.dve, "tensor_reduce", [T8], [bs], out=bs[0:QB, 1:2], in_=T8[0:QB, 0:nseg], axis=AX.X, op=ALU.max)
                        if self.diag:
                            k.do(k.dve, "tensor_reduce", [sc], [bs], out=bs[0:QB, 6:7], in_=sc[0:QB, LK - 64:LK], axis=AX.X, op=ALU.max)
                            k.do(k.dve, "tensor_tensor", [bs], [bs], out=bs[0:QB, 1:2], in0=bs[0:QB, 1:2], in1=bs[0:QB, 6:7], op=ALU.max)
                    else:
                        k.do(k.dve, "tensor_reduce", [sc], [bs], out=bs[0:QB, 0:1], in_=sc[0:QB, 0:LK], axis=AX.X, op=ALU.min)
                        k.do(k.dve, "tensor_reduce", [sc], [bs], out=bs[0:QB, 1:2], in_=sc[0:QB, 0:LK], axis=AX.X, op=ALU.max)
                    yield
                if self.diag:
                    k.do(k.dve, "memset", [], [sc], sc[0:64, LK - 64:LK], NEG)
                if LK > topk:
                    jD, jA, SA, MID, U, TA = self.jD, self.jA, self.SA, self.mid, self.LO, self.TA
                    k.do(k.dve, "tensor_tensor", [bs], [bs], out=bs[0:QB, 2:3], in0=bs[0:QB, 1:2], in1=bs[0:QB, 0:1], op=ALU.subtract)
                    k.do(k.dve, "tensor_scalar", [pw, bs], [Hh], out=Hh[0:QB, :], in0=pw[0:QB, :], scalar1=bs[0:QB, 2:3], scalar2=None, op0=ALU.mult)
                    k.do(k.dve, "tensor_tensor", [bs, Hh], [MID], out=MID[0:QB, 0:1], in0=bs[0:QB, 0:1], in1=Hh[0:QB, 0:1], op=ALU.add)
                    j0 = jD[0:QB, 0:1]
                    k1 = LK
                    if LK >= 1024:
                        k1 = max(128, int(round(LK * (1.0 - ACT_FRAC) / 128.0)) * 128)
                    nA = LK - k1
                    jb = bass.AP(j0.tensor, j0.offset, [list(j0.ap[0]), [0, k1]])
                    thrc = float(topk) - 0.5 * nA
                    if nA > 0:
                        a0 = jA[0:QB, 0:1]
                        jbA = bass.AP(a0.tensor, a0.offset, [list(a0.ap[0]), [0, nA]])
                        k.do(k.dve, "memset", [], [SA], SA[0:QB, :], 0.0)
                        k.do(k.dve, "memset", [], [TA], TA[0:QB, NITER:NITER + 1], thrc)
                    for it in range(NIT):
                        if nA > 0:
                            k.do(k.act, "activation", [sc, MID], [jA, SA], out=jbA, in_=sc[0:QB, k1:LK], func=AF.Sign, scale=-1.0, bias=MID[0:QB, it:it + 1], accum_out=SA[0:QB, it:it + 1])
                            k.do(k.act, "activation", [SA, TA], [TA], out=TA[0:QB, it:it + 1], in_=SA[0:QB, it:it + 1], func=AF.Identity, scale=0.5, bias=TA[0:QB, NITER:NITER + 1])
                        k.do(k.dve, "tensor_scalar", [sc, MID], [jD, bs], out=jb, in0=sc[0:QB, 0:k1], scalar1=MID[0:QB, it:it + 1], scalar2=None, op0=ALU.is_ge, op1=ALU.add, accum_out=bs[0:QB, 3:4])
                        if nA > 0:
                            k.do(k.dve, "tensor_scalar", [bs, TA, Hh], [U], out=U[0:QB, it:it + 1], in0=bs[0:QB, 3:4], scalar1=TA[0:QB, it:it + 1], scalar2=Hh[0:QB, it:it + 1], op0=ALU.is_ge, op1=ALU.mult)
                        else:
                            k.do(k.dve, "tensor_scalar", [bs, Hh], [U], out=U[0:QB, it:it + 1], in0=bs[0:QB, 3:4], scalar1=thrc, scalar2=Hh[0:QB, it:it + 1], op0=ALU.is_ge, op1=ALU.mult)
                        k.do(k.dve, "tensor_scalar", [U, MID, Hh], [MID], out=MID[0:QB, it + 1:it + 2], in0=U[0:QB, it:it + 1], scalar1=MID[0:QB, it:it + 1], scalar2=Hh[0:QB, it + 1:it + 2], op0=ALU.add, op1=ALU.subtract)
                        yield
                    k.do(k.dve, "tensor_tensor", [MID, Hh], [U], out=U[0:QB, NIT:NIT + 1], in0=MID[0:QB, NIT:NIT + 1], in1=Hh[0:QB, NIT:NIT + 1], op=ALU.subtract)
                    thr = U[0:QB, NIT:NIT + 1]
                    k.do(k.dve, "tensor_scalar", [sc, U], [mb], out=mb[0:QB, 0:LK], in0=sc[0:QB, 0:LK], scalar1=thr, scalar2=MASKV, op0=ALU.is_lt, op1=ALU.mult)
                else:
                    k.do(k.dve, "tensor_scalar", [sc], [mb], out=mb[0:QB, 0:LK], in0=sc[0:QB, 0:LK], scalar1=-1.0e29, scalar2=MASKV, op0=ALU.is_lt, op1=ALU.mult)
                yield

            def gen_A(self):
                QB, LK, q_, ga_, og_, mb, I4 = self.QB, self.LK, self.q_, self.ga_, self.og_, self.mb, self.I4
                PTb, rden = W["PTb"], W["rden"]
                N4 = 4 * QB
                nj = (LK + 127) // 128
                for g in range(2):
                    def qk(j):
                        wk = min(128, LK - j * 128)
                        p = psS[ctr["s"] % 2]
                        ctr["s"] += 1
                        k.do(k.pe, "matmul", [kT, q_], [p], p[0:wk, 0:N4], kT[:, g, j * 128:j * 128 + wk], q_[:, 4 * g:4 * g + 4, 0:QB], start=True, stop=False, sig=False)
                        k.do(k.pe, "matmul", [mb, I4], [p], p[0:wk, 0:N4], mb[0:QB, j * 128:j * 128 + wk], I4[0:QB, 0:N4], start=False, stop=True)
                        return p, wk
                    pend = qk(0)
                    for j in range(nj):
                        p, wk = pend
                        if j + 1 < nj:
                            pend = qk(j + 1)
                        pt = PTb[ctr["p"] % 3]
                        ctr["p"] += 1
                        k.do(k.act, "activation", [p], [pt], out=pt[0:wk, 0:N4], in_=p[0:wk, 0:N4], func=AF.Exp)
                        k.do(k.pe, "matmul", [Vc, pt], [psO], psO[:, 0:N4], Vc[0:wk, j, g * 128:(g + 1) * 128], pt[0:wk, 0:N4], start=(j == 0), stop=(j == nj - 1), sig=False)
                        k.do(k.pe, "matmul", [onesb, pt], [psN], psN[:, 0:N4], onesb[0:wk, :], pt[0:wk, 0:N4], start=(j == 0), stop=(j == nj - 1), sig=True)
                        yield
                    k.do(k.dve, "reciprocal", [psN], [rden], out=rden[:, 0:N4], in_=psN[:, 0:N4])
                    k.do(k.dve, "tensor_tensor", [psO, rden], [rden], out=rden[:, 0:N4], in0=psO[:, 0:N4], in1=rden[:, 0:N4], op=ALU.mult)
                    k.do(k.pool, "tensor_tensor", [rden, ga_], [og_], out=og_[:, 4 * g:4 * g + 4, 0:QB], in0=rden[:, 0:N4].rearrange("p (a b) -> p a b", b=QB), in1=ga_[:, 4 * g:4 * g + 4, 0:QB], op=ALU.mult)
                    yield
                k.dma(fm_view(S["og"], self.tok0, QB), og_[:, :, 0:QB], [og_], [S["og"]])

        def run(gen):
            for _ in gen:
                pass

        def interleave(gens):
            items = []
            for g in gens:
                steps = []
                items.append((g, steps))
            live = [g for g in gens]
            while live:
                for g in list(live):
                    try:
                        next(g)
                    except StopIteration:
                        live.remove(g)

        def interleave_n(gl):
            st_ = [[g, n, 0, False] for (g, n) in gl]
            while True:
                live = [e for e in st_ if not e[3]]
                if not live:
                    break
                e = min(live, key=lambda e: e[2] / max(e[1], 1))
                try:
                    next(e[0])
                    e[2] += 1
                except StopIteration:
                    e[3] = True

        def interleave_ratio(gA, nA, gT, nT):
            ia = it = 0
            doneA = doneT = False
            while not (doneA and doneT):
                fa = ia / max(nA, 1)
                ft = it / max(nT, 1)
                pick_T = (not doneT) and (doneA or ft <= fa)
                if pick_T:
                    try:
                        next(gT)
                        it += 1
                    except StopIteration:
                        doneT = True
                else:
                    try:
                        next(gA)
                        ia += 1
                    except StopIteration:
                        doneA = True

        jobs = []
        srcs = []
        for t0 in range(0, TP, 256):
            nt = min(2, (TP - t0) // 128)
            srcs.append((k_all.t[t0:t0 + nt * 128, :], v_all.t[t0:t0 + nt * 128, :], ik_all.t[t0:t0 + nt * 128, :], 128, nt, [k_all, v_all, ik_all]))
        jobs.append(("p", srcs))
        for s in range(NS):
            srcs = []
            for t0 in range(0, PAST, 256):
                nt = min(2, (PAST - t0) // 128)
                srcs.append((ck_d[s, t0:t0 + nt * 128, :], cv_d[s, t0:t0 + nt * 128, :], cik_d[s, t0:t0 + nt * 128, :], 128, nt, []))
            a0 = TP + s * TS
            srcs.append((k_all.t[a0:a0 + TS, :], v_all.t[a0:a0 + TS, :], ik_all.t[a0:a0 + TS, :], TS, 1, [k_all, v_all, ik_all]))
            jobs.append(("s%d" % s, srcs))
        gbi = 0
        for ji, (jn, srcs) in enumerate(jobs):
            with ExitStack() as st2:
                psT = ps(st2, "psT_" + jn, [128, 2, 512], BF16)
                psI = ps(st2, "psI_" + jn, [128, 512], BF16)
                W["kst"] = [sb(st2, "kst%d" % i, [128, 2, 256], F32) for i in range(2)]
                W["vst"] = [sb(st2, "vst%d" % i, [128, 2, 256], F32) for i in range(2)]
                W["ist"] = [sb(st2, "ist%d" % i, [128, 2, 64], F32) for i in range(2)]
                W["kbb"] = [sb(st2, "kbb%d" % i, [128, 2, 256], BF16) for i in range(2)]
                W["ibb"] = [sb(st2, "ibb%d" % i, [128, 2, 64], BF16) for i in range(2)]
                build_cache(srcs)
                k.barrier()
            with ExitStack() as st2:
                psD = [ps(st2, "psD%d_%s" % (i, jn), [128, 512], F32) for i in range(3)]
                psSc = ps(st2, "psSc_" + jn, [128, 512], F32)
                psS = [ps(st2, "psS%d_%s" % (i, jn), [128, 512], F32) for i in range(2)]
                psO = ps(st2, "psO_" + jn, [128, 512], F32)
                psN = ps(st2, "psN_" + jn, [128, 512], F32)
                W["qT"] = [sb(st2, "qT%d" % i, [128, 8, 128], BF16) for i in range(2)]
                W["gaT"] = [sb(st2, "gaT%d" % i, [128, 8, 128], BF16) for i in range(2)]
                W["ogT"] = [sb(st2, "ogT%d" % i, [128, 8, 128], BF16) for i in range(1)]
                W["iqT"] = [sb(st2, "iqT%d" % i, [64, 8, 128], BF16) for i in range(2)]
                W["iwt"] = [sb(st2, "iwt%d" % i, [128, 8], F32) for i in range(2)]
                W["Wd"] = [sb(st2, "Wd%d" % i, [128, 8, 128], BF16) for i in range(2)]
                W["Rb"] = [sb(st2, "Rb%d" % i, [128, 512], BF16) for i in range(3)]
                W["PTb"] = [sb(st2, "PTb%d" % i, [128, 512], BF16) for i in range(3)]
                W["rden"] = sb(st2, "rden", [128, 512], F32)
                W["bs"] = [sb(st2, "bs%d" % i, [128, 8], F32) for i in range(2)]
                W["Hh"] = [sb(st2, "Hh%d" % i, [128, NITER + 1], F32) for i in range(2)]
                W["LO"] = [sb(st2, "LO%d" % i, [128, NITER + 1], F32) for i in range(2)]
                W["M8"] = [sb(st2, "M8_%d" % i, [128, 256], F32) for i in range(2)]
                W["T8"] = [sb(st2, "T8_%d" % i, [128, 32], F32) for i in range(2)]
                W["mid"] = [sb(st2, "mid%d" % i, [128, NITER + 2], F32) for i in range(2)]
                W["TA"] = [sb(st2, "TA%d" % i, [128, NITER + 1], F32) for i in range(2)]
                W["jD"] = [sb(st2, "jD%d" % i, [128, 2], F32) for i in range(2)]
                W["jA"] = [sb(st2, "jA%d" % i, [128, 2], F32) for i in range(2)]
                W["SA"] = [sb(st2, "SA%d" % i, [128, NITER], F32) for i in range(2)]
                if ji == 0:
                    NB = TP // 128
                    blocks = [QB_(b, b * 128, 128, (b + 1) * 128, cfg.topk_p, True, I4p) for b in range(NB)]
                else:
                    blocks = [QB_(0, TP + (ji - 1) * TS, TS, cfg.LS, cfg.topk_s, False, I4s)]
                NB = len(blocks)
                blocks[0].early_loads()
                blocks[0].late_loads()
                run(blocks[0].gen_I())
                for b in range(NB):
                    gl = []
                    if b + 1 < NB:
                        blocks[b + 1].early_loads()
                    nT = (NITER + 2) if blocks[b].LK > blocks[b].topk else 1
                    gl.append((blocks[b].gen_T(), nT))
                    if b >= 1:
                        gl.append((blocks[b - 1].gen_A(), 2 * ((blocks[b - 1].LK + 127) // 128 + 1)))
                    if b + 1 < NB:
                        gl.append((blocks[b + 1].gen_I(), 8 * ((blocks[b + 1].LK + 511) // 512)))
                    interleave_n(gl)
                    if b + 1 < NB:
                        blocks[b + 1].late_loads()
                run(blocks[NB - 1].gen_A())
                k.barrier()
        k.barrier()

    TW3 = 512
    with ExitStack() as st:
        wab = sb(st, "wab", [128, 8, D], BF16)
        wbb = sb(st, "wbb", [128, 8, D], BF16)
        wob = sb(st, "wob", [128, 8, D], BF16)
        wdw = sb(st, "wdw", [128, 8, 31], F32)
        vec = sb(st, "vec", [128, 3, 8], F32)
        gfb = sb(st, "gfb", [128, D], F32)
        onesf = sb(st, "onesf", [128, 128], F32)
        gate_p = sb(st, "gate_p", [128, D], F32)
        gate_s = [sb(st, "gate_s%d" % i, [TS, D], F32) for i in range(NS)]
        k.dma(gate_p[:], G_p.t, [G_p], [gate_p])
        for i in range(NS):
            k.dma(gate_s[i][:], G_s[i].t, [G_s[i]], [gate_s[i]])
        k.do(k.dve, "memset", [], [onesf], onesf[:], 1.0 / D)
        k.dma(wdw[:], wdw_d.rearrange("p (a b) -> p a b", b=31), [], [wdw])
        k.dma(vec[:], vec_d.rearrange("p (a b) -> p a b", b=8), [], [vec])
        k.dma(gfb[:], gf_d, [], [gfb])
        with ExitStack() as st2:
            wl = [sb(st2, "wl%d" % i, [128, 2 * D], F32) for i in range(2)]
            li = 0
            for (src, dst) in [(wa_d, wab), (wb_d, wbb), (wo_d, wob)]:
                for q4 in range(4):
                    w_ = wl[li % 2]
                    li += 1
                    k.dma(w_[:], src[:, q4 * 2 * D:(q4 + 1) * 2 * D], [], [w_])
                    k.do(k.pool if li % 2 else k.dve, "tensor_copy", [w_], [dst], out=dst[:, 2 * q4:2 * q4 + 2, :], in_=w_[:].rearrange("p (a b) -> p a b", b=D))
            k.barrier()
        uext = [sb(st, "uext%d" % i, [128, 8, 30 + TW3], BF16) for i in range(2)]
        gbt = [sb(st, "gbt%d" % i, [128, 8, TW3], BF16) for i in range(1)]
        mat = [sb(st, "mat%d" % i, [128, 8, TW3], BF16) for i in range(1)]
        mbt = [sb(st, "mbt%d" % i, [128, 8, TW3], BF16) for i in range(1)]
        ogt = [sb(st, "ogt%d" % i, [128, 8, TW3], BF16) for i in range(1)]
        sstg = sb(st, "sstg", [128, 8, 30], F32)
        acc = [sb(st, "acc%d" % i, [128, TW3], F32) for i in range(8)]
        sq = [sb(st, "sq%d" % i, [128, TW3], F32) for i in range(2)]
        diag = [sb(st, "diag%d" % i, [128, 128], BF16) for i in range(6)]
        mean_sb = sb(st, "mean_sb", [128, TW3], F32)
        rstd_bc = sb(st, "rstd_bc", [128, TW3], F32)
        t1 = [sb(st, "t1_%d" % i, [128, TW3], F32) for i in range(2)]
        t2 = [sb(st, "t2_%d" % i, [128, TW3], F32) for i in range(2)]
        ybin = sb(st, "ybin", [128, 8, TW3], BF16)
        mixed = sb(st, "mixed", [128, 8, TW3], BF16)
        m1 = [sb(st, "m1_%d" % i, [128, TW3], F32) for i in range(2)]
        m2 = [sb(st, "m2_%d" % i, [128, TW3], F32) for i in range(2)]
        xt3 = [sb(st, "xt3_%d" % i, [128, D], F32) for i in range(2)]
        xn = [sb(st, "xn%d" % i, [128, D], F32) for i in range(2)]
        yt = [sb(st, "yt%d" % i, [128, D], F32) for i in range(2)]
        junk3 = sb(st, "junk3", [128, D], F32)
        st3 = [sb(st, "st3_%d" % i, [128, 4], F32) for i in range(2)]
        psM = ps(st, "psM", [128, 512], F32)
        psQ = ps(st, "psQ", [128, 512], F32)
        psA = [ps(st, "psA%d" % i, [128, 512], F32) for i in range(1)]
        psB = [ps(st, "psB%d" % i, [128, 512], F32) for i in range(1)]
        psC = [ps(st, "psC%d" % i, [128, 512], F32) for i in range(2)]
        psY = [ps(st, "psY%d" % i, [128, 512], F32) for i in range(2)]
        cnt = {"t": 0, "c": 0, "y": 0, "m": 0}

        def tile3(tok0, TW, first, sidx, gate):
            i = cnt["t"]
            cnt["t"] += 1
            ue, gb_, ma_, mb_, og_ = uext[i % 2], gbt[0], mat[0], mbt[0], ogt[0]
            if first:
                if sidx is None:
                    k.do(k.pool, "memset", [], [ue], ue[:, :, 0:30], 0.0)
                else:
                    k.dma(sstg[:], sconv_d[sidx].rearrange("p (a b) -> p a b", b=30), [], [sstg])
                    k.do(k.pool, "tensor_copy", [sstg], [ue], out=ue[:, :, 0:30], in_=sstg[:])
                k.dma(ue[:, :, 30:30 + TW], fm_view(S["ua"], tok0, TW), [S["ua"]], [ue])
            else:
                k.dma(ue[:, :, 0:30 + TW], fm_view(S["ua"], tok0 - 30, TW + 30), [S["ua"]], [ue])
            k.dma(gb_[:, :, 0:TW], fm_view(S["gb"], tok0, TW), [S["gb"]], [gb_])
            k.dma(ma_[:, :, 0:TW], fm_view(S["ma"], tok0, TW), [S["ma"]], [ma_])
            k.dma(mb_[:, :, 0:TW], fm_view(S["mb"], tok0, TW), [S["mb"]], [mb_])
            k.dma(og_[:, :, 0:TW], fm_view(S["og"], tok0, TW), [S["og"]], [og_])
            for c in range(8):
                pcv = psC[c % 2]
                for j in range(31):
                    dg = diag[cnt["c"] % 6]
                    cnt["c"] += 1
                    k.do(k.dve, "tensor_scalar", [identf, wdw], [dg], out=dg[:], in0=identf[:], scalar1=wdw[:, c, j:j + 1], scalar2=None, op0=ALU.mult)
                    k.do(k.pe, "matmul", [dg, ue], [pcv], pcv[:, 0:TW], dg[:], ue[:, c, j:j + TW], start=(j == 0), stop=(j == 30))
                k.do(k.act, "activation", [pcv, vec], [acc[c]], out=acc[c][:, 0:TW], in_=pcv[:, 0:TW], func=AF.Identity, bias=vec[:, 0, c:c + 1])
            for c in range(8):
                s_ = sq[c % 2]
                k.do(k.act, "activation", [acc[c]], [s_], out=s_[:, 0:TW], in_=acc[c][:, 0:TW], func=AF.Square)
                k.do(k.pe, "matmul", [onesf, acc[c]], [psM], psM[:, 0:TW], onesf[:], acc[c][:, 0:TW], start=(c == 0), stop=(c == 7), sig=False)
                k.do(k.pe, "matmul", [onesf, s_], [psQ], psQ[:, 0:TW], onesf[:], s_[:, 0:TW], start=(c == 0), stop=(c == 7), sig=True)
            k.do(k.act, "copy", [psM], [mean_sb], out=mean_sb[:, 0:TW], in_=psM[:, 0:TW])
            k.do(k.dve, "tensor_tensor", [mean_sb], [rstd_bc], out=rstd_bc[:, 0:TW], in0=mean_sb[:, 0:TW], in1=mean_sb[:, 0:TW], op=ALU.mult)
            k.do(k.dve, "tensor_tensor", [psQ, rstd_bc], [rstd_bc], out=rstd_bc[:, 0:TW], in0=psQ[:, 0:TW], in1=rstd_bc[:, 0:TW], op=ALU.subtract)
            k.do(k.act, "activation", [rstd_bc, epsT], [rstd_bc], out=rstd_bc[:, 0:TW], in_=rstd_bc[:, 0:TW], func=AF.Sqrt, bias=epsT[:, 0:1])
            k.do(k.dve, "reciprocal", [rstd_bc], [rstd_bc], out=rstd_bc[:, 0:TW], in_=rstd_bc[:, 0:TW])
            for c in range(8):
                a_, b_ = t1[c % 2], t2[c % 2]
                k.do(k.dve, "tensor_tensor", [acc[c], mean_sb], [a_], out=a_[:, 0:TW], in0=acc[c][:, 0:TW], in1=mean_sb[:, 0:TW], op=ALU.subtract)
                k.do(k.dve, "tensor_tensor", [a_, rstd_bc], [a_], out=a_[:, 0:TW], in0=a_[:, 0:TW], in1=rstd_bc[:, 0:TW], op=ALU.mult)
                k.do(k.act, "activation", [a_, vec], [b_], out=b_[:, 0:TW], in_=a_[:, 0:TW], func=AF.Silu, scale=vec[:, 1, c:c + 1], bias=vec[:, 2, c:c + 1])
                k.do(k.dve, "tensor_tensor", [b_, gb_], [ybin], out=ybin[:, c, 0:TW], in0=b_[:, 0:TW], in1=gb_[:, c, 0:TW], op=ALU.mult)
            for cc in range(8):
                pa, pb = psA[0], psB[0]
                for kc in range(8):
                    k.do(k.pe, "matmul", [wab, og_], [pa], pa[:, 0:TW], wab[:, kc, cc * 128:(cc + 1) * 128], og_[:, kc, 0:TW], start=(kc == 0), stop=(kc == 7), sig=(kc == 7))
                for kc in range(8):
                    k.do(k.pe, "matmul", [wbb, ybin], [pb], pb[:, 0:TW], wbb[:, kc, cc * 128:(cc + 1) * 128], ybin[:, kc, 0:TW], start=(kc == 0), stop=(kc == 7), sig=(kc == 7))
                a_, b_ = m1[cc % 2], m2[cc % 2]
                k.do(k.dve, "tensor_tensor", [pa, ma_], [a_], out=a_[:, 0:TW], in0=pa[:, 0:TW], in1=ma_[:, cc, 0:TW], op=ALU.mult)
                k.do(k.dve, "tensor_tensor", [pb, mb_], [b_], out=b_[:, 0:TW], in0=pb[:, 0:TW], in1=mb_[:, cc, 0:TW], op=ALU.mult)
                k.do(k.pool, "tensor_tensor", [a_, b_], [mixed], out=mixed[:, cc, 0:TW], in0=a_[:, 0:TW], in1=b_[:, 0:TW], op=ALU.add)
            for s0 in range(0, TW, 128):
                rows = min(128, TW - s0)
                yi = cnt["y"]
                cnt["y"] += 1
                x_, xn_, y_, s_ = xt3[yi % 2], xn[yi % 2], yt[yi % 2], st3[yi % 2]
                k.dma(x_[0:rows, :], x_all[tok0 + s0:tok0 + s0 + rows, :], [], [x_])
                for half in range(2):
                    py = psY[half]
                    for kc in range(8):
                        k.do(k.pe, "matmul", [mixed, wob], [py], py[0:rows, :], mixed[:, kc, s0:s0 + rows], wob[:, kc, half * 512:(half + 1) * 512], start=(kc == 0), stop=(kc == 7), sig=(kc == 7))
                    hs = slice(half * 512, half * 512 + 512)
                    k.do(k.dve, "tensor_tensor", [py, gate], [xn_], out=xn_[0:rows, hs], in0=py[0:rows, :], in1=gate[0:rows, hs], op=ALU.mult)
                k.do(k.pool, "tensor_tensor", [xn_, x_], [xn_], out=xn_[0:rows, :], in0=xn_[0:rows, :], in1=x_[0:rows, :], op=ALU.add)
                k.do(k.act, "activation", [xn_], [junk3, s_], out=junk3[0:rows, :], in_=xn_[0:rows, :], func=AF.Square, accum_out=s_[0:rows, 0:1])
                k.do(k.act, "activation", [s_, epsT], [s_], out=s_[0:rows, 1:2], in_=s_[0:rows, 0:1], func=AF.Sqrt, scale=1.0 / D, bias=epsT[0:rows, 0:1])
                k.do(k.dve, "reciprocal", [s_], [s_], out=s_[0:rows, 2:3], in_=s_[0:rows, 1:2])
                k.do(k.dve, "scalar_tensor_tensor", [xn_, s_, gfb], [y_], out=y_[0:rows, :], in0=xn_[0:rows, :], scalar=s_[0:rows, 2:3], in1=gfb[0:rows, :], op0=ALU.mult, op1=ALU.mult)
                k.dma(y_all.t[tok0 + s0:tok0 + s0 + rows, :], y_[0:rows, :], [y_], [y_all])

        for t0 in range(0, TP, TW3):
            tile3(t0, min(TW3, TP - t0), t0 == 0, None, gate_p)
        for s in range(NS):
            tile3(TP + s * TS, TS, True, s, gate_s[s])
        k.barrier()
    k.final_wait()
    P0.close()
    k.stack0.close()
    return nc, k


def _consts():
    c = np.zeros((128, 128 + 192 + 512 + 128), np.float32)
    c[:, 0:128] = np.eye(128, dtype=np.float32)
    c[0, 128:256] = 1.0
    c[1, 256:288] = 1.0
    c[2, 288:320] = 1.0
    for r in range(4):
        c[:, 320 + r * 128:320 + (r + 1) * 128] = np.eye(128, dtype=np.float32)
        c[0:32, 832 + r * 32:832 + (r + 1) * 32] = np.eye(32, dtype=np.float32)
    return c


def _kc_layout(w):
    n = w.shape[1]
    return np.ascontiguousarray(w.reshape(8, 128, n).transpose(1, 0, 2)).reshape(128, 8 * n)


def _fmT(v, n):
    return np.ascontiguousarray(v.reshape(n, 128).T)


def make_in_maps(cfg, x_prompt, x_sample, cache_k, cache_v, cache_idx_k, state_conv, c_prompt, c_sample,
                 w_ada, b_ada, g_norm, w_in, w_a, w_dw, b_dw, g_ln, b_ln, w_b, w_out, g_final, n_cores):
    f = lambda a: np.ascontiguousarray(np.asarray(a, dtype=np.float32))
    w_in0 = f(w_in[0])
    wada = _kc_layout(f(w_ada[0])).reshape(128, 8, 3072)
    bada = np.ascontiguousarray(np.broadcast_to(f(b_ada[0])[None, :], (3, 3072)))
    gn_bc = np.ascontiguousarray(np.broadcast_to(f(g_norm[0])[None, :], (128, D)))
    gf_bc = np.ascontiguousarray(np.broadcast_to(f(g_final)[None, :], (128, D)))
    wfm = np.stack([_kc_layout(w_in0[:, FM_OFF[kd] + c * 128: FM_OFF[kd] + (c + 1) * 128]) for kd in FM_KINDS for c in range(8)])
    wiq = np.stack([_kc_layout(w_in0[:, O_IQ + h * 64: O_IQ + (h + 1) * 64]) for h in range(8)])
    wkv = _kc_layout(np.concatenate([w_in0[:, O_K:O_K + 512], w_in0[:, O_IK:O_IK + 64], w_in0[:, O_IW:O_IW + 8]], axis=1))
    wa, wb, wo = _kc_layout(f(w_a[0])), _kc_layout(f(w_b[0])), _kc_layout(f(w_out[0]))
    wdwT = np.ascontiguousarray(f(w_dw[0]).T.reshape(8, 128, 31).transpose(1, 0, 2)).reshape(128, 8 * 31)
    vecT = np.concatenate([_fmT(f(b_dw[0]), 8), _fmT(f(g_ln[0]), 8), _fmT(f(b_ln[0]), 8)], axis=1)
    consts = _consts()
    NS, TS = cfg.NS, cfg.TS
    maps = []
    for i in range(n_cores):
        ss = slice(i * NS, (i + 1) * NS)
        xs = f(x_sample[ss]).reshape(NS * TS, D)
        x_all = np.concatenate([f(x_prompt[i]), xs], axis=0)
        cc = np.concatenate([f(c_prompt[i])[None], f(c_sample[ss])], axis=0)
        cT = np.ascontiguousarray(cc.reshape(3, 8, 128).transpose(2, 1, 0))
        sconv = f(state_conv[0][ss])
        sconvT = np.ascontiguousarray(sconv.reshape(NS, 30, 8, 128).transpose(0, 3, 2, 1)).reshape(NS, 128, 240)
        maps.append({
            "x_all": x_all, "cT": cT, "wada": wada, "bada": bada, "gn_bc": gn_bc, "wfm": wfm, "wiq": wiq, "wkv": wkv,
            "wa": wa, "wb": wb, "wo": wo, "wdwT": wdwT, "vecT": vecT, "gf_bc": gf_bc,
            "ck": f(cache_k[0][ss]).reshape(NS, cfg.PAST, 256), "cv": f(cache_v[0][ss]).reshape(NS, cfg.PAST, 256),
            "cik": f(cache_idx_k[0][ss]), "sconvT": sconvT, "consts": consts,
        })
    return maps


def assemble(cfg, results, n_cores):
    TP, NS, TS = cfg.TP, cfg.NS, cfg.TS
    g = lambda name: [np.asarray(r[name]) for r in results]
    y, kk, vv, ik, cv = g("y_all"), g("k_all"), g("v_all"), g("ik_all"), g("conv_o")
    y_p = np.stack([a[:TP] for a in y])
    y_s = np.concatenate([a[TP:].reshape(NS, TS, D) for a in y])
    k_p = np.stack([a[:TP].reshape(TP, 2, 128) for a in kk])[None]
    v_p = np.stack([a[:TP].reshape(TP, 2, 128) for a in vv])[None]
    ik_p = np.stack([a[:TP] for a in ik])[None]
    c_p = np.stack([a[0] for a in cv])[None]
    k_s = np.concatenate([a[TP:].reshape(NS, TS, 2, 128) for a in kk])[None]
    v_s = np.concatenate([a[TP:].reshape(NS, TS, 2, 128) for a in vv])[None]
    ik_s = np.concatenate([a[TP:].reshape(NS, TS, 64) for a in ik])[None]
    c_s = np.concatenate([a[1:] for a in cv])[None]
    return tuple(np.ascontiguousarray(a, dtype=np.float32) for a in (y_p, y_s, k_p, v_p, ik_p, c_p, k_s, v_s, ik_s, c_s))


def kernel(**inputs):
    cfg = Cfg()
    n = 8
    nc, _ = build(cfg)
    maps = make_in_maps(cfg, n_cores=n, **inputs)
    res = run_bass_kernel_spmd(nc, maps, core_ids=list(range(n)))
    return assemble(cfg, res.results, n)
```

```python
import numpy as np
from contextlib import ExitStack
import concourse.bass as bass
import concourse.mybir as mybir
from concourse.bass_utils import run_bass_kernel_spmd

F32 = mybir.dt.float32
BF16 = mybir.dt.bfloat16
AF = mybir.ActivationFunctionType
ALU = mybir.AluOpType
AX = mybir.AxisListType

D = 1024
NIN = 8264
CHUNK = 64
NITER = 22
import os
ACT_FRAC = float(os.environ.get("ACT_FRAC", "0.0"))
NEG = -1.0e30
MASKV = -30000.0

O_Q, O_K, O_V, O_GA, O_IQ, O_IW, O_IK, O_GLU, O_GB, O_M = 0, 1024, 1280, 1536, 2560, 3072, 3080, 3144, 5192, 6216
FM_KINDS = ["q", "ga", "ua", "ub", "gb", "ma", "mb"]
FM_OFF = {"q": O_Q, "ga": O_GA, "ua": O_GLU, "ub": O_GLU + 1024, "gb": O_GB, "ma": O_M, "mb": O_M + 1024}


class Sem:
    def __init__(self, k, name):
        self.h = k.stack0.enter_context(k.nc.semaphore(name))
        self.v = 0


class Buf:
    def __init__(self, t, disjoint=False):
        self.t = t
        self.w = {}
        self.r = {}
        self.disjoint = disjoint

    def __getitem__(self, idx):
        return self.t[idx]


def _merge(d, tok):
    if tok is None:
        return
    s, v = tok
    if id(s) not in d or d[id(s)][1] < v:
        d[id(s)] = (s, v)


class Eng:
    def __init__(self, k, e, name, same=True):
        self.k = k
        self.e = e
        self.name = name
        self.sem = Sem(k, "e_" + name)
        self.seen = {}
        self.same = same
        self.pending = []

    def wait_tok(self, s, v):
        if s is self.sem and not self.same:
            return
        if self.seen.get(id(s), 0) < v:
            self.e.wait_ge(s.h, v)
            self.seen[id(s)] = v

    def wait_bufs(self, reads, writes):
        for b in reads:
            for s, v in b.w.values():
                self.wait_tok(s, v)
        for b in writes:
            if b.disjoint:
                continue
            for s, v in b.w.values():
                self.wait_tok(s, v)
            for s, v in b.r.values():
                self.wait_tok(s, v)


class K:
    def __init__(self, nc):
        self.nc = nc
        self.stack0 = ExitStack()
        self.pe = Eng(self, nc.tensor, "pe", same=False)
        self.act = Eng(self, nc.scalar, "act")
        self.dve = Eng(self, nc.vector, "dve")
        self.pool = Eng(self, nc.gpsimd, "pool")
        self.sp = Eng(self, nc.sync, "sp")
        self.engs = [self.pe, self.act, self.dve, self.pool, self.sp]
        self.dsems = [Sem(self, "d%d" % i) for i in range(24)]
        self.di = 0
        self.nins = 0

    def _post(self, E, tok, reads, writes):
        for b in writes:
            if not b.disjoint:
                b.w = {}
                b.r = {}
            _merge(b.w, tok)
        for b in reads:
            _merge(b.r, tok)

    def do(self, E, fn, reads, writes, *a, sig=True, **kw):
        E.wait_bufs(reads, writes)
        ins = getattr(E.e, fn)(*a, **kw)
        self.nins += 1
        if sig:
            E.sem.v += 1
            ins.then_inc(E.sem.h, 1)
            tok = (E.sem, E.sem.v)
            for (rr, ww) in E.pending:
                self._post(E, tok, rr, ww)
            E.pending = []
            self._post(E, tok, reads, writes)
            return tok
        E.pending.append((reads, writes))
        return None

    def dma(self, out, in_, reads, writes, E=None, **kw):
        E = E or self.sp
        E.wait_bufs(reads, writes)
        s = self.dsems[self.di % len(self.dsems)]
        self.di += 1
        if s.v > 0:
            E.wait_tok(s, s.v)
        s.v += 16
        E.e.dma_start(out=out, in_=in_, **kw).then_inc(s.h, 16)
        self.nins += 1
        tok = (s, s.v)
        for b in writes:
            _merge(b.w, tok)
        for b in reads:
            _merge(b.r, tok)
        return tok

    def barrier(self):
        sems = [e.sem for e in self.engs] + self.dsems
        for E in [self.pe, self.act, self.dve, self.pool, self.sp]:
            for s in sems:
                if s.v > 0 and s is not E.sem:
                    E.wait_tok(s, s.v)

    def final_wait(self):
        for s in self.dsems:
            if s.v > 0:
                self.sp.wait_tok(s, s.v)
        for e in self.engs:
            if e is not self.sp and e.sem.v > 0:
                self.sp.wait_tok(e.sem, e.sem.v)


class Cfg:
    def __init__(self, TP=8192, PAST=2048, TS=32, NS=2, topk_max=256, debug=False):
        self.TP, self.PAST, self.TS, self.NS = TP, PAST, TS, NS
        self.debug = debug
        self.NTOK = TP + NS * TS
        self.topk_p = min(topk_max, TP // 4)
        self.topk_s = min(topk_max, (PAST + TS) // 4)
        self.LS = PAST + TS
        self.LMAX = max(TP, ((self.LS + 127) // 128) * 128)
        self.NKT = self.LMAX // 128


def build(cfg):
    TP, NTOK, TS, NS, PAST = cfg.TP, cfg.NTOK, cfg.TS, cfg.NS, cfg.PAST
    SROWS = NS * TS
    nc = bass.Bass("TRN2", target_bir_lowering=False)
    k = K(nc)

    def din(name, shape, dt=F32):
        return nc.dram_tensor(name, list(shape), dt, kind="ExternalInput").ap()

    def dout(name, shape, dt=F32):
        return nc.dram_tensor(name, list(shape), dt, kind="ExternalOutput").ap()

    def dscr(name, shape, dt=BF16):
        return Buf(nc.dram_tensor(name, list(shape), dt, kind="Internal").ap(), disjoint=True)

    x_all = din("x_all", [NTOK, D])
    cT_d = din("cT", [128, 8, 3])
    wada_d = din("wada", [128, 8, 3072])
    bada_d = din("bada", [3, 3072])
    gn_d = din("gn_bc", [128, D])
    wfm_d = din("wfm", [56, 128, 8 * 128])
    wiq_d = din("wiq", [8, 128, 8 * 64])
    wkv_d = din("wkv", [128, 8 * 584])
    wa_d = din("wa", [128, 8 * D])
    wb_d = din("wb", [128, 8 * D])
    wo_d = din("wo", [128, 8 * D])
    wdw_d = din("wdwT", [128, 8 * 31])
    vec_d = din("vecT", [128, 3 * 8])
    gf_d = din("gf_bc", [128, D])
    ck_d = din("ck", [NS, PAST, 256])
    cv_d = din("cv", [NS, PAST, 256])
    cik_d = din("cik", [NS, PAST, 64])
    sconv_d = din("sconvT", [NS, 128, 8 * 30])
    const_d = din("consts", [128, 128 + 192 + 512 + 128])

    y_all = Buf(disjoint=True, t=dout("y_all", [NTOK, D]))
    k_all = Buf(disjoint=True, t=dout("k_all", [NTOK, 256]))
    v_all = Buf(disjoint=True, t=dout("v_all", [NTOK, 256]))
    ik_all = Buf(disjoint=True, t=dout("ik_all", [NTOK, 64]))
    conv_o = Buf(disjoint=True, t=dout("conv_o", [1 + NS, 30, D]))

    S = {kind: dscr("s_" + kind, [8, 128, NTOK]) for kind in FM_KINDS}
    if cfg.debug:
        S["og"] = Buf(nc.dram_tensor("s_og", [8, 128, NTOK], BF16, kind="ExternalOutput").ap(), disjoint=True)
    else:
        S["og"] = dscr("s_og", [8, 128, NTOK])
    S_iq = dscr("s_iq", [8, 64, NTOK])
    S_iw = dscr("s_iw", [NTOK, 8], F32)

    def fm_view(b, t0, n):
        return b.t.rearrange("h d t -> d h t")[:, :, t0:t0 + n]

    P0 = ExitStack()

    uniq = [0]

    def sb(st, name, shape, dt):
        uniq[0] += 1
        return Buf(st.enter_context(nc.sbuf_tensor("%s_u%d" % (name, uniq[0]), list(shape), dt)))

    def ps(st, name, shape, dt):
        uniq[0] += 1
        return Buf(st.enter_context(nc.psum_tensor("%s_u%d" % (name, uniq[0]), list(shape), dt)))

    identf = sb(P0, "identf", [128, 128], F32)
    identb = sb(P0, "identb", [128, 128], BF16)
    G_p = Buf(nc.dram_tensor("g_p", [128, D], F32, kind="Internal").ap(), disjoint=True)
    G_s = [Buf(nc.dram_tensor("g_s%d" % i, [TS, D], F32, kind="Internal").ap(), disjoint=True) for i in range(NS)]
    epsT = sb(P0, "epsT", [128, 1], F32)
    onesb = sb(P0, "onesb", [128, 128], BF16)
    I4p = sb(P0, "I4p", [128, 512], BF16)
    I4s = sb(P0, "I4s", [32, 128], BF16)
    PC = ExitStack()
    cst = sb(PC, "cst", [128, 128 + 192 + 512 + 128], F32)

    k.dma(cst[:], const_d, [], [cst])
    k.do(k.dve, "tensor_copy", [cst], [identf], out=identf[:], in_=cst[:, 0:128])
    k.do(k.dve, "tensor_copy", [cst], [identb], out=identb[:], in_=cst[:, 0:128])
    k.do(k.dve, "memset", [], [epsT], epsT[:], 1e-6)
    k.do(k.dve, "memset", [], [onesb], onesb[:], 1.0)
    k.do(k.dve, "tensor_copy", [cst], [I4p], out=I4p[:], in_=cst[:, 320:832])
    k.do(k.dve, "tensor_copy", [cst], [I4s], out=I4s[:], in_=cst[0:32, 832:960])
    sel = lambda a, b_: cst[0:3, 128 + a:128 + b_]

    P1 = ExitStack()
    hT = sb(P1, "hT", [128, 8, NTOK], BF16)
    P1a = ExitStack()
    AB_p = sb(P1a, "AB_p", [128, 2 * D], F32)
    AB_s = sb(P1a, "AB_s", [SROWS, 2 * D], F32)
    with ExitStack() as st:
        cT = sb(st, "cT_t", [128, 8, 3], F32)
        sc = sb(st, "sc_t", [128, 8, 3], F32)
        modrows = sb(st, "modrows", [3, 3072], F32)
        gn = sb(st, "gn", [128, D], F32)
        wat = [sb(st, "wat%d" % i, [128, 8, 256], F32) for i in range(2)]
        pm = ps(st, "pm", [128, 512], F32)
        pbp = [ps(st, "pbp%d" % i, [128, 512], F32) for i in range(2)]
        k.dma(cT[:], cT_d, [], [cT])
        k.dma(modrows[:], bada_d, [], [modrows])
        k.dma(gn[:], gn_d, [], [gn])
        k.do(k.act, "activation", [cT], [sc], out=sc[:], in_=cT[:], func=AF.Silu)
        wview = wada_d
        for i in range(12):
            w = wat[i % 2]
            k.dma(w[:], wview[:, :, i * 256:(i + 1) * 256], [], [w])
            for kc in range(8):
                k.do(k.pe, "matmul", [sc, w], [pm], pm[0:3, 0:256], sc[:, kc, :], w[:, kc, :], start=(kc == 0), stop=(kc == 7), sig=(kc == 7))
            k.do(k.dve, "tensor_tensor", [pm, modrows], [modrows], out=modrows[:, i * 256:(i + 1) * 256], in0=pm[0:3, 0:256], in1=modrows[:, i * 256:(i + 1) * 256], op=ALU.add)
        gstg = [sb(st, "gstg%d" % i, [128, 512], F32) for i in range(2)]
        targets = [((0, 128), 128, AB_p, G_p), ((128, 128 + SROWS), SROWS, AB_s, None)]
        for si in range(NS):
            targets.append(((128 + si * 32, 128 + si * 32 + TS), TS, None, G_s[si]))
        bi = 0
        for (sc0, sc1), M, AB, gt in targets:
            for part in range(3):
                if part < 2 and AB is None:
                    continue
                if part == 2 and gt is None:
                    continue
                for half in range(2):
                    pb = pbp[bi % 2]
                    bi += 1
                    k.do(k.pe, "matmul", [cst, modrows], [pb], pb[0:M, :], sel(sc0, sc1), modrows[:, part * 1024 + half * 512: part * 1024 + half * 512 + 512], start=True, stop=True)
                    cs = slice(half * 512, half * 512 + 512)
                    if part == 1:
                        k.do(k.dve, "scalar_tensor_tensor", [pb, gn], [AB], out=AB[0:M, cs], in0=pb[0:M, :], scalar=1.0, in1=gn[0:M, cs], op0=ALU.add, op1=ALU.mult)
                    elif part == 0:
                        k.do(k.act, "copy", [pb], [AB], out=AB[0:M, D + half * 512: D + half * 512 + 512], in_=pb[0:M, :])
                    else:
                        g_ = gstg[bi % 2]
                        k.do(k.act, "copy", [pb], [g_], out=g_[0:M, :], in_=pb[0:M, :])
                        k.dma(gt.t[0:M, cs], g_[0:M, :], [g_], [gt])
        k.barrier()

    NT = (NTOK + 127) // 128
    with ExitStack() as st:
        wkvb = sb(st, "wkvb", [128, 8, 584], BF16)
        kvst = [sb(st, "kvst%d" % i, [128, 584], F32) for i in range(2)]
        pk0 = ps(st, "pk0", [128, 512], F32)
        pk1 = ps(st, "pk1", [128, 512], F32)
        for kc in range(8):
            b_ = kvst[kc % 2]
            k.dma(b_[:], wkv_d[:, kc * 584:(kc + 1) * 584], [], [b_])
            k.do(k.pool, "tensor_copy", [b_], [wkvb], out=wkvb[:, kc, :], in_=b_[:])
        xb = [sb(st, "xb%d" % i, [128, D], F32) for i in range(3)]
        h1 = [sb(st, "h1_%d" % i, [128, D], F32) for i in range(2)]
        hb = [sb(st, "hb%d" % i, [128, D], BF16) for i in range(2)]
        junk = sb(st, "junk1", [128, D], F32)
        stats = [sb(st, "stats%d" % i, [128, 4], F32) for i in range(3)]
        pT = [ps(st, "pT%d" % i, [128, 8, 128], BF16) for i in range(2)]
        hTt = [Buf(hT.t) for _ in range(NT)]

        def xload(tt):
            t0 = tt * 128
            rows = min(128, NTOK - t0)
            k.dma(xb[tt % 3][0:rows, :], x_all[t0:t0 + rows, :], [], [xb[tt % 3]])
        for tt in range(min(2, NT)):
            xload(tt)
        for tt in range(NT):
            t0 = tt * 128
            rows = min(128, NTOK - t0)
            AB = AB_p if t0 < TP else AB_s
            x = xb[tt % 3]
            s_ = stats[tt % 3]
            if tt + 2 < NT:
                xload(tt + 2)
            k.do(k.act, "activation", [x], [junk, s_], out=junk[0:rows, :], in_=x[0:rows, :], func=AF.Square, accum_out=s_[0:rows, 0:1])
            k.do(k.act, "activation", [s_, epsT], [s_], out=s_[0:rows, 1:2], in_=s_[0:rows, 0:1], func=AF.Sqrt, scale=1.0 / D, bias=epsT[0:rows, 0:1])
            k.do(k.dve, "reciprocal", [s_], [s_], out=s_[0:rows, 2:3], in_=s_[0:rows, 1:2])
            h_ = h1[tt % 2]
            k.do(k.dve, "scalar_tensor_tensor", [x, s_, AB], [h_], out=h_[0:rows, :], in0=x[0:rows, :], scalar=s_[0:rows, 2:3], in1=AB[0:rows, 0:D], op0=ALU.mult, op1=ALU.mult)
            hb_ = hb[tt % 2]
            k.do(k.dve, "tensor_tensor", [h_, AB], [hb_], out=hb_[0:rows, :], in0=h_[0:rows, :], in1=AB[0:rows, D:2 * D], op=ALU.add)
            p = pT[tt % 2]
            for c in range(8):
                k.do(k.pe, "transpose", [hb_, identb], [p], p[:, c, 0:rows], hb_[0:rows, c * 128:(c + 1) * 128], identb[0:rows, 0:rows], sig=(c == 7))
            k.do(k.act, "copy", [p], [hTt[tt]], out=hT[:, :, t0:t0 + rows], in_=p[:, :, 0:rows])
            for kc in range(8):
                k.do(k.pe, "matmul", [hTt[tt], wkvb], [pk0], pk0[0:rows, :], hT[:, kc, t0:t0 + rows], wkvb[:, kc, 0:512], start=(kc == 0), stop=(kc == 7), sig=False)
            for kc in range(8):
                k.do(k.pe, "matmul", [hTt[tt], wkvb], [pk1], pk1[0:rows, 0:72], hT[:, kc, t0:t0 + rows], wkvb[:, kc, 512:584], start=(kc == 0), stop=(kc == 7), sig=(kc == 7))
            o_ = kvst[tt % 2]
            k.do(k.act, "copy", [pk0], [o_], out=o_[0:rows, 0:512], in_=pk0[0:rows, :])
            k.do(k.dve, "tensor_copy", [pk1], [o_], out=o_[0:rows, 512:584], in_=pk1[0:rows, 0:72])
            k.dma(k_all.t[t0:t0 + rows, :], o_[0:rows, 0:256], [o_], [k_all])
            k.dma(v_all.t[t0:t0 + rows, :], o_[0:rows, 256:512], [o_], [v_all])
            k.dma(ik_all.t[t0:t0 + rows, :], o_[0:rows, 512:576], [o_], [ik_all])
            k.dma(S_iw.t[t0:t0 + rows, :], o_[0:rows, 576:584], [o_], [S_iw])
        k.barrier()
    P1a.close()

    groups = [(g0, min(512, NTOK - g0)) for g0 in range(0, NTOK, 512)]
    with ExitStack() as st:
        wst = [sb(st, "wst%d" % i, [128, 8 * 128], F32) for i in range(2)]
        wbf = [sb(st, "wbf%d" % i, [128, 8, 128], BF16) for i in range(3)]
        stg = [sb(st, "stg%d" % i, [128, 512], BF16) for i in range(3)]
        sig_ = [sb(st, "sig%d" % i, [128, 512], F32) for i in range(2)]
        cstg = [sb(st, "cstg%d" % i, [64, 256], F32) for i in range(2)]
        pz = [ps(st, "pz%d" % i, [128, 512], F32) for i in range(4)]
        pc = ps(st, "pcv", [128, 512], F32)
        wi = [0]
        zi = [0]
        si_ = [0]

        def load_w(src_ap, width):
            i = wi[0]
            wi[0] += 1
            ws = wst[i % 2]
            wb_ = wbf[i % 3]
            k.dma(ws[:, 0:8 * width], src_ap, [], [ws])
            k.do(k.pool, "tensor_copy", [ws], [wb_], out=wb_[:, :, 0:width], in_=ws[:, 0:8 * width].rearrange("p (a b) -> p a b", b=width))
            return wb_

        def proj(wb_, width, g0, n):
            p = pz[zi[0] % 4]
            zi[0] += 1
            for kc in range(8):
                k.do(k.pe, "matmul", [hT, wb_], [p], p[0:width, 0:n], wb_[:, kc, 0:width], hT[:, kc, g0:g0 + n], start=(kc == 0), stop=(kc == 7), sig=(kc == 7))
            return p

        def single(kind, c, func, scale=1.0):
            wb_ = load_w(wfm_d[FM_KINDS.index(kind) * 8 + c], 128)
            for (g0, n) in groups:
                p = proj(wb_, 128, g0, n)
                o_ = stg[si_[0] % 3]
                si_[0] += 1
                k.do(k.act, "activation", [p], [o_], out=o_[:, 0:n], in_=p[:, 0:n], func=func, scale=scale)
                k.dma(S[kind].t[c, :, g0:g0 + n], o_[:, 0:n], [o_], [S[kind]])

        for c in range(8):
            single("q", c, AF.Copy, scale=128.0 ** -0.5)
        for h in range(8):
            wb_ = load_w(wiq_d[h], 64)
            for (g0, n) in groups:
                p = proj(wb_, 64, g0, n)
                o_ = stg[si_[0] % 3]
                si_[0] += 1
                k.do(k.dve, "tensor_copy", [p], [o_], out=o_[0:64, 0:n], in_=p[0:64, 0:n])
                k.dma(S_iq.t[h, :, g0:g0 + n], o_[0:64, 0:n], [o_], [S_iq])
        crow_sets = [(TP - 32, 32, [(0, 2, 32)]), (TP, SROWS, [(1 + s, TS * s + 2, TS * s + TS) for s in range(NS)])]
        ci = 0
        for c in range(8):
            wA = load_w(wfm_d[FM_KINDS.index("ua") * 8 + c], 128)
            wB = load_w(wfm_d[FM_KINDS.index("ub") * 8 + c], 128)
            for (g0, n) in groups:
                pA = proj(wA, 128, g0, n)
                pB = proj(wB, 128, g0, n)
                sg = sig_[si_[0] % 2]
                o_ = stg[si_[0] % 3]
                si_[0] += 1
                k.do(k.act, "activation", [pB], [sg], out=sg[:, 0:n], in_=pB[:, 0:n], func=AF.Sigmoid)
                k.do(k.dve, "tensor_tensor", [pA, sg], [o_], out=o_[:, 0:n], in0=pA[:, 0:n], in1=sg[:, 0:n], op=ALU.mult)
                k.dma(S["ua"].t[c, :, g0:g0 + n], o_[:, 0:n], [o_], [S["ua"]])
            for (r0, M, outs) in crow_sets:
                for kc in range(8):
                    k.do(k.pe, "matmul", [hT, wA], [pc], pc[0:M, 0:128], hT[:, kc, r0:r0 + M], wA[:, kc, :], start=(kc == 0), stop=(kc == 7), sig=False)
                for kc in range(8):
                    k.do(k.pe, "matmul", [hT, wB], [pc], pc[0:M, 128:256], hT[:, kc, r0:r0 + M], wB[:, kc, :], start=(kc == 0), stop=(kc == 7), sig=(kc == 7))
                cs_ = cstg[ci % 2]
                ci += 1
                k.do(k.act, "activation", [pc], [cs_], out=cs_[0:M, 128:256], in_=pc[0:M, 128:256], func=AF.Sigmoid)
                k.do(k.dve, "tensor_tensor", [pc, cs_], [cs_], out=cs_[0:M, 0:128], in0=pc[0:M, 0:128], in1=cs_[0:M, 128:256], op=ALU.mult)
                for (job, ra, rb) in outs:
                    k.dma(conv_o.t[job, :, c * 128:(c + 1) * 128], cs_[ra:rb, 0:128], [cs_], [conv_o])
        for c in range(8):
            single("ga", c, AF.Silu)
        for c in range(8):
            single("gb", c, AF.Silu)
        for c in range(8):
            single("ma", c, AF.Sigmoid)
        for c in range(8):
            single("mb", c, AF.Sigmoid)
        k.barrier()
    P1.close()
    PC.close()

    LMAX, NKT = cfg.LMAX, cfg.NKT
    with ExitStack() as st:
        Vc = sb(st, "Vc", [128, NKT, 256], BF16)
        kT = sb(st, "kT", [128, 2, LMAX], BF16)
        ikT = sb(st, "ikT", [64, LMAX], BF16)
        score = [sb(st, "score%d" % i, [128, LMAX], F32) for i in range(2)]
        mbias = [sb(st, "mbias%d" % i, [128, LMAX], BF16) for i in range(2)]
        pw = sb(st, "pw", [128, NITER + 1], F32)
        for i in range(NITER + 1):
            k.do(k.dve, "memset", [], [pw], pw[:, i:i + 1], 2.0 ** -(i + 1))
        psD = psSc = psS = psO = psN = psT = psI = None
        W = {}

        def build_cache(srcs):
            kst, vst, ist, kbb, ibb = W["kst"], W["vst"], W["ist"], W["kbb"], W["ibb"]
            kpos = 0
            bi = 0
            for (kap, vap, iap, rows, nt, deps) in srcs:
                a = bi % 2
                bi += 1
                ks_, vs_, is_, kb_, ib_ = kst[a], vst[a], ist[a], kbb[a], ibb[a]
                if nt > 1 or rows == 128:
                    k.dma(ks_[:, 0:nt, :], kap.rearrange("(a p) c -> p a c", p=128), deps, [ks_])
                    k.dma(vs_[:, 0:nt, :], vap.rearrange("(a p) c -> p a c", p=128), deps, [vs_])
                    k.dma(is_[:, 0:nt, :], iap.rearrange("(a p) c -> p a c", p=128), deps, [is_])
                else:
                    k.dma(ks_[0:rows, 0, :], kap, deps, [ks_])
                    k.dma(vs_[0:rows, 0, :], vap, deps, [vs_])
                    k.dma(is_[0:rows, 0, :], iap, deps, [is_])
                kt0 = kpos // 128
                k.do(k.pool, "tensor_copy", [ks_], [kb_], out=kb_[0:rows, 0:nt, :], in_=ks_[0:rows, 0:nt, :])
                k.do(k.dve, "tensor_copy", [vs_], [Vc], out=Vc[0:rows, kt0:kt0 + nt, :], in_=vs_[0:rows, 0:nt, :])
                k.do(k.pool, "tensor_copy", [is_], [ib_], out=ib_[0:rows, 0:nt, :], in_=is_[0:rows, 0:nt, :])
                for a_ in range(nt):
                    for g in range(2):
                        k.do(k.pe, "transpose", [kb_, identb], [psT], psT[:, g, a_ * 128:a_ * 128 + rows], kb_[0:rows, a_, g * 128:(g + 1) * 128], identb[0:rows, 0:rows], sig=False)
                    k.do(k.pe, "transpose", [ib_, identb], [psI], psI[0:64, a_ * 128:a_ * 128 + rows], ib_[0:rows, a_, :], identb[0:rows, 0:rows], sig=(a_ == nt - 1))
                n = (nt - 1) * 128 + rows
                k.do(k.act, "copy", [psT], [kT], out=kT[:, :, kpos:kpos + n], in_=psT[:, :, 0:n])
                k.do(k.dve, "tensor_copy", [psI], [ikT], out=ikT[:, kpos:kpos + n], in_=psI[0:64, 0:n])
                kpos += n
            return kpos

        ctr = {"d": 0, "r": 0, "s": 0, "p": 0}

        class QB_:
            def __init__(self, bi, tok0, QB, LK, topk, diag, I4):
                self.bi, self.tok0, self.QB, self.LK, self.topk, self.diag, self.I4 = bi, tok0, QB, LK, topk, diag, I4
                self.q_, self.ga_, self.og_ = W["qT"][bi % 2], W["gaT"][bi % 2], W["ogT"][0]
                self.iq_, self.iw_, self.wd_ = W["iqT"][bi % 2], W["iwt"][bi % 2], W["Wd"][bi % 2]
                self.sc, self.mb = score[bi % 2], mbias[bi % 2]
                self.bs, self.Hh, self.LO, self.mid = W["bs"][bi % 2], W["Hh"][bi % 2], W["LO"][bi % 2], W["mid"][bi % 2]
                self.jD, self.jA, self.SA, self.TA = W["jD"][bi % 2], W["jA"][bi % 2], W["SA"][bi % 2], W["TA"][bi % 2]

            def early_loads(self):
                QB, tok0 = self.QB, self.tok0
                k.dma(self.iq_[:, :, 0:QB], S_iq.t.rearrange("h d t -> d h t")[:, :, tok0:tok0 + QB], [S_iq], [self.iq_])
                k.dma(self.iw_[0:QB, :], S_iw.t[tok0:tok0 + QB, :], [S_iw], [self.iw_])
                for h in range(8):
                    k.do(k.pool, "tensor_scalar", [identf, self.iw_], [self.wd_], out=self.wd_[0:QB, h, 0:QB], in0=identf[0:QB, 0:QB], scalar1=self.iw_[0:QB, h:h + 1], scalar2=None, op0=ALU.mult, sig=(h == 7))

            def late_loads(self):
                QB, tok0 = self.QB, self.tok0
                k.dma(self.q_[:, :, 0:QB], fm_view(S["q"], tok0, QB), [S["q"]], [self.q_])
                k.dma(self.ga_[:, :, 0:QB], fm_view(S["ga"], tok0, QB), [S["ga"]], [self.ga_])

            def gen_I(self):
                QB, LK, iq_, wd_, sc = self.QB, self.LK, self.iq_, self.wd_, self.sc
                Rb = W["Rb"]
                nkt = (LK + 511) // 512
                for kt in range(nkt):
                    k0 = kt * 512
                    wk = min(512, LK - k0)

                    def dmm(h):
                        p = psD[ctr["d"] % 3]
                        ctr["d"] += 1
                        k.do(k.pe, "matmul", [iq_, ikT], [p], p[0:QB, 0:wk], iq_[:, h, 0:QB], ikT[:, k0:k0 + wk], start=True, stop=True)
                        return p
                    pq = [dmm(0), dmm(1)]
                    rq = []
                    for h in range(8):
                        p = pq.pop(0)
                        r_ = Rb[ctr["r"] % 3]
                        ctr["r"] += 1
                        k.do(k.act, "activation", [p], [r_], out=r_[0:QB, 0:wk], in_=p[0:QB, 0:wk], func=AF.Relu)
                        if h + 2 < 8:
                            pq.append(dmm(h + 2))
                        k.do(k.pe, "matmul", [wd_, r_], [psSc], psSc[0:QB, 0:wk], wd_[0:QB, h, 0:QB], r_[0:QB, 0:wk], start=(h == 0), stop=(h == 7), sig=(h == 7))
                        if h < 7:
                            yield
                    k.do(k.act, "copy", [psSc], [sc], out=sc[0:QB, k0:k0 + wk], in_=psSc[0:QB, 0:wk])
                    yield

            def gen_T(self):
                QB, LK, topk, sc, mb = self.QB, self.LK, self.topk, self.sc, self.mb
                bs, Hh, LO, mid = self.bs, self.Hh, self.LO, self.mid
                if LK > topk:
                    k.do(k.dve, "tensor_reduce", [sc], [bs], out=bs[0:QB, 0:1], in_=sc[0:QB, 0:LK], axis=AX.X, op=ALU.min)
                    k.do(k.dve, "tensor_reduce", [sc], [bs], out=bs[0:QB, 1:2], in_=sc[0:QB, 0:LK], axis=AX.X, op=ALU.max)
                    yield
                if self.diag:
                    k.do(k.dve, "memset", [], [sc], sc[0:64, LK - 64:LK], NEG)
                if LK > topk:
                    jD, jA, SA, MID, U, TA = self.jD, self.jA, self.SA, self.mid, self.LO, self.TA
                    k.do(k.dve, "tensor_tensor", [bs], [bs], out=bs[0:QB, 2:3], in0=bs[0:QB, 1:2], in1=bs[0:QB, 0:1], op=ALU.subtract)
                    k.do(k.dve, "tensor_scalar", [pw, bs], [Hh], out=Hh[0:QB, :], in0=pw[0:QB, :], scalar1=bs[0:QB, 2:3], scalar2=None, op0=ALU.mult)
                    k.do(k.dve, "tensor_tensor", [bs, Hh], [MID], out=MID[0:QB, 0:1], in0=bs[0:QB, 0:1], in1=Hh[0:QB, 0:1], op=ALU.add)
                    j0 = jD[0:QB, 0:1]
                    k1 = LK
                    if LK >= 1024:
                        k1 = max(128, int(round(LK * (1.0 - ACT_FRAC) / 128.0)) * 128)
                    nA = LK - k1
                    jb = bass.AP(j0.tensor, j0.offset, [list(j0.ap[0]), [0, k1]])
                    thrc = float(topk) - 0.5 * nA
                    if nA > 0:
                        a0 = jA[0:QB, 0:1]
                        jbA = bass.AP(a0.tensor, a0.offset, [list(a0.ap[0]), [0, nA]])
                        k.do(k.dve, "memset", [], [SA], SA[0:QB, :], 0.0)
                        k.do(k.dve, "memset", [], [TA], TA[0:QB, NITER:NITER + 1], thrc)
                    for it in range(NITER):
                        if nA > 0:
                            k.do(k.act, "activation", [sc, MID], [jA, SA], out=jbA, in_=sc[0:QB, k1:LK], func=AF.Sign, scale=-1.0, bias=MID[0:QB, it:it + 1], accum_out=SA[0:QB, it:it + 1])
                            k.do(k.act, "activation", [SA, TA], [TA], out=TA[0:QB, it:it + 1], in_=SA[0:QB, it:it + 1], func=AF.Identity, scale=0.5, bias=TA[0:QB, NITER:NITER + 1])
                        k.do(k.dve, "tensor_scalar", [sc, MID], [jD, bs], out=jb, in0=sc[0:QB, 0:k1], scalar1=MID[0:QB, it:it + 1], scalar2=None, op0=ALU.is_ge, op1=ALU.add, accum_out=bs[0:QB, 3:4])
                        if nA > 0:
                            k.do(k.dve, "tensor_scalar", [bs, TA, Hh], [U], out=U[0:QB, it:it + 1], in0=bs[0:QB, 3:4], scalar1=TA[0:QB, it:it + 1], scalar2=Hh[0:QB, it:it + 1], op0=ALU.is_ge, op1=ALU.mult)
                        else:
                            k.do(k.dve, "tensor_scalar", [bs, Hh], [U], out=U[0:QB, it:it + 1], in0=bs[0:QB, 3:4], scalar1=thrc, scalar2=Hh[0:QB, it:it + 1], op0=ALU.is_ge, op1=ALU.mult)
                        k.do(k.dve, "tensor_scalar", [U, MID, Hh], [MID], out=MID[0:QB, it + 1:it + 2], in0=U[0:QB, it:it + 1], scalar1=MID[0:QB, it:it + 1], scalar2=Hh[0:QB, it + 1:it + 2], op0=ALU.add, op1=ALU.subtract)
                        yield
                    k.do(k.dve, "tensor_tensor", [MID, Hh], [U], out=U[0:QB, NITER:NITER + 1], in0=MID[0:QB, NITER:NITER + 1], in1=Hh[0:QB, NITER:NITER + 1], op=ALU.subtract)
                    thr = U[0:QB, NITER:NITER + 1]
                    k.do(k.dve, "tensor_scalar", [sc, U], [mb], out=mb[0:QB, 0:LK], in0=sc[0:QB, 0:LK], scalar1=thr, scalar2=MASKV, op0=ALU.is_lt, op1=ALU.mult)
                else:
                    k.do(k.dve, "tensor_scalar", [sc], [mb], out=mb[0:QB, 0:LK], in0=sc[0:QB, 0:LK], scalar1=-1.0e29, scalar2=MASKV, op0=ALU.is_lt, op1=ALU.mult)
                yield

            def gen_A(self):
                QB, LK, q_, ga_, og_, mb, I4 = self.QB, self.LK, self.q_, self.ga_, self.og_, self.mb, self.I4
                PTb, rden = W["PTb"], W["rden"]
                N4 = 4 * QB
                nj = (LK + 127) // 128
                for g in range(2):
                    def qk(j):
                        wk = min(128, LK - j * 128)
                        p = psS[ctr["s"] % 2]
                        ctr["s"] += 1
                        k.do(k.pe, "matmul", [kT, q_], [p], p[0:wk, 0:N4], kT[:, g, j * 128:j * 128 + wk], q_[:, 4 * g:4 * g + 4, 0:QB], start=True, stop=False, sig=False)
                        k.do(k.pe, "matmul", [mb, I4], [p], p[0:wk, 0:N4], mb[0:QB, j * 128:j * 128 + wk], I4[0:QB, 0:N4], start=False, stop=True)
                        return p, wk
                    pend = qk(0)
                    for j in range(nj):
                        p, wk = pend
                        if j + 1 < nj:
                            pend = qk(j + 1)
                        pt = PTb[ctr["p"] % 3]
                        ctr["p"] += 1
                        k.do(k.act, "activation", [p], [pt], out=pt[0:wk, 0:N4], in_=p[0:wk, 0:N4], func=AF.Exp)
                        k.do(k.pe, "matmul", [Vc, pt], [psO], psO[:, 0:N4], Vc[0:wk, j, g * 128:(g + 1) * 128], pt[0:wk, 0:N4], start=(j == 0), stop=(j == nj - 1), sig=False)
                        k.do(k.pe, "matmul", [onesb, pt], [psN], psN[:, 0:N4], onesb[0:wk, :], pt[0:wk, 0:N4], start=(j == 0), stop=(j == nj - 1), sig=True)
                        yield
                    k.do(k.dve, "reciprocal", [psN], [rden], out=rden[:, 0:N4], in_=psN[:, 0:N4])
                    k.do(k.dve, "tensor_tensor", [psO, rden], [rden], out=rden[:, 0:N4], in0=psO[:, 0:N4], in1=rden[:, 0:N4], op=ALU.mult)
                    k.do(k.pool, "tensor_tensor", [rden, ga_], [og_], out=og_[:, 4 * g:4 * g + 4, 0:QB], in0=rden[:, 0:N4].rearrange("p (a b) -> p a b", b=QB), in1=ga_[:, 4 * g:4 * g + 4, 0:QB], op=ALU.mult)
                    yield
                k.dma(fm_view(S["og"], self.tok0, QB), og_[:, :, 0:QB], [og_], [S["og"]])

        def run(gen):
            for _ in gen:
                pass

        def interleave(gens):
            items = []
            for g in gens:
                steps = []
                items.append((g, steps))
            live = [g for g in gens]
            while live:
                for g in list(live):
                    try:
                        next(g)
                    except StopIteration:
                        live.remove(g)

        def interleave_n(gl):
            st_ = [[g, n, 0, False] for (g, n) in gl]
            while True:
                live = [e for e in st_ if not e[3]]
                if not live:
                    break
                e = min(live, key=lambda e: e[2] / max(e[1], 1))
                try:
                    next(e[0])
                    e[2] += 1
                except StopIteration:
                    e[3] = True

        def interleave_ratio(gA, nA, gT, nT):
            ia = it = 0
            doneA = doneT = False
            while not (doneA and doneT):
                fa = ia / max(nA, 1)
                ft = it / max(nT, 1)
                pick_T = (not doneT) and (doneA or ft <= fa)
                if pick_T:
                    try:
                        next(gT)
                        it += 1
                    except StopIteration:
                        doneT = True
                else:
                    try:
                        next(gA)
                        ia += 1
                    except StopIteration:
                        doneA = True

        jobs = []
        srcs = []
        for t0 in range(0, TP, 256):
            nt = min(2, (TP - t0) // 128)
            srcs.append((k_all.t[t0:t0 + nt * 128, :], v_all.t[t0:t0 + nt * 128, :], ik_all.t[t0:t0 + nt * 128, :], 128, nt, [k_all, v_all, ik_all]))
        jobs.append(("p", srcs))
        for s in range(NS):
            srcs = []
            for t0 in range(0, PAST, 256):
                nt = min(2, (PAST - t0) // 128)
                srcs.append((ck_d[s, t0:t0 + nt * 128, :], cv_d[s, t0:t0 + nt * 128, :], cik_d[s, t0:t0 + nt * 128, :], 128, nt, []))
            a0 = TP + s * TS
            srcs.append((k_all.t[a0:a0 + TS, :], v_all.t[a0:a0 + TS, :], ik_all.t[a0:a0 + TS, :], TS, 1, [k_all, v_all, ik_all]))
            jobs.append(("s%d" % s, srcs))
        gbi = 0
        for ji, (jn, srcs) in enumerate(jobs):
            with ExitStack() as st2:
                psT = ps(st2, "psT_" + jn, [128, 2, 512], BF16)
                psI = ps(st2, "psI_" + jn, [128, 512], BF16)
                W["kst"] = [sb(st2, "kst%d" % i, [128, 2, 256], F32) for i in range(2)]
                W["vst"] = [sb(st2, "vst%d" % i, [128, 2, 256], F32) for i in range(2)]
                W["ist"] = [sb(st2, "ist%d" % i, [128, 2, 64], F32) for i in range(2)]
                W["kbb"] = [sb(st2, "kbb%d" % i, [128, 2, 256], BF16) for i in range(2)]
                W["ibb"] = [sb(st2, "ibb%d" % i, [128, 2, 64], BF16) for i in range(2)]
                build_cache(srcs)
                k.barrier()
            with ExitStack() as st2:
                psD = [ps(st2, "psD%d_%s" % (i, jn), [128, 512], F32) for i in range(3)]
                psSc = ps(st2, "psSc_" + jn, [128, 512], F32)
                psS = [ps(st2, "psS%d_%s" % (i, jn), [128, 512], F32) for i in range(2)]
                psO = ps(st2, "psO_" + jn, [128, 512], F32)
                psN = ps(st2, "psN_" + jn, [128, 512], F32)
                W["qT"] = [sb(st2, "qT%d" % i, [128, 8, 128], BF16) for i in range(2)]
                W["gaT"] = [sb(st2, "gaT%d" % i, [128, 8, 128], BF16) for i in range(2)]
                W["ogT"] = [sb(st2, "ogT%d" % i, [128, 8, 128], BF16) for i in range(1)]
                W["iqT"] = [sb(st2, "iqT%d" % i, [64, 8, 128], BF16) for i in range(2)]
                W["iwt"] = [sb(st2, "iwt%d" % i, [128, 8], F32) for i in range(2)]
                W["Wd"] = [sb(st2, "Wd%d" % i, [128, 8, 128], BF16) for i in range(2)]
                W["Rb"] = [sb(st2, "Rb%d" % i, [128, 512], BF16) for i in range(3)]
                W["PTb"] = [sb(st2, "PTb%d" % i, [128, 512], BF16) for i in range(3)]
                W["rden"] = sb(st2, "rden", [128, 512], F32)
                W["bs"] = [sb(st2, "bs%d" % i, [128, 8], F32) for i in range(2)]
                W["Hh"] = [sb(st2, "Hh%d" % i, [128, NITER + 1], F32) for i in range(2)]
                W["LO"] = [sb(st2, "LO%d" % i, [128, NITER + 1], F32) for i in range(2)]
                W["mid"] = [sb(st2, "mid%d" % i, [128, NITER + 2], F32) for i in range(2)]
                W["TA"] = [sb(st2, "TA%d" % i, [128, NITER + 1], F32) for i in range(2)]
                W["jD"] = [sb(st2, "jD%d" % i, [128, 2], F32) for i in range(2)]
                W["jA"] = [sb(st2, "jA%d" % i, [128, 2], F32) for i in range(2)]
                W["SA"] = [sb(st2, "SA%d" % i, [128, NITER], F32) for i in range(2)]
                if ji == 0:
                    NB = TP // 128
                    blocks = [QB_(b, b * 128, 128, (b + 1) * 128, cfg.topk_p, True, I4p) for b in range(NB)]
                else:
                    blocks = [QB_(0, TP + (ji - 1) * TS, TS, cfg.LS, cfg.topk_s, False, I4s)]
                NB = len(blocks)
                blocks[0].early_loads()
                blocks[0].late_loads()
                run(blocks[0].gen_I())
                for b in range(NB):
                    gl = []
                    if b + 1 < NB:
                        blocks[b + 1].early_loads()
                    nT = (NITER + 2) if blocks[b].LK > blocks[b].topk else 1
                    gl.append((blocks[b].gen_T(), nT))
                    if b >= 1:
                        gl.append((blocks[b - 1].gen_A(), 2 * ((blocks[b - 1].LK + 127) // 128 + 1)))
                    if b + 1 < NB:
                        gl.append((blocks[b + 1].gen_I(), 8 * ((blocks[b + 1].LK + 511) // 512)))
                    interleave_n(gl)
                    if b + 1 < NB:
                        blocks[b + 1].late_loads()
                run(blocks[NB - 1].gen_A())
                k.barrier()
        k.barrier()

    TW3 = 512
    with ExitStack() as st:
        wab = sb(st, "wab", [128, 8, D], BF16)
        wbb = sb(st, "wbb", [128, 8, D], BF16)
        wob = sb(st, "wob", [128, 8, D], BF16)
        wdw = sb(st, "wdw", [128, 8, 31], F32)
        vec = sb(st, "vec", [128, 3, 8], F32)
        gfb = sb(st, "gfb", [128, D], F32)
        onesf = sb(st, "onesf", [128, 128], F32)
        gate_p = sb(st, "gate_p", [128, D], F32)
        gate_s = [sb(st, "gate_s%d" % i, [TS, D], F32) for i in range(NS)]
        k.dma(gate_p[:], G_p.t, [G_p], [gate_p])
        for i in range(NS):
            k.dma(gate_s[i][:], G_s[i].t, [G_s[i]], [gate_s[i]])
        k.do(k.dve, "memset", [], [onesf], onesf[:], 1.0 / D)
        k.dma(wdw[:], wdw_d.rearrange("p (a b) -> p a b", b=31), [], [wdw])
        k.dma(vec[:], vec_d.rearrange("p (a b) -> p a b", b=8), [], [vec])
        k.dma(gfb[:], gf_d, [], [gfb])
        with ExitStack() as st2:
            wl = [sb(st2, "wl%d" % i, [128, 2 * D], F32) for i in range(2)]
            li = 0
            for (src, dst) in [(wa_d, wab), (wb_d, wbb), (wo_d, wob)]:
                for q4 in range(4):
                    w_ = wl[li % 2]
                    li += 1
                    k.dma(w_[:], src[:, q4 * 2 * D:(q4 + 1) * 2 * D], [], [w_])
                    k.do(k.pool if li % 2 else k.dve, "tensor_copy", [w_], [dst], out=dst[:, 2 * q4:2 * q4 + 2, :], in_=w_[:].rearrange("p (a b) -> p a b", b=D))
            k.barrier()
        uext = [sb(st, "uext%d" % i, [128, 8, 30 + TW3], BF16) for i in range(2)]
        gbt = [sb(st, "gbt%d" % i, [128, 8, TW3], BF16) for i in range(1)]
        mat = [sb(st, "mat%d" % i, [128, 8, TW3], BF16) for i in range(1)]
        mbt = [sb(st, "mbt%d" % i, [128, 8, TW3], BF16) for i in range(1)]
        ogt = [sb(st, "ogt%d" % i, [128, 8, TW3], BF16) for i in range(1)]
        sstg = sb(st, "sstg", [128, 8, 30], F32)
        acc = [sb(st, "acc%d" % i, [128, TW3], F32) for i in range(8)]
        sq = [sb(st, "sq%d" % i, [128, TW3], F32) for i in range(2)]
        diag = [sb(st, "diag%d" % i, [128, 128], BF16) for i in range(6)]
        mean_sb = sb(st, "mean_sb", [128, TW3], F32)
        rstd_bc = sb(st, "rstd_bc", [128, TW3], F32)
        t1 = [sb(st, "t1_%d" % i, [128, TW3], F32) for i in range(2)]
        t2 = [sb(st, "t2_%d" % i, [128, TW3], F32) for i in range(2)]
        ybin = sb(st, "ybin", [128, 8, TW3], BF16)
        mixed = sb(st, "mixed", [128, 8, TW3], BF16)
        m1 = [sb(st, "m1_%d" % i, [128, TW3], F32) for i in range(2)]
        m2 = [sb(st, "m2_%d" % i, [128, TW3], F32) for i in range(2)]
        xt3 = [sb(st, "xt3_%d" % i, [128, D], F32) for i in range(2)]
        xn = [sb(st, "xn%d" % i, [128, D], F32) for i in range(2)]
        yt = [sb(st, "yt%d" % i, [128, D], F32) for i in range(2)]
        junk3 = sb(st, "junk3", [128, D], F32)
        st3 = [sb(st, "st3_%d" % i, [128, 4], F32) for i in range(2)]
        psM = ps(st, "psM", [128, 512], F32)
        psQ = ps(st, "psQ", [128, 512], F32)
        psA = [ps(st, "psA%d" % i, [128, 512], F32) for i in range(1)]
        psB = [ps(st, "psB%d" % i, [128, 512], F32) for i in range(1)]
        psC = [ps(st, "psC%d" % i, [128, 512], F32) for i in range(2)]
        psY = [ps(st, "psY%d" % i, [128, 512], F32) for i in range(2)]
        cnt = {"t": 0, "c": 0, "y": 0, "m": 0}

        def tile3(tok0, TW, first, sidx, gate):
            i = cnt["t"]
            cnt["t"] += 1
            ue, gb_, ma_, mb_, og_ = uext[i % 2], gbt[0], mat[0], mbt[0], ogt[0]
            if first:
                if sidx is None:
                    k.do(k.pool, "memset", [], [ue], ue[:, :, 0:30], 0.0)
                else:
                    k.dma(sstg[:], sconv_d[sidx].rearrange("p (a b) -> p a b", b=30), [], [sstg])
                    k.do(k.pool, "tensor_copy", [sstg], [ue], out=ue[:, :, 0:30], in_=sstg[:])
                k.dma(ue[:, :, 30:30 + TW], fm_view(S["ua"], tok0, TW), [S["ua"]], [ue])
            else:
                k.dma(ue[:, :, 0:30 + TW], fm_view(S["ua"], tok0 - 30, TW + 30), [S["ua"]], [ue])
            k.dma(gb_[:, :, 0:TW], fm_view(S["gb"], tok0, TW), [S["gb"]], [gb_])
            k.dma(ma_[:, :, 0:TW], fm_view(S["ma"], tok0, TW), [S["ma"]], [ma_])
            k.dma(mb_[:, :, 0:TW], fm_view(S["mb"], tok0, TW), [S["mb"]], [mb_])
            k.dma(og_[:, :, 0:TW], fm_view(S["og"], tok0, TW), [S["og"]], [og_])
            for c in range(8):
                pcv = psC[c % 2]
                for j in range(31):
                    dg = diag[cnt["c"] % 6]
                    cnt["c"] += 1
                    k.do(k.dve, "tensor_scalar", [identf, wdw], [dg], out=dg[:], in0=identf[:], scalar1=wdw[:, c, j:j + 1], scalar2=None, op0=ALU.mult)
                    k.do(k.pe, "matmul", [dg, ue], [pcv], pcv[:, 0:TW], dg[:], ue[:, c, j:j + TW], start=(j == 0), stop=(j == 30))
                k.do(k.act, "activation", [pcv, vec], [acc[c]], out=acc[c][:, 0:TW], in_=pcv[:, 0:TW], func=AF.Identity, bias=vec[:, 0, c:c + 1])
            for c in range(8):
                s_ = sq[c % 2]
                k.do(k.act, "activation", [acc[c]], [s_], out=s_[:, 0:TW], in_=acc[c][:, 0:TW], func=AF.Square)
                k.do(k.pe, "matmul", [onesf, acc[c]], [psM], psM[:, 0:TW], onesf[:], acc[c][:, 0:TW], start=(c == 0), stop=(c == 7), sig=False)
                k.do(k.pe, "matmul", [onesf, s_], [psQ], psQ[:, 0:TW], onesf[:], s_[:, 0:TW], start=(c == 0), stop=(c == 7), sig=True)
            k.do(k.act, "copy", [psM], [mean_sb], out=mean_sb[:, 0:TW], in_=psM[:, 0:TW])
            k.do(k.dve, "tensor_tensor", [mean_sb], [rstd_bc], out=rstd_bc[:, 0:TW], in0=mean_sb[:, 0:TW], in1=mean_sb[:, 0:TW], op=ALU.mult)
            k.do(k.dve, "tensor_tensor", [psQ, rstd_bc], [rstd_bc], out=rstd_bc[:, 0:TW], in0=psQ[:, 0:TW], in1=rstd_bc[:, 0:TW], op=ALU.subtract)
            k.do(k.act, "activation", [rstd_bc, epsT], [rstd_bc], out=rstd_bc[:, 0:TW], in_=rstd_bc[:, 0:TW], func=AF.Sqrt, bias=epsT[:, 0:1])
            k.do(k.dve, "reciprocal", [rstd_bc], [rstd_bc], out=rstd_bc[:, 0:TW], in_=rstd_bc[:, 0:TW])
            for c in range(8):
                a_, b_ = t1[c % 2], t2[c % 2]
                k.do(k.dve, "tensor_tensor", [acc[c], mean_sb], [a_], out=a_[:, 0:TW], in0=acc[c][:, 0:TW], in1=mean_sb[:, 0:TW], op=ALU.subtract)
                k.do(k.dve, "tensor_tensor", [a_, rstd_bc], [a_], out=a_[:, 0:TW], in0=a_[:, 0:TW], in1=rstd_bc[:, 0:TW], op=ALU.mult)
                k.do(k.act, "activation", [a_, vec], [b_], out=b_[:, 0:TW], in_=a_[:, 0:TW], func=AF.Silu, scale=vec[:, 1, c:c + 1], bias=vec[:, 2, c:c + 1])
                k.do(k.dve, "tensor_tensor", [b_, gb_], [ybin], out=ybin[:, c, 0:TW], in0=b_[:, 0:TW], in1=gb_[:, c, 0:TW], op=ALU.mult)
            for cc in range(8):
                pa, pb = psA[0], psB[0]
                for kc in range(8):
                    k.do(k.pe, "matmul", [wab, og_], [pa], pa[:, 0:TW], wab[:, kc, cc * 128:(cc + 1) * 128], og_[:, kc, 0:TW], start=(kc == 0), stop=(kc == 7), sig=(kc == 7))
                for kc in range(8):
                    k.do(k.pe, "matmul", [wbb, ybin], [pb], pb[:, 0:TW], wbb[:, kc, cc * 128:(cc + 1) * 128], ybin[:, kc, 0:TW], start=(kc == 0), stop=(kc == 7), sig=(kc == 7))
                a_, b_ = m1[cc % 2], m2[cc % 2]
                k.do(k.dve, "tensor_tensor", [pa, ma_], [a_], out=a_[:, 0:TW], in0=pa[:, 0:TW], in1=ma_[:, cc, 0:TW], op=ALU.mult)
                k.do(k.dve, "tensor_tensor", [pb, mb_], [b_], out=b_[:, 0:TW], in0=pb[:, 0:TW], in1=mb_[:, cc, 0:TW], op=ALU.mult)
                k.do(k.pool, "tensor_tensor", [a_, b_], [mixed], out=mixed[:, cc, 0:TW], in0=a_[:, 0:TW], in1=b_[:, 0:TW], op=ALU.add)
            for s0 in range(0, TW, 128):
                rows = min(128, TW - s0)
                yi = cnt["y"]
                cnt["y"] += 1
                x_, xn_, y_, s_ = xt3[yi % 2], xn[yi % 2], yt[yi % 2], st3[yi % 2]
                k.dma(x_[0:rows, :], x_all[tok0 + s0:tok0 + s0 + rows, :], [], [x_])
                for half in range(2):
                    py = psY[half]
                    for kc in range(8):
                        k.do(k.pe, "matmul", [mixed, wob], [py], py[0:rows, :], mixed[:, kc, s0:s0 + rows], wob[:, kc, half * 512:(half + 1) * 512], start=(kc == 0), stop=(kc == 7), sig=(kc == 7))
                    hs = slice(half * 512, half * 512 + 512)
                    k.do(k.dve, "tensor_tensor", [py, gate], [xn_], out=xn_[0:rows, hs], in0=py[0:rows, :], in1=gate[0:rows, hs], op=ALU.mult)
                k.do(k.pool, "tensor_tensor", [xn_, x_], [xn_], out=xn_[0:rows, :], in0=xn_[0:rows, :], in1=x_[0:rows, :], op=ALU.add)
                k.do(k.act, "activation", [xn_], [junk3, s_], out=junk3[0:rows, :], in_=xn_[0:rows, :], func=AF.Square, accum_out=s_[0:rows, 0:1])
                k.do(k.act, "activation", [s_, epsT], [s_], out=s_[0:rows, 1:2], in_=s_[0:rows, 0:1], func=AF.Sqrt, scale=1.0 / D, bias=epsT[0:rows, 0:1])
                k.do(k.dve, "reciprocal", [s_], [s_], out=s_[0:rows, 2:3], in_=s_[0:rows, 1:2])
                k.do(k.dve, "scalar_tensor_tensor", [xn_, s_, gfb], [y_], out=y_[0:rows, :], in0=xn_[0:rows, :], scalar=s_[0:rows, 2:3], in1=gfb[0:rows, :], op0=ALU.mult, op1=ALU.mult)
                k.dma(y_all.t[tok0 + s0:tok0 + s0 + rows, :], y_[0:rows, :], [y_], [y_all])

        for t0 in range(0, TP, TW3):
            tile3(t0, min(TW3, TP - t0), t0 == 0, None, gate_p)
        for s in range(NS):
            tile3(TP + s * TS, TS, True, s, gate_s[s])
        k.barrier()
    k.final_wait()
    P0.close()
    k.stack0.close()
    return nc, k


def _consts():
    c = np.zeros((128, 128 + 192 + 512 + 128), np.float32)
    c[:, 0:128] = np.eye(128, dtype=np.float32)
    c[0, 128:256] = 1.0
    c[1, 256:288] = 1.0
    c[2, 288:320] = 1.0
    for r in range(4):
        c[:, 320 + r * 128:320 + (r + 1) * 128] = np.eye(128, dtype=np.float32)
        c[0:32, 832 + r * 32:832 + (r + 1) * 32] = np.eye(32, dtype=np.float32)
    return c


def _kc_layout(w):
    n = w.shape[1]
    return np.ascontiguousarray(w.reshape(8, 128, n).transpose(1, 0, 2)).reshape(128, 8 * n)


def _fmT(v, n):
    return np.ascontiguousarray(v.reshape(n, 128).T)


def make_in_maps(cfg, x_prompt, x_sample, cache_k, cache_v, cache_idx_k, state_conv, c_prompt, c_sample,
                 w_ada, b_ada, g_norm, w_in, w_a, w_dw, b_dw, g_ln, b_ln, w_b, w_out, g_final, n_cores):
    f = lambda a: np.ascontiguousarray(np.asarray(a, dtype=np.float32))
    w_in0 = f(w_in[0])
    wada = _kc_layout(f(w_ada[0])).reshape(128, 8, 3072)
    bada = np.ascontiguousarray(np.broadcast_to(f(b_ada[0])[None, :], (3, 3072)))
    gn_bc = np.ascontiguousarray(np.broadcast_to(f(g_norm[0])[None, :], (128, D)))
    gf_bc = np.ascontiguousarray(np.broadcast_to(f(g_final)[None, :], (128, D)))
    wfm = np.stack([_kc_layout(w_in0[:, FM_OFF[kd] + c * 128: FM_OFF[kd] + (c + 1) * 128]) for kd in FM_KINDS for c in range(8)])
    wiq = np.stack([_kc_layout(w_in0[:, O_IQ + h * 64: O_IQ + (h + 1) * 64]) for h in range(8)])
    wkv = _kc_layout(np.concatenate([w_in0[:, O_K:O_K + 512], w_in0[:, O_IK:O_IK + 64], w_in0[:, O_IW:O_IW + 8]], axis=1))
    wa, wb, wo = _kc_layout(f(w_a[0])), _kc_layout(f(w_b[0])), _kc_layout(f(w_out[0]))
    wdwT = np.ascontiguousarray(f(w_dw[0]).T.reshape(8, 128, 31).transpose(1, 0, 2)).reshape(128, 8 * 31)
    vecT = np.concatenate([_fmT(f(b_dw[0]), 8), _fmT(f(g_ln[0]), 8), _fmT(f(b_ln[0]), 8)], axis=1)
    consts = _consts()
    NS, TS = cfg.NS, cfg.TS
    maps = []
    for i in range(n_cores):
        ss = slice(i * NS, (i + 1) * NS)
        xs = f(x_sample[ss]).reshape(NS * TS, D)
        x_all = np.concatenate([f(x_prompt[i]), xs], axis=0)
        cc = np.concatenate([f(c_prompt[i])[None], f(c_sample[ss])], axis=0)
        cT = np.ascontiguousarray(cc.reshape(3, 8, 128).transpose(2, 1, 0))
        sconv = f(state_conv[0][ss])
        sconvT = np.ascontiguousarray(sconv.reshape(NS, 30, 8, 128).transpose(0, 3, 2, 1)).reshape(NS, 128, 240)
        maps.append({
            "x_all": x_all, "cT": cT, "wada": wada, "bada": bada, "gn_bc": gn_bc, "wfm": wfm, "wiq": wiq, "wkv": wkv,
            "wa": wa, "wb": wb, "wo": wo, "wdwT": wdwT, "vecT": vecT, "gf_bc": gf_bc,
            "ck": f(cache_k[0][ss]).reshape(NS, cfg.PAST, 256), "cv": f(cache_v[0][ss]).reshape(NS, cfg.PAST, 256),
            "cik": f(cache_idx_k[0][ss]), "sconvT": sconvT, "consts": consts,
        })
    return maps


def assemble(cfg, results, n_cores):
    TP, NS, TS = cfg.TP, cfg.NS, cfg.TS
    g = lambda name: [np.asarray(r[name]) for r in results]
    y, kk, vv, ik, cv = g("y_all"), g("k_all"), g("v_all"), g("ik_all"), g("conv_o")
    y_p = np.stack([a[:TP] for a in y])
    y_s = np.concatenate([a[TP:].reshape(NS, TS, D) for a in y])
    k_p = np.stack([a[:TP].reshape(TP, 2, 128) for a in kk])[None]
    v_p = np.stack([a[:TP].reshape(TP, 2, 128) for a in vv])[None]
    ik_p = np.stack([a[:TP] for a in ik])[None]
    c_p = np.stack([a[0] for a in cv])[None]
    k_s = np.concatenate([a[TP:].reshape(NS, TS, 2, 128) for a in kk])[None]
    v_s = np.concatenate([a[TP:].reshape(NS, TS, 2, 128) for a in vv])[None]
    ik_s = np.concatenate([a[TP:].reshape(NS, TS, 64) for a in ik])[None]
    c_s = np.concatenate([a[1:] for a in cv])[None]
    return tuple(np.ascontiguousarray(a, dtype=np.float32) for a in (y_p, y_s, k_p, v_p, ik_p, c_p, k_s, v_s, ik_s, c_s))


def kernel(**inputs):
    cfg = Cfg()
    n = 8
    nc, _ = build(cfg)
    maps = make_in_maps(cfg, n_cores=n, **inputs)
    res = run_bass_kernel_spmd(nc, maps, core_ids=list(range(n)))
    return assemble(cfg, res.results, n)
```

```python
import numpy as np
from contextlib import ExitStack
import concourse.bass as bass
import concourse.mybir as mybir
from concourse.bass_utils import run_bass_kernel_spmd

F32 = mybir.dt.float32
BF16 = mybir.dt.bfloat16
AF = mybir.ActivationFunctionType
ALU = mybir.AluOpType
AX = mybir.AxisListType

D = 1024
NIN = 8264
CHUNK = 64
NITER = 22
import os
ACT_FRAC = float(os.environ.get("ACT_FRAC", "0.0"))
NEG = -1.0e30
MASKV = -30000.0

O_Q, O_K, O_V, O_GA, O_IQ, O_IW, O_IK, O_GLU, O_GB, O_M = 0, 1024, 1280, 1536, 2560, 3072, 3080, 3144, 5192, 6216
FM_KINDS = ["q", "ga", "ua", "ub", "gb", "ma", "mb"]
FM_OFF = {"q": O_Q, "ga": O_GA, "ua": O_GLU, "ub": O_GLU + 1024, "gb": O_GB, "ma": O_M, "mb": O_M + 1024}


class Sem:
    def __init__(self, k, name):
        self.h = k.stack0.enter_context(k.nc.semaphore(name))
        self.v = 0


class Buf:
    def __init__(self, t, disjoint=False):
        self.t = t
        self.w = {}
        self.r = {}
        self.disjoint = disjoint

    def __getitem__(self, idx):
        return self.t[idx]


def _merge(d, tok):
    if tok is None:
        return
    s, v = tok
    if id(s) not in d or d[id(s)][1] < v:
        d[id(s)] = (s, v)


class Eng:
    def __init__(self, k, e, name, same=True):
        self.k = k
        self.e = e
        self.name = name
        self.sem = Sem(k, "e_" + name)
        self.seen = {}
        self.same = same
        self.pending = []

    def wait_tok(self, s, v):
        if s is self.sem and not self.same:
            return
        if self.seen.get(id(s), 0) < v:
            self.e.wait_ge(s.h, v)
            self.seen[id(s)] = v

    def wait_bufs(self, reads, writes):
        for b in reads:
            for s, v in b.w.values():
                self.wait_tok(s, v)
        for b in writes:
            if b.disjoint:
                continue
            for s, v in b.w.values():
                self.wait_tok(s, v)
            for s, v in b.r.values():
                self.wait_tok(s, v)


class K:
    def __init__(self, nc):
        self.nc = nc
        self.stack0 = ExitStack()
        self.pe = Eng(self, nc.tensor, "pe", same=False)
        self.act = Eng(self, nc.scalar, "act")
        self.dve = Eng(self, nc.vector, "dve")
        self.pool = Eng(self, nc.gpsimd, "pool")
        self.sp = Eng(self, nc.sync, "sp")
        self.engs = [self.pe, self.act, self.dve, self.pool, self.sp]
        self.dsems = [Sem(self, "d%d" % i) for i in range(24)]
        self.di = 0
        self.nins = 0

    def _post(self, E, tok, reads, writes):
        for b in writes:
            if not b.disjoint:
                b.w = {}
                b.r = {}
            _merge(b.w, tok)
        for b in reads:
            _merge(b.r, tok)

    def do(self, E, fn, reads, writes, *a, sig=True, **kw):
        E.wait_bufs(reads, writes)
        ins = getattr(E.e, fn)(*a, **kw)
        self.nins += 1
        if sig:
            E.sem.v += 1
            ins.then_inc(E.sem.h, 1)
            tok = (E.sem, E.sem.v)
            for (rr, ww) in E.pending:
                self._post(E, tok, rr, ww)
            E.pending = []
            self._post(E, tok, reads, writes)
            return tok
        E.pending.append((reads, writes))
        return None

    def dma(self, out, in_, reads, writes, E=None, **kw):
        E = E or self.sp
        E.wait_bufs(reads, writes)
        s = self.dsems[self.di % len(self.dsems)]
        self.di += 1
        if s.v > 0:
            E.wait_tok(s, s.v)
        s.v += 16
        E.e.dma_start(out=out, in_=in_, **kw).then_inc(s.h, 16)
        self.nins += 1
        tok = (s, s.v)
        for b in writes:
            _merge(b.w, tok)
        for b in reads:
            _merge(b.r, tok)
        return tok

    def barrier(self):
        sems = [e.sem for e in self.engs] + self.dsems
        for E in [self.pe, self.act, self.dve, self.pool, self.sp]:
            for s in sems:
                if s.v > 0 and s is not E.sem:
                    E.wait_tok(s, s.v)

    def final_wait(self):
        for s in self.dsems:
            if s.v > 0:
                self.sp.wait_tok(s, s.v)
        for e in self.engs:
            if e is not self.sp and e.sem.v > 0:
                self.sp.wait_tok(e.sem, e.sem.v)


class Cfg:
    def __init__(self, TP=8192, PAST=2048, TS=32, NS=2, topk_max=256, debug=False):
        self.TP, self.PAST, self.TS, self.NS = TP, PAST, TS, NS
        self.debug = debug
        self.NTOK = TP + NS * TS
        self.topk_p = min(topk_max, TP // 4)
        self.topk_s = min(topk_max, (PAST + TS) // 4)
        self.LS = PAST + TS
        self.LMAX = max(TP, ((self.LS + 127) // 128) * 128)
        self.NKT = self.LMAX // 128


def build(cfg):
    TP, NTOK, TS, NS, PAST = cfg.TP, cfg.NTOK, cfg.TS, cfg.NS, cfg.PAST
    SROWS = NS * TS
    nc = bass.Bass("TRN2", target_bir_lowering=False)
    k = K(nc)

    def din(name, shape, dt=F32):
        return nc.dram_tensor(name, list(shape), dt, kind="ExternalInput").ap()

    def dout(name, shape, dt=F32):
        return nc.dram_tensor(name, list(shape), dt, kind="ExternalOutput").ap()

    def dscr(name, shape, dt=BF16):
        return Buf(nc.dram_tensor(name, list(shape), dt, kind="Internal").ap(), disjoint=True)

    x_all = din("x_all", [NTOK, D])
    cT_d = din("cT", [128, 8, 3])
    wada_d = din("wada", [128, 8, 3072])
    bada_d = din("bada", [3, 3072])
    gn_d = din("gn_bc", [128, D])
    wfm_d = din("wfm", [56, 128, 8 * 128])
    wiq_d = din("wiq", [8, 128, 8 * 64])
    wkv_d = din("wkv", [128, 8 * 584])
    wa_d = din("wa", [128, 8 * D])
    wb_d = din("wb", [128, 8 * D])
    wo_d = din("wo", [128, 8 * D])
    wdw_d = din("wdwT", [128, 8 * 31])
    vec_d = din("vecT", [128, 3 * 8])
    gf_d = din("gf_bc", [128, D])
    ck_d = din("ck", [NS, PAST, 256])
    cv_d = din("cv", [NS, PAST, 256])
    cik_d = din("cik", [NS, PAST, 64])
    sconv_d = din("sconvT", [NS, 128, 8 * 30])
    const_d = din("consts", [128, 128 + 192 + 512 + 128])

    y_all = Buf(disjoint=True, t=dout("y_all", [NTOK, D]))
    k_all = Buf(disjoint=True, t=dout("k_all", [NTOK, 256]))
    v_all = Buf(disjoint=True, t=dout("v_all", [NTOK, 256]))
    ik_all = Buf(disjoint=True, t=dout("ik_all", [NTOK, 64]))
    conv_o = Buf(disjoint=True, t=dout("conv_o", [1 + NS, 30, D]))

    S = {kind: dscr("s_" + kind, [8, 128, NTOK]) for kind in FM_KINDS}
    if cfg.debug:
        S["og"] = Buf(nc.dram_tensor("s_og", [8, 128, NTOK], BF16, kind="ExternalOutput").ap(), disjoint=True)
    else:
        S["og"] = dscr("s_og", [8, 128, NTOK])
    S_iq = dscr("s_iq", [8, 64, NTOK])
    S_iw = dscr("s_iw", [NTOK, 8], F32)

    def fm_view(b, t0, n):
        return b.t.rearrange("h d t -> d h t")[:, :, t0:t0 + n]

    P0 = ExitStack()

    uniq = [0]

    def sb(st, name, shape, dt):
        uniq[0] += 1
        return Buf(st.enter_context(nc.sbuf_tensor("%s_u%d" % (name, uniq[0]), list(shape), dt)))

    def ps(st, name, shape, dt):
        uniq[0] += 1
        return Buf(st.enter_context(nc.psum_tensor("%s_u%d" % (name, uniq[0]), list(shape), dt)))

    identf = sb(P0, "identf", [128, 128], F32)
    identb = sb(P0, "identb", [128, 128], BF16)
    G_p = Buf(nc.dram_tensor("g_p", [128, D], F32, kind="Internal").ap(), disjoint=True)
    G_s = [Buf(nc.dram_tensor("g_s%d" % i, [TS, D], F32, kind="Internal").ap(), disjoint=True) for i in range(NS)]
    epsT = sb(P0, "epsT", [128, 1], F32)
    onesb = sb(P0, "onesb", [128, 128], BF16)
    I4p = sb(P0, "I4p", [128, 512], BF16)
    I4s = sb(P0, "I4s", [32, 128], BF16)
    PC = ExitStack()
    cst = sb(PC, "cst", [128, 128 + 192 + 512 + 128], F32)

    k.dma(cst[:], const_d, [], [cst])
    k.do(k.dve, "tensor_copy", [cst], [identf], out=identf[:], in_=cst[:, 0:128])
    k.do(k.dve, "tensor_copy", [cst], [identb], out=identb[:], in_=cst[:, 0:128])
    k.do(k.dve, "memset", [], [epsT], epsT[:], 1e-6)
    k.do(k.dve, "memset", [], [onesb], onesb[:], 1.0)
    k.do(k.dve, "tensor_copy", [cst], [I4p], out=I4p[:], in_=cst[:, 320:832])
    k.do(k.dve, "tensor_copy", [cst], [I4s], out=I4s[:], in_=cst[0:32, 832:960])
    sel = lambda a, b_: cst[0:3, 128 + a:128 + b_]

    P1 = ExitStack()
    hT = sb(P1, "hT", [128, 8, NTOK], BF16)
    P1a = ExitStack()
    AB_p = sb(P1a, "AB_p", [128, 2 * D], F32)
    AB_s = sb(P1a, "AB_s", [SROWS, 2 * D], F32)
    with ExitStack() as st:
        cT = sb(st, "cT_t", [128, 8, 3], F32)
        sc = sb(st, "sc_t", [128, 8, 3], F32)
        modrows = sb(st, "modrows", [3, 3072], F32)
        gn = sb(st, "gn", [128, D], F32)
        wat = [sb(st, "wat%d" % i, [128, 8, 256], F32) for i in range(2)]
        pm = ps(st, "pm", [128, 512], F32)
        pbp = [ps(st, "pbp%d" % i, [128, 512], F32) for i in range(2)]
        k.dma(cT[:], cT_d, [], [cT])
        k.dma(modrows[:], bada_d, [], [modrows])
        k.dma(gn[:], gn_d, [], [gn])
        k.do(k.act, "activation", [cT], [sc], out=sc[:], in_=cT[:], func=AF.Silu)
        wview = wada_d
        for i in range(12):
            w = wat[i % 2]
            k.dma(w[:], wview[:, :, i * 256:(i + 1) * 256], [], [w])
            for kc in range(8):
                k.do(k.pe, "matmul", [sc, w], [pm], pm[0:3, 0:256], sc[:, kc, :], w[:, kc, :], start=(kc == 0), stop=(kc == 7), sig=(kc == 7))
            k.do(k.dve, "tensor_tensor", [pm, modrows], [modrows], out=modrows[:, i * 256:(i + 1) * 256], in0=pm[0:3, 0:256], in1=modrows[:, i * 256:(i + 1) * 256], op=ALU.add)
        gstg = [sb(st, "gstg%d" % i, [128, 512], F32) for i in range(2)]
        targets = [((0, 128), 128, AB_p, G_p), ((128, 128 + SROWS), SROWS, AB_s, None)]
        for si in range(NS):
            targets.append(((128 + si * 32, 128 + si * 32 + TS), TS, None, G_s[si]))
        bi = 0
        for (sc0, sc1), M, AB, gt in targets:
            for part in range(3):
                if part < 2 and AB is None:
                    continue
                if part == 2 and gt is None:
                    continue
                for half in range(2):
                    pb = pbp[bi % 2]
                    bi += 1
                    k.do(k.pe, "matmul", [cst, modrows], [pb], pb[0:M, :], sel(sc0, sc1), modrows[:, part * 1024 + half * 512: part * 1024 + half * 512 + 512], start=True, stop=True)
                    cs = slice(half * 512, half * 512 + 512)
                    if part == 1:
                        k.do(k.dve, "scalar_tensor_tensor", [pb, gn], [AB], out=AB[0:M, cs], in0=pb[0:M, :], scalar=1.0, in1=gn[0:M, cs], op0=ALU.add, op1=ALU.mult)
                    elif part == 0:
                        k.do(k.act, "copy", [pb], [AB], out=AB[0:M, D + half * 512: D + half * 512 + 512], in_=pb[0:M, :])
                    else:
                        g_ = gstg[bi % 2]
                        k.do(k.act, "copy", [pb], [g_], out=g_[0:M, :], in_=pb[0:M, :])
                        k.dma(gt.t[0:M, cs], g_[0:M, :], [g_], [gt])
        k.barrier()

    NT = (NTOK + 127) // 128
    with ExitStack() as st:
        xb = [sb(st, "xb%d" % i, [128, D], F32) for i in range(3)]
        h1 = [sb(st, "h1_%d" % i, [128, D], F32) for i in range(2)]
        hb = [sb(st, "hb%d" % i, [128, D], BF16) for i in range(2)]
        junk = sb(st, "junk1", [128, D], F32)
        stats = [sb(st, "stats%d" % i, [128, 4], F32) for i in range(3)]
        pT = [ps(st, "pT%d" % i, [128, 8, 128], BF16) for i in range(2)]
        def stage1(tt):
            t0 = tt * 128
            rows = min(128, NTOK - t0)
            x = xb[tt % 3]
            s_ = stats[tt % 3]
            k.dma(x[0:rows, :], x_all[t0:t0 + rows, :], [], [x])
            k.do(k.act, "activation", [x], [junk, s_], out=junk[0:rows, :], in_=x[0:rows, :], func=AF.Square, accum_out=s_[0:rows, 0:1])
            k.do(k.act, "activation", [s_, epsT], [s_], out=s_[0:rows, 1:2], in_=s_[0:rows, 0:1], func=AF.Sqrt, scale=1.0 / D, bias=epsT[0:rows, 0:1])
        stage1(0)
        for tt in range(NT):
            t0 = tt * 128
            rows = min(128, NTOK - t0)
            AB = AB_p if t0 < TP else AB_s
            x = xb[tt % 3]
            s_ = stats[tt % 3]
            if tt + 1 < NT:
                stage1(tt + 1)
            k.do(k.dve, "reciprocal", [s_], [s_], out=s_[0:rows, 2:3], in_=s_[0:rows, 1:2])
            h_ = h1[tt % 2]
            k.do(k.dve, "scalar_tensor_tensor", [x, s_, AB], [h_], out=h_[0:rows, :], in0=x[0:rows, :], scalar=s_[0:rows, 2:3], in1=AB[0:rows, 0:D], op0=ALU.mult, op1=ALU.mult)
            hb_ = hb[tt % 2]
            k.do(k.dve, "tensor_tensor", [h_, AB], [hb_], out=hb_[0:rows, :], in0=h_[0:rows, :], in1=AB[0:rows, D:2 * D], op=ALU.add)
            p = pT[tt % 2]
            for c in range(8):
                k.do(k.pe, "transpose", [hb_, identb], [p], p[:, c, 0:rows], hb_[0:rows, c * 128:(c + 1) * 128], identb[0:rows, 0:rows], sig=(c == 7))
            k.do(k.act, "copy", [p], [hT], out=hT[:, :, t0:t0 + rows], in_=p[:, :, 0:rows])
        k.barrier()
    P1a.close()

    groups = [(g0, min(512, NTOK - g0)) for g0 in range(0, NTOK, 512)]
    with ExitStack() as st:
        wkvb = sb(st, "wkvb", [128, 8, 584], BF16)
        kvst = [sb(st, "kvst%d" % i, [128, 584], F32) for i in range(2)]
        wst = [sb(st, "wst%d" % i, [128, 8 * 128], F32) for i in range(2)]
        wbf = [sb(st, "wbf%d" % i, [128, 8, 128], BF16) for i in range(3)]
        stg = [sb(st, "stg%d" % i, [128, 512], BF16) for i in range(3)]
        sig_ = [sb(st, "sig%d" % i, [128, 512], F32) for i in range(2)]
        cstg = [sb(st, "cstg%d" % i, [64, 256], F32) for i in range(2)]
        pz = [ps(st, "pz%d" % i, [128, 512], F32) for i in range(4)]
        pk0 = ps(st, "pk0", [128, 512], F32)
        pk1 = ps(st, "pk1", [128, 512], F32)
        pc = ps(st, "pcv", [128, 512], F32)
        for kc in range(8):
            b_ = kvst[kc % 2]
            k.dma(b_[:], wkv_d[:, kc * 584:(kc + 1) * 584], [], [b_])
            k.do(k.pool, "tensor_copy", [b_], [wkvb], out=wkvb[:, kc, :], in_=b_[:])
        for tt in range(NT):
            t0 = tt * 128
            rows = min(128, NTOK - t0)
            for kc in range(8):
                k.do(k.pe, "matmul", [hT, wkvb], [pk0], pk0[0:rows, :], hT[:, kc, t0:t0 + rows], wkvb[:, kc, 0:512], start=(kc == 0), stop=(kc == 7), sig=False)
            for kc in range(8):
                k.do(k.pe, "matmul", [hT, wkvb], [pk1], pk1[0:rows, 0:72], hT[:, kc, t0:t0 + rows], wkvb[:, kc, 512:584], start=(kc == 0), stop=(kc == 7), sig=(kc == 7))
            o_ = kvst[tt % 2]
            k.do(k.act, "copy", [pk0], [o_], out=o_[0:rows, 0:512], in_=pk0[0:rows, :])
            k.do(k.dve, "tensor_copy", [pk1], [o_], out=o_[0:rows, 512:584], in_=pk1[0:rows, 0:72])
            k.dma(k_all.t[t0:t0 + rows, :], o_[0:rows, 0:256], [o_], [k_all])
            k.dma(v_all.t[t0:t0 + rows, :], o_[0:rows, 256:512], [o_], [v_all])
            k.dma(ik_all.t[t0:t0 + rows, :], o_[0:rows, 512:576], [o_], [ik_all])
            k.dma(S_iw.t[t0:t0 + rows, :], o_[0:rows, 576:584], [o_], [S_iw])

        wi = [0]
        zi = [0]
        si_ = [0]

        def load_w(src_ap, width):
            i = wi[0]
            wi[0] += 1
            ws = wst[i % 2]
            wb_ = wbf[i % 3]
            k.dma(ws[:, 0:8 * width], src_ap, [], [ws])
            k.do(k.pool, "tensor_copy", [ws], [wb_], out=wb_[:, :, 0:width], in_=ws[:, 0:8 * width].rearrange("p (a b) -> p a b", b=width))
            return wb_

        def proj(wb_, width, g0, n):
            p = pz[zi[0] % 4]
            zi[0] += 1
            for kc in range(8):
                k.do(k.pe, "matmul", [hT, wb_], [p], p[0:width, 0:n], wb_[:, kc, 0:width], hT[:, kc, g0:g0 + n], start=(kc == 0), stop=(kc == 7), sig=(kc == 7))
            return p

        def single(kind, c, func, scale=1.0):
            wb_ = load_w(wfm_d[FM_KINDS.index(kind) * 8 + c], 128)
            for (g0, n) in groups:
                p = proj(wb_, 128, g0, n)
                o_ = stg[si_[0] % 3]
                si_[0] += 1
                k.do(k.act, "activation", [p], [o_], out=o_[:, 0:n], in_=p[:, 0:n], func=func, scale=scale)
                k.dma(S[kind].t[c, :, g0:g0 + n], o_[:, 0:n], [o_], [S[kind]])

        for c in range(8):
            single("q", c, AF.Copy, scale=128.0 ** -0.5)
        for h in range(8):
            wb_ = load_w(wiq_d[h], 64)
            for (g0, n) in groups:
                p = proj(wb_, 64, g0, n)
                o_ = stg[si_[0] % 3]
                si_[0] += 1
                k.do(k.dve, "tensor_copy", [p], [o_], out=o_[0:64, 0:n], in_=p[0:64, 0:n])
                k.dma(S_iq.t[h, :, g0:g0 + n], o_[0:64, 0:n], [o_], [S_iq])
        crow_sets = [(TP - 32, 32, [(0, 2, 32)]), (TP, SROWS, [(1 + s, TS * s + 2, TS * s + TS) for s in range(NS)])]
        ci = 0
        for c in range(8):
            wA = load_w(wfm_d[FM_KINDS.index("ua") * 8 + c], 128)
            wB = load_w(wfm_d[FM_KINDS.index("ub") * 8 + c], 128)
            for (g0, n) in groups:
                pA = proj(wA, 128, g0, n)
                pB = proj(wB, 128, g0, n)
                sg = sig_[si_[0] % 2]
                o_ = stg[si_[0] % 3]
                si_[0] += 1
                k.do(k.act, "activation", [pB], [sg], out=sg[:, 0:n], in_=pB[:, 0:n], func=AF.Sigmoid)
                k.do(k.dve, "tensor_tensor", [pA, sg], [o_], out=o_[:, 0:n], in0=pA[:, 0:n], in1=sg[:, 0:n], op=ALU.mult)
                k.dma(S["ua"].t[c, :, g0:g0 + n], o_[:, 0:n], [o_], [S["ua"]])
            for (r0, M, outs) in crow_sets:
                for kc in range(8):
                    k.do(k.pe, "matmul", [hT, wA], [pc], pc[0:M, 0:128], hT[:, kc, r0:r0 + M], wA[:, kc, :], start=(kc == 0), stop=(kc == 7), sig=False)
                for kc in range(8):
                    k.do(k.pe, "matmul", [hT, wB], [pc], pc[0:M, 128:256], hT[:, kc, r0:r0 + M], wB[:, kc, :], start=(kc == 0), stop=(kc == 7), sig=(kc == 7))
                cs_ = cstg[ci % 2]
                ci += 1
                k.do(k.act, "activation", [pc], [cs_], out=cs_[0:M, 128:256], in_=pc[0:M, 128:256], func=AF.Sigmoid)
                k.do(k.dve, "tensor_tensor", [pc, cs_], [cs_], out=cs_[0:M, 0:128], in0=pc[0:M, 0:128], in1=cs_[0:M, 128:256], op=ALU.mult)
                for (job, ra, rb) in outs:
                    k.dma(conv_o.t[job, :, c * 128:(c + 1) * 128], cs_[ra:rb, 0:128], [cs_], [conv_o])
        for c in range(8):
            single("ga", c, AF.Silu)
        for c in range(8):
            single("gb", c, AF.Silu)
        for c in range(8):
            single("ma", c, AF.Sigmoid)
        for c in range(8):
            single("mb", c, AF.Sigmoid)
        k.barrier()
    P1.close()
    PC.close()

    LMAX, NKT = cfg.LMAX, cfg.NKT
    with ExitStack() as st:
        Vc = sb(st, "Vc", [128, NKT, 256], BF16)
        kT = sb(st, "kT", [128, 2, LMAX], BF16)
        ikT = sb(st, "ikT", [64, LMAX], BF16)
        score = [sb(st, "score%d" % i, [128, LMAX], F32) for i in range(2)]
        mbias = [sb(st, "mbias%d" % i, [128, LMAX], BF16) for i in range(2)]
        pw = sb(st, "pw", [128, NITER + 1], F32)
        for i in range(NITER + 1):
            k.do(k.dve, "memset", [], [pw], pw[:, i:i + 1], 2.0 ** -(i + 1))
        psD = psSc = psS = psO = psN = psT = psI = None
        W = {}

        def build_cache(srcs):
            kst, vst, ist, kbb, ibb = W["kst"], W["vst"], W["ist"], W["kbb"], W["ibb"]
            kpos = 0
            bi = 0
            for (kap, vap, iap, rows, nt, deps) in srcs:
                a = bi % 2
                bi += 1
                ks_, vs_, is_, kb_, ib_ = kst[a], vst[a], ist[a], kbb[a], ibb[a]
                if nt > 1 or rows == 128:
                    k.dma(ks_[:, 0:nt, :], kap.rearrange("(a p) c -> p a c", p=128), deps, [ks_])
                    k.dma(vs_[:, 0:nt, :], vap.rearrange("(a p) c -> p a c", p=128), deps, [vs_])
                    k.dma(is_[:, 0:nt, :], iap.rearrange("(a p) c -> p a c", p=128), deps, [is_])
                else:
                    k.dma(ks_[0:rows, 0, :], kap, deps, [ks_])
                    k.dma(vs_[0:rows, 0, :], vap, deps, [vs_])
                    k.dma(is_[0:rows, 0, :], iap, deps, [is_])
                kt0 = kpos // 128
                k.do(k.pool, "tensor_copy", [ks_], [kb_], out=kb_[0:rows, 0:nt, :], in_=ks_[0:rows, 0:nt, :])
                k.do(k.dve, "tensor_copy", [vs_], [Vc], out=Vc[0:rows, kt0:kt0 + nt, :], in_=vs_[0:rows, 0:nt, :])
                k.do(k.pool, "tensor_copy", [is_], [ib_], out=ib_[0:rows, 0:nt, :], in_=is_[0:rows, 0:nt, :])
                for a_ in range(nt):
                    for g in range(2):
                        k.do(k.pe, "transpose", [kb_, identb], [psT], psT[:, g, a_ * 128:a_ * 128 + rows], kb_[0:rows, a_, g * 128:(g + 1) * 128], identb[0:rows, 0:rows], sig=False)
                    k.do(k.pe, "transpose", [ib_, identb], [psI], psI[0:64, a_ * 128:a_ * 128 + rows], ib_[0:rows, a_, :], identb[0:rows, 0:rows], sig=(a_ == nt - 1))
                n = (nt - 1) * 128 + rows
                k.do(k.act, "copy", [psT], [kT], out=kT[:, :, kpos:kpos + n], in_=psT[:, :, 0:n])
                k.do(k.dve, "tensor_copy", [psI], [ikT], out=ikT[:, kpos:kpos + n], in_=psI[0:64, 0:n])
                kpos += n
            return kpos

        ctr = {"d": 0, "r": 0, "s": 0, "p": 0}

        class QB_:
            def __init__(self, bi, tok0, QB, LK, topk, diag, I4):
                self.bi, self.tok0, self.QB, self.LK, self.topk, self.diag, self.I4 = bi, tok0, QB, LK, topk, diag, I4
                self.q_, self.ga_, self.og_ = W["qT"][bi % 2], W["gaT"][bi % 2], W["ogT"][0]
                self.iq_, self.iw_, self.wd_ = W["iqT"][bi % 2], W["iwt"][bi % 2], W["Wd"][bi % 2]
                self.sc, self.mb = score[bi % 2], mbias[bi % 2]
                self.bs, self.Hh, self.LO, self.mid = W["bs"][bi % 2], W["Hh"][bi % 2], W["LO"][bi % 2], W["mid"][bi % 2]
                self.jD, self.jA, self.SA, self.TA = W["jD"][bi % 2], W["jA"][bi % 2], W["SA"][bi % 2], W["TA"][bi % 2]

            def early_loads(self):
                QB, tok0 = self.QB, self.tok0
                k.dma(self.iq_[:, :, 0:QB], S_iq.t.rearrange("h d t -> d h t")[:, :, tok0:tok0 + QB], [S_iq], [self.iq_])
                k.dma(self.iw_[0:QB, :], S_iw.t[tok0:tok0 + QB, :], [S_iw], [self.iw_])
                for h in range(8):
                    k.do(k.pool, "tensor_scalar", [identf, self.iw_], [self.wd_], out=self.wd_[0:QB, h, 0:QB], in0=identf[0:QB, 0:QB], scalar1=self.iw_[0:QB, h:h + 1], scalar2=None, op0=ALU.mult, sig=(h == 7))

            def late_loads(self):
                QB, tok0 = self.QB, self.tok0
                k.dma(self.q_[:, :, 0:QB], fm_view(S["q"], tok0, QB), [S["q"]], [self.q_])
                k.dma(self.ga_[:, :, 0:QB], fm_view(S["ga"], tok0, QB), [S["ga"]], [self.ga_])

            def gen_I(self):
                QB, LK, iq_, wd_, sc = self.QB, self.LK, self.iq_, self.wd_, self.sc
                Rb = W["Rb"]
                nkt = (LK + 511) // 512
                for kt in range(nkt):
                    k0 = kt * 512
                    wk = min(512, LK - k0)

                    def dmm(h):
                        p = psD[ctr["d"] % 3]
                        ctr["d"] += 1
                        k.do(k.pe, "matmul", [iq_, ikT], [p], p[0:QB, 0:wk], iq_[:, h, 0:QB], ikT[:, k0:k0 + wk], start=True, stop=True)
                        return p
                    pq = [dmm(0), dmm(1)]
                    rq = []
                    for h in range(8):
                        p = pq.pop(0)
                        r_ = Rb[ctr["r"] % 3]
                        ctr["r"] += 1
                        k.do(k.act, "activation", [p], [r_], out=r_[0:QB, 0:wk], in_=p[0:QB, 0:wk], func=AF.Relu)
                        if h + 2 < 8:
                            pq.append(dmm(h + 2))
                        k.do(k.pe, "matmul", [wd_, r_], [psSc], psSc[0:QB, 0:wk], wd_[0:QB, h, 0:QB], r_[0:QB, 0:wk], start=(h == 0), stop=(h == 7), sig=(h == 7))
                        if h < 7:
                            yield
                    k.do(k.act, "copy", [psSc], [sc], out=sc[0:QB, k0:k0 + wk], in_=psSc[0:QB, 0:wk])
                    yield

            def gen_T(self):
                QB, LK, topk, sc, mb = self.QB, self.LK, self.topk, self.sc, self.mb
                bs, Hh, LO, mid = self.bs, self.Hh, self.LO, self.mid
                if LK > topk:
                    k.do(k.dve, "tensor_reduce", [sc], [bs], out=bs[0:QB, 0:1], in_=sc[0:QB, 0:LK], axis=AX.X, op=ALU.min)
                    k.do(k.dve, "tensor_reduce", [sc], [bs], out=bs[0:QB, 1:2], in_=sc[0:QB, 0:LK], axis=AX.X, op=ALU.max)
                    yield
                if self.diag:
                    k.do(k.dve, "memset", [], [sc], sc[0:64, LK - 64:LK], NEG)
                if LK > topk:
                    jD, jA, SA, MID, U, TA = self.jD, self.jA, self.SA, self.mid, self.LO, self.TA
                    k.do(k.dve, "tensor_tensor", [bs], [bs], out=bs[0:QB, 2:3], in0=bs[0:QB, 1:2], in1=bs[0:QB, 0:1], op=ALU.subtract)
                    k.do(k.dve, "tensor_scalar", [pw, bs], [Hh], out=Hh[0:QB, :], in0=pw[0:QB, :], scalar1=bs[0:QB, 2:3], scalar2=None, op0=ALU.mult)
                    k.do(k.dve, "tensor_tensor", [bs, Hh], [MID], out=MID[0:QB, 0:1], in0=bs[0:QB, 0:1], in1=Hh[0:QB, 0:1], op=ALU.add)
                    j0 = jD[0:QB, 0:1]
                    k1 = LK
                    if LK >= 1024:
                        k1 = max(128, int(round(LK * (1.0 - ACT_FRAC) / 128.0)) * 128)
                    nA = LK - k1
                    jb = bass.AP(j0.tensor, j0.offset, [list(j0.ap[0]), [0, k1]])
                    thrc = float(topk) - 0.5 * nA
                    if nA > 0:
                        a0 = jA[0:QB, 0:1]
                        jbA = bass.AP(a0.tensor, a0.offset, [list(a0.ap[0]), [0, nA]])
                        k.do(k.dve, "memset", [], [SA], SA[0:QB, :], 0.0)
                        k.do(k.dve, "memset", [], [TA], TA[0:QB, NITER:NITER + 1], thrc)
                    for it in range(NITER):
                        if nA > 0:
                            k.do(k.act, "activation", [sc, MID], [jA, SA], out=jbA, in_=sc[0:QB, k1:LK], func=AF.Sign, scale=-1.0, bias=MID[0:QB, it:it + 1], accum_out=SA[0:QB, it:it + 1])
                            k.do(k.act, "activation", [SA, TA], [TA], out=TA[0:QB, it:it + 1], in_=SA[0:QB, it:it + 1], func=AF.Identity, scale=0.5, bias=TA[0:QB, NITER:NITER + 1])
                        k.do(k.dve, "tensor_scalar", [sc, MID], [jD, bs], out=jb, in0=sc[0:QB, 0:k1], scalar1=MID[0:QB, it:it + 1], scalar2=None, op0=ALU.is_ge, op1=ALU.add, accum_out=bs[0:QB, 3:4])
                        if nA > 0:
                            k.do(k.dve, "tensor_scalar", [bs, TA, Hh], [U], out=U[0:QB, it:it + 1], in0=bs[0:QB, 3:4], scalar1=TA[0:QB, it:it + 1], scalar2=Hh[0:QB, it:it + 1], op0=ALU.is_ge, op1=ALU.mult)
                        else:
                            k.do(k.dve, "tensor_scalar", [bs, Hh], [U], out=U[0:QB, it:it + 1], in0=bs[0:QB, 3:4], scalar1=thrc, scalar2=Hh[0:QB, it:it + 1], op0=ALU.is_ge, op1=ALU.mult)
                        k.do(k.dve, "tensor_scalar", [U, MID, Hh], [MID], out=MID[0:QB, it + 1:it + 2], in0=U[0:QB, it:it + 1], scalar1=MID[0:QB, it:it + 1], scalar2=Hh[0:QB, it + 1:it + 2], op0=ALU.add, op1=ALU.subtract)
                        yield
                    k.do(k.dve, "tensor_tensor", [MID, Hh], [U], out=U[0:QB, NITER:NITER + 1], in0=MID[0:QB, NITER:NITER + 1], in1=Hh[0:QB, NITER:NITER + 1], op=ALU.subtract)
                    thr = U[0:QB, NITER:NITER + 1]
                    k.do(k.dve, "tensor_scalar", [sc, U], [mb], out=mb[0:QB, 0:LK], in0=sc[0:QB, 0:LK], scalar1=thr, scalar2=MASKV, op0=ALU.is_lt, op1=ALU.mult)
                else:
                    k.do(k.dve, "tensor_scalar", [sc], [mb], out=mb[0:QB, 0:LK], in0=sc[0:QB, 0:LK], scalar1=-1.0e29, scalar2=MASKV, op0=ALU.is_lt, op1=ALU.mult)
                yield

            def gen_A(self):
                QB, LK, q_, ga_, og_, mb, I4 = self.QB, self.LK, self.q_, self.ga_, self.og_, self.mb, self.I4
                PTb, rden = W["PTb"], W["rden"]
                N4 = 4 * QB
                nj = (LK + 127) // 128
                for g in range(2):
                    def qk(j):
                        wk = min(128, LK - j * 128)
                        p = psS[ctr["s"] % 2]
                        ctr["s"] += 1
                        k.do(k.pe, "matmul", [kT, q_], [p], p[0:wk, 0:N4], kT[:, g, j * 128:j * 128 + wk], q_[:, 4 * g:4 * g + 4, 0:QB], start=True, stop=False, sig=False)
                        k.do(k.pe, "matmul", [mb, I4], [p], p[0:wk, 0:N4], mb[0:QB, j * 128:j * 128 + wk], I4[0:QB, 0:N4], start=False, stop=True)
                        return p, wk
                    pend = qk(0)
                    for j in range(nj):
                        p, wk = pend
                        if j + 1 < nj:
                            pend = qk(j + 1)
                        pt = PTb[ctr["p"] % 3]
                        ctr["p"] += 1
                        k.do(k.act, "activation", [p], [pt], out=pt[0:wk, 0:N4], in_=p[0:wk, 0:N4], func=AF.Exp)
                        k.do(k.pe, "matmul", [Vc, pt], [psO], psO[:, 0:N4], Vc[0:wk, j, g * 128:(g + 1) * 128], pt[0:wk, 0:N4], start=(j == 0), stop=(j == nj - 1), sig=False)
                        k.do(k.pe, "matmul", [onesb, pt], [psN], psN[:, 0:N4], onesb[0:wk, :], pt[0:wk, 0:N4], start=(j == 0), stop=(j == nj - 1), sig=True)
                        yield
                    k.do(k.dve, "reciprocal", [psN], [rden], out=rden[:, 0:N4], in_=psN[:, 0:N4])
                    k.do(k.dve, "tensor_tensor", [psO, rden], [rden], out=rden[:, 0:N4], in0=psO[:, 0:N4], in1=rden[:, 0:N4], op=ALU.mult)
                    k.do(k.pool, "tensor_tensor", [rden, ga_], [og_], out=og_[:, 4 * g:4 * g + 4, 0:QB], in0=rden[:, 0:N4].rearrange("p (a b) -> p a b", b=QB), in1=ga_[:, 4 * g:4 * g + 4, 0:QB], op=ALU.mult)
                    yield
                k.dma(fm_view(S["og"], self.tok0, QB), og_[:, :, 0:QB], [og_], [S["og"]])

        def run(gen):
            for _ in gen:
                pass

        def interleave(gens):
            items = []
            for g in gens:
                steps = []
                items.append((g, steps))
            live = [g for g in gens]
            while live:
                for g in list(live):
                    try:
                        next(g)
                    except StopIteration:
                        live.remove(g)

        def interleave_n(gl):
            st_ = [[g, n, 0, False] for (g, n) in gl]
            while True:
                live = [e for e in st_ if not e[3]]
                if not live:
                    break
                e = min(live, key=lambda e: e[2] / max(e[1], 1))
                try:
                    next(e[0])
                    e[2] += 1
                except StopIteration:
                    e[3] = True

        def interleave_ratio(gA, nA, gT, nT):
            ia = it = 0
            doneA = doneT = False
            while not (doneA and doneT):
                fa = ia / max(nA, 1)
                ft = it / max(nT, 1)
                pick_T = (not doneT) and (doneA or ft <= fa)
                if pick_T:
                    try:
                        next(gT)
                        it += 1
                    except StopIteration:
                        doneT = True
                else:
                    try:
                        next(gA)
                        ia += 1
                    except StopIteration:
                        doneA = True

        jobs = []
        srcs = []
        for t0 in range(0, TP, 256):
            nt = min(2, (TP - t0) // 128)
            srcs.append((k_all.t[t0:t0 + nt * 128, :], v_all.t[t0:t0 + nt * 128, :], ik_all.t[t0:t0 + nt * 128, :], 128, nt, [k_all, v_all, ik_all]))
        jobs.append(("p", srcs))
        for s in range(NS):
            srcs = []
            for t0 in range(0, PAST, 256):
                nt = min(2, (PAST - t0) // 128)
                srcs.append((ck_d[s, t0:t0 + nt * 128, :], cv_d[s, t0:t0 + nt * 128, :], cik_d[s, t0:t0 + nt * 128, :], 128, nt, []))
            a0 = TP + s * TS
            srcs.append((k_all.t[a0:a0 + TS, :], v_all.t[a0:a0 + TS, :], ik_all.t[a0:a0 + TS, :], TS, 1, [k_all, v_all, ik_all]))
            jobs.append(("s%d" % s, srcs))
        gbi = 0
        for ji, (jn, srcs) in enumerate(jobs):
            with ExitStack() as st2:
                psT = ps(st2, "psT_" + jn, [128, 2, 512], BF16)
                psI = ps(st2, "psI_" + jn, [128, 512], BF16)
                W["kst"] = [sb(st2, "kst%d" % i, [128, 2, 256], F32) for i in range(2)]
                W["vst"] = [sb(st2, "vst%d" % i, [128, 2, 256], F32) for i in range(2)]
                W["ist"] = [sb(st2, "ist%d" % i, [128, 2, 64], F32) for i in range(2)]
                W["kbb"] = [sb(st2, "kbb%d" % i, [128, 2, 256], BF16) for i in range(2)]
                W["ibb"] = [sb(st2, "ibb%d" % i, [128, 2, 64], BF16) for i in range(2)]
                build_cache(srcs)
                k.barrier()
            with ExitStack() as st2:
                psD = [ps(st2, "psD%d_%s" % (i, jn), [128, 512], F32) for i in range(3)]
                psSc = ps(st2, "psSc_" + jn, [128, 512], F32)
                psS = [ps(st2, "psS%d_%s" % (i, jn), [128, 512], F32) for i in range(2)]
                psO = ps(st2, "psO_" + jn, [128, 512], F32)
                psN = ps(st2, "psN_" + jn, [128, 512], F32)
                W["qT"] = [sb(st2, "qT%d" % i, [128, 8, 128], BF16) for i in range(2)]
                W["gaT"] = [sb(st2, "gaT%d" % i, [128, 8, 128], BF16) for i in range(2)]
                W["ogT"] = [sb(st2, "ogT%d" % i, [128, 8, 128], BF16) for i in range(1)]
                W["iqT"] = [sb(st2, "iqT%d" % i, [64, 8, 128], BF16) for i in range(2)]
                W["iwt"] = [sb(st2, "iwt%d" % i, [128, 8], F32) for i in range(2)]
                W["Wd"] = [sb(st2, "Wd%d" % i, [128, 8, 128], BF16) for i in range(2)]
                W["Rb"] = [sb(st2, "Rb%d" % i, [128, 512], BF16) for i in range(3)]
                W["PTb"] = [sb(st2, "PTb%d" % i, [128, 512], BF16) for i in range(3)]
                W["rden"] = sb(st2, "rden", [128, 512], F32)
                W["bs"] = [sb(st2, "bs%d" % i, [128, 8], F32) for i in range(2)]
                W["Hh"] = [sb(st2, "Hh%d" % i, [128, NITER + 1], F32) for i in range(2)]
                W["LO"] = [sb(st2, "LO%d" % i, [128, NITER + 1], F32) for i in range(2)]
                W["mid"] = [sb(st2, "mid%d" % i, [128, NITER + 2], F32) for i in range(2)]
                W["TA"] = [sb(st2, "TA%d" % i, [128, NITER + 1], F32) for i in range(2)]
                W["jD"] = [sb(st2, "jD%d" % i, [128, 2], F32) for i in range(2)]
                W["jA"] = [sb(st2, "jA%d" % i, [128, 2], F32) for i in range(2)]
                W["SA"] = [sb(st2, "SA%d" % i, [128, NITER], F32) for i in range(2)]
                if ji == 0:
                    NB = TP // 128
                    blocks = [QB_(b, b * 128, 128, (b + 1) * 128, cfg.topk_p, True, I4p) for b in range(NB)]
                else:
                    blocks = [QB_(0, TP + (ji - 1) * TS, TS, cfg.LS, cfg.topk_s, False, I4s)]
                NB = len(blocks)
                blocks[0].early_loads()
                blocks[0].late_loads()
                run(blocks[0].gen_I())
                for b in range(NB):
                    gl = []
                    if b + 1 < NB:
                        blocks[b + 1].early_loads()
                    nT = (NITER + 2) if blocks[b].LK > blocks[b].topk else 1
                    gl.append((blocks[b].gen_T(), nT))
                    if b >= 1:
                        gl.append((blocks[b - 1].gen_A(), 2 * ((blocks[b - 1].LK + 127) // 128 + 1)))
                    if b + 1 < NB:
                        gl.append((blocks[b + 1].gen_I(), 8 * ((blocks[b + 1].LK + 511) // 512)))
                    interleave_n(gl)
                    if b + 1 < NB:
                        blocks[b + 1].late_loads()
                run(blocks[NB - 1].gen_A())
                k.barrier()
        k.barrier()

    TW3 = 512
    with ExitStack() as st:
        wab = sb(st, "wab", [128, 8, D], BF16)
        wbb = sb(st, "wbb", [128, 8, D], BF16)
        wob = sb(st, "wob", [128, 8, D], BF16)
        wdw = sb(st, "wdw", [128, 8, 31], F32)
        vec = sb(st, "vec", [128, 3, 8], F32)
        gfb = sb(st, "gfb", [128, D], F32)
        onesf = sb(st, "onesf", [128, 128], F32)
        gate_p = sb(st, "gate_p", [128, D], F32)
        gate_s = [sb(st, "gate_s%d" % i, [TS, D], F32) for i in range(NS)]
        k.dma(gate_p[:], G_p.t, [G_p], [gate_p])
        for i in range(NS):
            k.dma(gate_s[i][:], G_s[i].t, [G_s[i]], [gate_s[i]])
        k.do(k.dve, "memset", [], [onesf], onesf[:], 1.0 / D)
        k.dma(wdw[:], wdw_d.rearrange("p (a b) -> p a b", b=31), [], [wdw])
        k.dma(vec[:], vec_d.rearrange("p (a b) -> p a b", b=8), [], [vec])
        k.dma(gfb[:], gf_d, [], [gfb])
        with ExitStack() as st2:
            wl = [sb(st2, "wl%d" % i, [128, 2 * D], F32) for i in range(2)]
            li = 0
            for (src, dst) in [(wa_d, wab), (wb_d, wbb), (wo_d, wob)]:
                for q4 in range(4):
                    w_ = wl[li % 2]
                    li += 1
                    k.dma(w_[:], src[:, q4 * 2 * D:(q4 + 1) * 2 * D], [], [w_])
                    k.do(k.pool if li % 2 else k.dve, "tensor_copy", [w_], [dst], out=dst[:, 2 * q4:2 * q4 + 2, :], in_=w_[:].rearrange("p (a b) -> p a b", b=D))
            k.barrier()
        uext = [sb(st, "uext%d" % i, [128, 8, 30 + TW3], BF16) for i in range(2)]
        gbt = [sb(st, "gbt%d" % i, [128, 8, TW3], BF16) for i in range(1)]
        mat = [sb(st, "mat%d" % i, [128, 8, TW3], BF16) for i in range(1)]
        mbt = [sb(st, "mbt%d" % i, [128, 8, TW3], BF16) for i in range(1)]
        ogt = [sb(st, "ogt%d" % i, [128, 8, TW3], BF16) for i in range(1)]
        sstg = sb(st, "sstg", [128, 8, 30], F32)
        acc = [sb(st, "acc%d" % i, [128, TW3], F32) for i in range(8)]
        sq = [sb(st, "sq%d" % i, [128, TW3], F32) for i in range(2)]
        diag = [sb(st, "diag%d" % i, [128, 128], BF16) for i in range(6)]
        mean_sb = sb(st, "mean_sb", [128, TW3], F32)
        rstd_bc = sb(st, "rstd_bc", [128, TW3], F32)
        t1 = [sb(st, "t1_%d" % i, [128, TW3], F32) for i in range(2)]
        t2 = [sb(st, "t2_%d" % i, [128, TW3], F32) for i in range(2)]
        ybin = sb(st, "ybin", [128, 8, TW3], BF16)
        mixed = sb(st, "mixed", [128, 8, TW3], BF16)
        m1 = [sb(st, "m1_%d" % i, [128, TW3], F32) for i in range(2)]
        m2 = [sb(st, "m2_%d" % i, [128, TW3], F32) for i in range(2)]
        xt3 = [sb(st, "xt3_%d" % i, [128, D], F32) for i in range(2)]
        xn = [sb(st, "xn%d" % i, [128, D], F32) for i in range(2)]
        yt = [sb(st, "yt%d" % i, [128, D], F32) for i in range(2)]
        junk3 = sb(st, "junk3", [128, D], F32)
        st3 = [sb(st, "st3_%d" % i, [128, 4], F32) for i in range(2)]
        psM = ps(st, "psM", [128, 512], F32)
        psQ = ps(st, "psQ", [128, 512], F32)
        psA = [ps(st, "psA%d" % i, [128, 512], F32) for i in range(1)]
        psB = [ps(st, "psB%d" % i, [128, 512], F32) for i in range(1)]
        psC = [ps(st, "psC%d" % i, [128, 512], F32) for i in range(2)]
        psY = [ps(st, "psY%d" % i, [128, 512], F32) for i in range(2)]
        cnt = {"t": 0, "c": 0, "y": 0, "m": 0}

        def tile3(tok0, TW, first, sidx, gate):
            i = cnt["t"]
            cnt["t"] += 1
            ue, gb_, ma_, mb_, og_ = uext[i % 2], gbt[0], mat[0], mbt[0], ogt[0]
            if first:
                if sidx is None:
                    k.do(k.pool, "memset", [], [ue], ue[:, :, 0:30], 0.0)
                else:
                    k.dma(sstg[:], sconv_d[sidx].rearrange("p (a b) -> p a b", b=30), [], [sstg])
                    k.do(k.pool, "tensor_copy", [sstg], [ue], out=ue[:, :, 0:30], in_=sstg[:])
                k.dma(ue[:, :, 30:30 + TW], fm_view(S["ua"], tok0, TW), [S["ua"]], [ue])
            else:
                k.dma(ue[:, :, 0:30 + TW], fm_view(S["ua"], tok0 - 30, TW + 30), [S["ua"]], [ue])
            k.dma(gb_[:, :, 0:TW], fm_view(S["gb"], tok0, TW), [S["gb"]], [gb_])
            k.dma(ma_[:, :, 0:TW], fm_view(S["ma"], tok0, TW), [S["ma"]], [ma_])
            k.dma(mb_[:, :, 0:TW], fm_view(S["mb"], tok0, TW), [S["mb"]], [mb_])
            k.dma(og_[:, :, 0:TW], fm_view(S["og"], tok0, TW), [S["og"]], [og_])
            for c in range(8):
                pcv = psC[c % 2]
                for j in range(31):
                    dg = diag[cnt["c"] % 6]
                    cnt["c"] += 1
                    k.do(k.dve, "tensor_scalar", [identf, wdw], [dg], out=dg[:], in0=identf[:], scalar1=wdw[:, c, j:j + 1], scalar2=None, op0=ALU.mult)
                    k.do(k.pe, "matmul", [dg, ue], [pcv], pcv[:, 0:TW], dg[:], ue[:, c, j:j + TW], start=(j == 0), stop=(j == 30))
                k.do(k.act, "activation", [pcv, vec], [acc[c]], out=acc[c][:, 0:TW], in_=pcv[:, 0:TW], func=AF.Identity, bias=vec[:, 0, c:c + 1])
            for c in range(8):
                s_ = sq[c % 2]
                k.do(k.act, "activation", [acc[c]], [s_], out=s_[:, 0:TW], in_=acc[c][:, 0:TW], func=AF.Square)
                k.do(k.pe, "matmul", [onesf, acc[c]], [psM], psM[:, 0:TW], onesf[:], acc[c][:, 0:TW], start=(c == 0), stop=(c == 7), sig=False)
                k.do(k.pe, "matmul", [onesf, s_], [psQ], psQ[:, 0:TW], onesf[:], s_[:, 0:TW], start=(c == 0), stop=(c == 7), sig=True)
            k.do(k.act, "copy", [psM], [mean_sb], out=mean_sb[:, 0:TW], in_=psM[:, 0:TW])
            k.do(k.dve, "tensor_tensor", [mean_sb], [rstd_bc], out=rstd_bc[:, 0:TW], in0=mean_sb[:, 0:TW], in1=mean_sb[:, 0:TW], op=ALU.mult)
            k.do(k.dve, "tensor_tensor", [psQ, rstd_bc], [rstd_bc], out=rstd_bc[:, 0:TW], in0=psQ[:, 0:TW], in1=rstd_bc[:, 0:TW], op=ALU.subtract)
            k.do(k.act, "activation", [rstd_bc, epsT], [rstd_bc], out=rstd_bc[:, 0:TW], in_=rstd_bc[:, 0:TW], func=AF.Sqrt, bias=epsT[:, 0:1])
            k.do(k.dve, "reciprocal", [rstd_bc], [rstd_bc], out=rstd_bc[:, 0:TW], in_=rstd_bc[:, 0:TW])
            for c in range(8):
                a_, b_ = t1[c % 2], t2[c % 2]
                k.do(k.dve, "tensor_tensor", [acc[c], mean_sb], [a_], out=a_[:, 0:TW], in0=acc[c][:, 0:TW], in1=mean_sb[:, 0:TW], op=ALU.subtract)
                k.do(k.dve, "tensor_tensor", [a_, rstd_bc], [a_], out=a_[:, 0:TW], in0=a_[:, 0:TW], in1=rstd_bc[:, 0:TW], op=ALU.mult)
                k.do(k.act, "activation", [a_, vec], [b_], out=b_[:, 0:TW], in_=a_[:, 0:TW], func=AF.Silu, scale=vec[:, 1, c:c + 1], bias=vec[:, 2, c:c + 1])
                k.do(k.dve, "tensor_tensor", [b_, gb_], [ybin], out=ybin[:, c, 0:TW], in0=b_[:, 0:TW], in1=gb_[:, c, 0:TW], op=ALU.mult)
            for cc in range(8):
                pa, pb = psA[0], psB[0]
                for kc in range(8):
                    k.do(k.pe, "matmul", [wab, og_], [pa], pa[:, 0:TW], wab[:, kc, cc * 128:(cc + 1) * 128], og_[:, kc, 0:TW], start=(kc == 0), stop=(kc == 7), sig=(kc == 7))
                for kc in range(8):
                    k.do(k.pe, "matmul", [wbb, ybin], [pb], pb[:, 0:TW], wbb[:, kc, cc * 128:(cc + 1) * 128], ybin[:, kc, 0:TW], start=(kc == 0), stop=(kc == 7), sig=(kc == 7))
                a_, b_ = m1[cc % 2], m2[cc % 2]
                k.do(k.dve, "tensor_tensor", [pa, ma_], [a_], out=a_[:, 0:TW], in0=pa[:, 0:TW], in1=ma_[:, cc, 0:TW], op=ALU.mult)
                k.do(k.dve, "tensor_tensor", [pb, mb_], [b_], out=b_[:, 0:TW], in0=pb[:, 0:TW], in1=mb_[:, cc, 0:TW], op=ALU.mult)
                k.do(k.pool, "tensor_tensor", [a_, b_], [mixed], out=mixed[:, cc, 0:TW], in0=a_[:, 0:TW], in1=b_[:, 0:TW], op=ALU.add)
            for s0 in range(0, TW, 128):
                rows = min(128, TW - s0)
                yi = cnt["y"]
                cnt["y"] += 1
                x_, xn_, y_, s_ = xt3[yi % 2], xn[yi % 2], yt[yi % 2], st3[yi % 2]
                k.dma(x_[0:rows, :], x_all[tok0 + s0:tok0 + s0 + rows, :], [], [x_])
                for half in range(2):
                    py = psY[half]
                    for kc in range(8):
                        k.do(k.pe, "matmul", [mixed, wob], [py], py[0:rows, :], mixed[:, kc, s0:s0 + rows], wob[:, kc, half * 512:(half + 1) * 512], start=(kc == 0), stop=(kc == 7), sig=(kc == 7))
                    hs = slice(half * 512, half * 512 + 512)
                    k.do(k.dve, "tensor_tensor", [py, gate], [xn_], out=xn_[0:rows, hs], in0=py[0:rows, :], in1=gate[0:rows, hs], op=ALU.mult)
                k.do(k.pool, "tensor_tensor", [xn_, x_], [xn_], out=xn_[0:rows, :], in0=xn_[0:rows, :], in1=x_[0:rows, :], op=ALU.add)
                k.do(k.act, "activation", [xn_], [junk3, s_], out=junk3[0:rows, :], in_=xn_[0:rows, :], func=AF.Square, accum_out=s_[0:rows, 0:1])
                k.do(k.act, "activation", [s_, epsT], [s_], out=s_[0:rows, 1:2], in_=s_[0:rows, 0:1], func=AF.Sqrt, scale=1.0 / D, bias=epsT[0:rows, 0:1])
                k.do(k.dve, "reciprocal", [s_], [s_], out=s_[0:rows, 2:3], in_=s_[0:rows, 1:2])
                k.do(k.dve, "scalar_tensor_tensor", [xn_, s_, gfb], [y_], out=y_[0:rows, :], in0=xn_[0:rows, :], scalar=s_[0:rows, 2:3], in1=gfb[0:rows, :], op0=ALU.mult, op1=ALU.mult)
                k.dma(y_all.t[tok0 + s0:tok0 + s0 + rows, :], y_[0:rows, :], [y_], [y_all])

        for t0 in range(0, TP, TW3):
            tile3(t0, min(TW3, TP - t0), t0 == 0, None, gate_p)
        for s in range(NS):
            tile3(TP + s * TS, TS, True, s, gate_s[s])
        k.barrier()
    k.final_wait()
    P0.close()
    k.stack0.close()
    return nc, k


def _consts():
    c = np.zeros((128, 128 + 192 + 512 + 128), np.float32)
    c[:, 0:128] = np.eye(128, dtype=np.float32)
    c[0, 128:256] = 1.0
    c[1, 256:288] = 1.0
    c[2, 288:320] = 1.0
    for r in range(4):
        c[:, 320 + r * 128:320 + (r + 1) * 128] = np.eye(128, dtype=np.float32)
        c[0:32, 832 + r * 32:832 + (r + 1) * 32] = np.eye(32, dtype=np.float32)
    return c


def _kc_layout(w):
    n = w.shape[1]
    return np.ascontiguousarray(w.reshape(8, 128, n).transpose(1, 0, 2)).reshape(128, 8 * n)


def _fmT(v, n):
    return np.ascontiguousarray(v.reshape(n, 128).T)


def make_in_maps(cfg, x_prompt, x_sample, cache_k, cache_v, cache_idx_k, state_conv, c_prompt, c_sample,
                 w_ada, b_ada, g_norm, w_in, w_a, w_dw, b_dw, g_ln, b_ln, w_b, w_out, g_final, n_cores):
    f = lambda a: np.ascontiguousarray(np.asarray(a, dtype=np.float32))
    w_in0 = f(w_in[0])
    wada = _kc_layout(f(w_ada[0])).reshape(128, 8, 3072)
    bada = np.ascontiguousarray(np.broadcast_to(f(b_ada[0])[None, :], (3, 3072)))
    gn_bc = np.ascontiguousarray(np.broadcast_to(f(g_norm[0])[None, :], (128, D)))
    gf_bc = np.ascontiguousarray(np.broadcast_to(f(g_final)[None, :], (128, D)))
    wfm = np.stack([_kc_layout(w_in0[:, FM_OFF[kd] + c * 128: FM_OFF[kd] + (c + 1) * 128]) for kd in FM_KINDS for c in range(8)])
    wiq = np.stack([_kc_layout(w_in0[:, O_IQ + h * 64: O_IQ + (h + 1) * 64]) for h in range(8)])
    wkv = _kc_layout(np.concatenate([w_in0[:, O_K:O_K + 512], w_in0[:, O_IK:O_IK + 64], w_in0[:, O_IW:O_IW + 8]], axis=1))
    wa, wb, wo = _kc_layout(f(w_a[0])), _kc_layout(f(w_b[0])), _kc_layout(f(w_out[0]))
    wdwT = np.ascontiguousarray(f(w_dw[0]).T.reshape(8, 128, 31).transpose(1, 0, 2)).reshape(128, 8 * 31)
    vecT = np.concatenate([_fmT(f(b_dw[0]), 8), _fmT(f(g_ln[0]), 8), _fmT(f(b_ln[0]), 8)], axis=1)
    consts = _consts()
    NS, TS = cfg.NS, cfg.TS
    maps = []
    for i in range(n_cores):
        ss = slice(i * NS, (i + 1) * NS)
        xs = f(x_sample[ss]).reshape(NS * TS, D)
        x_all = np.concatenate([f(x_prompt[i]), xs], axis=0)
        cc = np.concatenate([f(c_prompt[i])[None], f(c_sample[ss])], axis=0)
        cT = np.ascontiguousarray(cc.reshape(3, 8, 128).transpose(2, 1, 0))
        sconv = f(state_conv[0][ss])
        sconvT = np.ascontiguousarray(sconv.reshape(NS, 30, 8, 128).transpose(0, 3, 2, 1)).reshape(NS, 128, 240)
        maps.append({
            "x_all": x_all, "cT": cT, "wada": wada, "bada": bada, "gn_bc": gn_bc, "wfm": wfm, "wiq": wiq, "wkv": wkv,
            "wa": wa, "wb": wb, "wo": wo, "wdwT": wdwT, "vecT": vecT, "gf_bc": gf_bc,
            "ck": f(cache_k[0][ss]).reshape(NS, cfg.PAST, 256), "cv": f(cache_v[0][ss]).reshape(NS, cfg.PAST, 256),
            "cik": f(cache_idx_k[0][ss]), "sconvT": sconvT, "consts": consts,
        })
    return maps


def assemble(cfg, results, n_cores):
    TP, NS, TS = cfg.TP, cfg.NS, cfg.TS
    g = lambda name: [np.asarray(r[name]) for r in results]
    y, kk, vv, ik, cv = g("y_all"), g("k_all"), g("v_all"), g("ik_all"), g("conv_o")
    y_p = np.stack([a[:TP] for a in y])
    y_s = np.concatenate([a[TP:].reshape(NS, TS, D) for a in y])
    k_p = np.stack([a[:TP].reshape(TP, 2, 128) for a in kk])[None]
    v_p = np.stack([a[:TP].reshape(TP, 2, 128) for a in vv])[None]
    ik_p = np.stack([a[:TP] for a in ik])[None]
    c_p = np.stack([a[0] for a in cv])[None]
    k_s = np.concatenate([a[TP:].reshape(NS, TS, 2, 128) for a in kk])[None]
    v_s = np.concatenate([a[TP:].reshape(NS, TS, 2, 128) for a in vv])[None]
    ik_s = np.concatenate([a[TP:].reshape(NS, TS, 64) for a in ik])[None]
    c_s = np.concatenate([a[1:] for a in cv])[None]
    return tuple(np.ascontiguousarray(a, dtype=np.float32) for a in (y_p, y_s, k_p, v_p, ik_p, c_p, k_s, v_s, ik_s, c_s))


def kernel(**inputs):
    cfg = Cfg()
    n = 8
    nc, _ = build(cfg)
    maps = make_in_maps(cfg, n_cores=n, **inputs)
    res = run_bass_kernel_spmd(nc, maps, core_ids=list(range(n)))
    return assemble(cfg, res.results, n)
```

```python
import numpy as np
from contextlib import ExitStack
import concourse.bass as bass
import concourse.mybir as mybir
from concourse.bass_utils import run_bass_kernel_spmd

F32 = mybir.dt.float32
BF16 = mybir.dt.bfloat16
AF = mybir.ActivationFunctionType
ALU = mybir.AluOpType
AX = mybir.AxisListType

D = 1024
NIN = 8264
CHUNK = 64
NITER = 22
import os
ACT_FRAC = float(os.environ.get("ACT_FRAC", "0.0"))
NEG = -1.0e30
MASKV = -30000.0

O_Q, O_K, O_V, O_GA, O_IQ, O_IW, O_IK, O_GLU, O_GB, O_M = 0, 1024, 1280, 1536, 2560, 3072, 3080, 3144, 5192, 6216
FM_KINDS = ["q", "ga", "ua", "ub", "gb", "ma", "mb"]
FM_OFF = {"q": O_Q, "ga": O_GA, "ua": O_GLU, "ub": O_GLU + 1024, "gb": O_GB, "ma": O_M, "mb": O_M + 1024}


class Sem:
    def __init__(self, k, name):
        self.h = k.stack0.enter_context(k.nc.semaphore(name))
        self.v = 0


class Buf:
    def __init__(self, t, disjoint=False):
        self.t = t
        self.w = {}
        self.r = {}
        self.disjoint = disjoint

    def __getitem__(self, idx):
        return self.t[idx]


def _merge(d, tok):
    if tok is None:
        return
    s, v = tok
    if id(s) not in d or d[id(s)][1] < v:
        d[id(s)] = (s, v)


class Eng:
    def __init__(self, k, e, name, same=True):
        self.k = k
        self.e = e
        self.name = name
        self.sem = Sem(k, "e_" + name)
        self.seen = {}
        self.same = same
        self.pending = []

    def wait_tok(self, s, v):
        if s is self.sem and not self.same:
            return
        if self.seen.get(id(s), 0) < v:
            self.e.wait_ge(s.h, v)
            self.seen[id(s)] = v

    def wait_bufs(self, reads, writes):
        for b in reads:
            for s, v in b.w.values():
                self.wait_tok(s, v)
        for b in writes:
            if b.disjoint:
                continue
            for s, v in b.w.values():
                self.wait_tok(s, v)
            for s, v in b.r.values():
                self.wait_tok(s, v)


class K:
    def __init__(self, nc):
        self.nc = nc
        self.stack0 = ExitStack()
        self.pe = Eng(self, nc.tensor, "pe", same=False)
        self.act = Eng(self, nc.scalar, "act")
        self.dve = Eng(self, nc.vector, "dve")
        self.pool = Eng(self, nc.gpsimd, "pool")
        self.sp = Eng(self, nc.sync, "sp")
        self.engs = [self.pe, self.act, self.dve, self.pool, self.sp]
        self.dsems = [Sem(self, "d%d" % i) for i in range(24)]
        self.di = 0
        self.nins = 0

    def _post(self, E, tok, reads, writes):
        for b in writes:
            if not b.disjoint:
                b.w = {}
                b.r = {}
            _merge(b.w, tok)
        for b in reads:
            _merge(b.r, tok)

    def do(self, E, fn, reads, writes, *a, sig=True, **kw):
        E.wait_bufs(reads, writes)
        ins = getattr(E.e, fn)(*a, **kw)
        self.nins += 1
        if sig:
            E.sem.v += 1
            ins.then_inc(E.sem.h, 1)
            tok = (E.sem, E.sem.v)
            for (rr, ww) in E.pending:
                self._post(E, tok, rr, ww)
            E.pending = []
            self._post(E, tok, reads, writes)
            return tok
        E.pending.append((reads, writes))
        return None

    def dma(self, out, in_, reads, writes, E=None, **kw):
        E = E or self.sp
        E.wait_bufs(reads, writes)
        s = self.dsems[self.di % len(self.dsems)]
        self.di += 1
        if s.v > 0:
            E.wait_tok(s, s.v)
        s.v += 16
        E.e.dma_start(out=out, in_=in_, **kw).then_inc(s.h, 16)
        self.nins += 1
        tok = (s, s.v)
        for b in writes:
            _merge(b.w, tok)
        for b in reads:
            _merge(b.r, tok)
        return tok

    def barrier(self):
        sems = [e.sem for e in self.engs] + self.dsems
        for E in [self.pe, self.act, self.dve, self.pool, self.sp]:
            for s in sems:
                if s.v > 0 and s is not E.sem:
                    E.wait_tok(s, s.v)

    def final_wait(self):
        for s in self.dsems:
            if s.v > 0:
                self.sp.wait_tok(s, s.v)
        for e in self.engs:
            if e is not self.sp and e.sem.v > 0:
                self.sp.wait_tok(e.sem, e.sem.v)


class Cfg:
    def __init__(self, TP=8192, PAST=2048, TS=32, NS=2, topk_max=256, debug=False):
        self.TP, self.PAST, self.TS, self.NS = TP, PAST, TS, NS
        self.debug = debug
        self.NTOK = TP + NS * TS
        self.topk_p = min(topk_max, TP // 4)
        self.topk_s = min(topk_max, (PAST + TS) // 4)
        self.LS = PAST + TS
        self.LMAX = max(TP, ((self.LS + 127) // 128) * 128)
        self.NKT = self.LMAX // 128


def build(cfg):
    TP, NTOK, TS, NS, PAST = cfg.TP, cfg.NTOK, cfg.TS, cfg.NS, cfg.PAST
    SROWS = NS * TS
    nc = bass.Bass("TRN2", target_bir_lowering=False)
    k = K(nc)

    def din(name, shape, dt=F32):
        return nc.dram_tensor(name, list(shape), dt, kind="ExternalInput").ap()

    def dout(name, shape, dt=F32):
        return nc.dram_tensor(name, list(shape), dt, kind="ExternalOutput").ap()

    def dscr(name, shape, dt=BF16):
        return Buf(nc.dram_tensor(name, list(shape), dt, kind="Internal").ap(), disjoint=True)

    x_all = din("x_all", [NTOK, D])
    cT_d = din("cT", [128, 8, 3])
    wada_d = din("wada", [128, 8, 3072])
    bada_d = din("bada", [3, 3072])
    gn_d = din("gn_bc", [128, D])
    wfm_d = din("wfm", [56, 128, 8 * 128])
    wiq_d = din("wiq", [8, 128, 8 * 64])
    wkv_d = din("wkv", [128, 8 * 584])
    wa_d = din("wa", [128, 8 * D])
    wb_d = din("wb", [128, 8 * D])
    wo_d = din("wo", [128, 8 * D])
    wdw_d = din("wdwT", [128, 8 * 31])
    vec_d = din("vecT", [128, 3 * 8])
    gf_d = din("gf_bc", [128, D])
    ck_d = din("ck", [NS, PAST, 256])
    cv_d = din("cv", [NS, PAST, 256])
    cik_d = din("cik", [NS, PAST, 64])
    sconv_d = din("sconvT", [NS, 128, 8 * 30])
    const_d = din("consts", [128, 128 + 192 + 512 + 128])

    y_all = Buf(disjoint=True, t=dout("y_all", [NTOK, D]))
    k_all = Buf(disjoint=True, t=dout("k_all", [NTOK, 256]))
    v_all = Buf(disjoint=True, t=dout("v_all", [NTOK, 256]))
    ik_all = Buf(disjoint=True, t=dout("ik_all", [NTOK, 64]))
    conv_o = Buf(disjoint=True, t=dout("conv_o", [1 + NS, 30, D]))

    S = {kind: dscr("s_" + kind, [8, 128, NTOK]) for kind in FM_KINDS}
    if cfg.debug:
        S["og"] = Buf(nc.dram_tensor("s_og", [8, 128, NTOK], BF16, kind="ExternalOutput").ap(), disjoint=True)
    else:
        S["og"] = dscr("s_og", [8, 128, NTOK])
    S_iq = dscr("s_iq", [8, 64, NTOK])
    S_iw = dscr("s_iw", [NTOK, 8], F32)

    def fm_view(b, t0, n):
        return b.t.rearrange("h d t -> d h t")[:, :, t0:t0 + n]

    P0 = ExitStack()

    uniq = [0]

    def sb(st, name, shape, dt):
        uniq[0] += 1
        return Buf(st.enter_context(nc.sbuf_tensor("%s_u%d" % (name, uniq[0]), list(shape), dt)))

    def ps(st, name, shape, dt):
        uniq[0] += 1
        return Buf(st.enter_context(nc.psum_tensor("%s_u%d" % (name, uniq[0]), list(shape), dt)))

    identf = sb(P0, "identf", [128, 128], F32)
    identb = sb(P0, "identb", [128, 128], BF16)
    G_p = Buf(nc.dram_tensor("g_p", [128, D], F32, kind="Internal").ap(), disjoint=True)
    G_s = [Buf(nc.dram_tensor("g_s%d" % i, [TS, D], F32, kind="Internal").ap(), disjoint=True) for i in range(NS)]
    epsT = sb(P0, "epsT", [128, 1], F32)
    onesb = sb(P0, "onesb", [128, 128], BF16)
    I4p = sb(P0, "I4p", [128, 512], BF16)
    I4s = sb(P0, "I4s", [32, 128], BF16)
    PC = ExitStack()
    cst = sb(PC, "cst", [128, 128 + 192 + 512 + 128], F32)

    k.dma(cst[:], const_d, [], [cst])
    k.do(k.dve, "tensor_copy", [cst], [identf], out=identf[:], in_=cst[:, 0:128])
    k.do(k.dve, "tensor_copy", [cst], [identb], out=identb[:], in_=cst[:, 0:128])
    k.do(k.dve, "memset", [], [epsT], epsT[:], 1e-6)
    k.do(k.dve, "memset", [], [onesb], onesb[:], 1.0)
    k.do(k.dve, "tensor_copy", [cst], [I4p], out=I4p[:], in_=cst[:, 320:832])
    k.do(k.dve, "tensor_copy", [cst], [I4s], out=I4s[:], in_=cst[0:32, 832:960])
    sel = lambda a, b_: cst[0:3, 128 + a:128 + b_]

    P1 = ExitStack()
    hT = sb(P1, "hT", [128, 8, NTOK], BF16)
    P1a = ExitStack()
    AB_p = sb(P1a, "AB_p", [128, 2 * D], F32)
    AB_s = sb(P1a, "AB_s", [SROWS, 2 * D], F32)
    with ExitStack() as st:
        cT = sb(st, "cT_t", [128, 8, 3], F32)
        sc = sb(st, "sc_t", [128, 8, 3], F32)
        modrows = sb(st, "modrows", [3, 3072], F32)
        gn = sb(st, "gn", [128, D], F32)
        wat = [sb(st, "wat%d" % i, [128, 8, 256], F32) for i in range(2)]
        pm = ps(st, "pm", [128, 512], F32)
        pbp = [ps(st, "pbp%d" % i, [128, 512], F32) for i in range(2)]
        k.dma(cT[:], cT_d, [], [cT])
        k.dma(modrows[:], bada_d, [], [modrows])
        k.dma(gn[:], gn_d, [], [gn])
        k.do(k.act, "activation", [cT], [sc], out=sc[:], in_=cT[:], func=AF.Silu)
        wview = wada_d
        for i in range(12):
            w = wat[i % 2]
            k.dma(w[:], wview[:, :, i * 256:(i + 1) * 256], [], [w])
            for kc in range(8):
                k.do(k.pe, "matmul", [sc, w], [pm], pm[0:3, 0:256], sc[:, kc, :], w[:, kc, :], start=(kc == 0), stop=(kc == 7), sig=(kc == 7))
            k.do(k.dve, "tensor_tensor", [pm, modrows], [modrows], out=modrows[:, i * 256:(i + 1) * 256], in0=pm[0:3, 0:256], in1=modrows[:, i * 256:(i + 1) * 256], op=ALU.add)
        gstg = [sb(st, "gstg%d" % i, [128, 512], F32) for i in range(2)]
        targets = [((0, 128), 128, AB_p, G_p), ((128, 128 + SROWS), SROWS, AB_s, None)]
        for si in range(NS):
            targets.append(((128 + si * 32, 128 + si * 32 + TS), TS, None, G_s[si]))
        bi = 0
        for (sc0, sc1), M, AB, gt in targets:
            for part in range(3):
                if part < 2 and AB is None:
                    continue
                if part == 2 and gt is None:
                    continue
                for half in range(2):
                    pb = pbp[bi % 2]
                    bi += 1
                    k.do(k.pe, "matmul", [cst, modrows], [pb], pb[0:M, :], sel(sc0, sc1), modrows[:, part * 1024 + half * 512: part * 1024 + half * 512 + 512], start=True, stop=True)
                    cs = slice(half * 512, half * 512 + 512)
                    if part == 1:
                        k.do(k.dve, "scalar_tensor_tensor", [pb, gn], [AB], out=AB[0:M, cs], in0=pb[0:M, :], scalar=1.0, in1=gn[0:M, cs], op0=ALU.add, op1=ALU.mult)
                    elif part == 0:
                        k.do(k.act, "copy", [pb], [AB], out=AB[0:M, D + half * 512: D + half * 512 + 512], in_=pb[0:M, :])
                    else:
                        g_ = gstg[bi % 2]
                        k.do(k.act, "copy", [pb], [g_], out=g_[0:M, :], in_=pb[0:M, :])
                        k.dma(gt.t[0:M, cs], g_[0:M, :], [g_], [gt])
        k.barrier()

    NT = (NTOK + 127) // 128
    with ExitStack() as st:
        xb = [sb(st, "xb%d" % i, [128, D], F32) for i in range(3)]
        h1 = [sb(st, "h1_%d" % i, [128, D], F32) for i in range(2)]
        hb = [sb(st, "hb%d" % i, [128, D], BF16) for i in range(2)]
        junk = sb(st, "junk1", [128, D], F32)
        stats = [sb(st, "stats%d" % i, [128, 4], F32) for i in range(3)]
        pT = [ps(st, "pT%d" % i, [128, 8, 128], BF16) for i in range(2)]
        def stage1(tt):
            t0 = tt * 128
            rows = min(128, NTOK - t0)
            x = xb[tt % 3]
            s_ = stats[tt % 3]
            k.dma(x[0:rows, :], x_all[t0:t0 + rows, :], [], [x])
            k.do(k.act, "activation", [x], [junk, s_], out=junk[0:rows, :], in_=x[0:rows, :], func=AF.Square, accum_out=s_[0:rows, 0:1])
            k.do(k.act, "activation", [s_, epsT], [s_], out=s_[0:rows, 1:2], in_=s_[0:rows, 0:1], func=AF.Sqrt, scale=1.0 / D, bias=epsT[0:rows, 0:1])
        stage1(0)
        for tt in range(NT):
            t0 = tt * 128
            rows = min(128, NTOK - t0)
            AB = AB_p if t0 < TP else AB_s
            x = xb[tt % 3]
            s_ = stats[tt % 3]
            if tt + 1 < NT:
                stage1(tt + 1)
            k.do(k.dve, "reciprocal", [s_], [s_], out=s_[0:rows, 2:3], in_=s_[0:rows, 1:2])
            h_ = h1[tt % 2]
            k.do(k.dve, "scalar_tensor_tensor", [x, s_, AB], [h_], out=h_[0:rows, :], in0=x[0:rows, :], scalar=s_[0:rows, 2:3], in1=AB[0:rows, 0:D], op0=ALU.mult, op1=ALU.mult)
            hb_ = hb[tt % 2]
            k.do(k.dve, "tensor_tensor", [h_, AB], [hb_], out=hb_[0:rows, :], in0=h_[0:rows, :], in1=AB[0:rows, D:2 * D], op=ALU.add)
            p = pT[tt % 2]
            for c in range(8):
                k.do(k.pe, "transpose", [hb_, identb], [p], p[:, c, 0:rows], hb_[0:rows, c * 128:(c + 1) * 128], identb[0:rows, 0:rows], sig=(c == 7))
            k.do(k.act, "copy", [p], [hT], out=hT[:, :, t0:t0 + rows], in_=p[:, :, 0:rows])
        k.barrier()
    P1a.close()

    groups = [(g0, min(512, NTOK - g0)) for g0 in range(0, NTOK, 512)]
    with ExitStack() as st:
        wkvb = sb(st, "wkvb", [128, 8, 584], BF16)
        kvst = [sb(st, "kvst%d" % i, [128, 584], F32) for i in range(2)]
        wst = [sb(st, "wst%d" % i, [128, 8 * 128], F32) for i in range(2)]
        wbf = [sb(st, "wbf%d" % i, [128, 8, 128], BF16) for i in range(3)]
        stg = [sb(st, "stg%d" % i, [128, 512], BF16) for i in range(3)]
        sig_ = [sb(st, "sig%d" % i, [128, 512], F32) for i in range(2)]
        cstg = [sb(st, "cstg%d" % i, [64, 256], F32) for i in range(2)]
        pz = [ps(st, "pz%d" % i, [128, 512], F32) for i in range(4)]
        pk0 = ps(st, "pk0", [128, 512], F32)
        pk1 = ps(st, "pk1", [128, 512], F32)
        pc = ps(st, "pcv", [128, 512], F32)
        for kc in range(8):
            b_ = kvst[kc % 2]
            k.dma(b_[:], wkv_d[:, kc * 584:(kc + 1) * 584], [], [b_])
            k.do(k.pool, "tensor_copy", [b_], [wkvb], out=wkvb[:, kc, :], in_=b_[:])
        for tt in range(NT):
            t0 = tt * 128
            rows = min(128, NTOK - t0)
            for kc in range(8):
                k.do(k.pe, "matmul", [hT, wkvb], [pk0], pk0[0:rows, :], hT[:, kc, t0:t0 + rows], wkvb[:, kc, 0:512], start=(kc == 0), stop=(kc == 7), sig=False)
            for kc in range(8):
                k.do(k.pe, "matmul", [hT, wkvb], [pk1], pk1[0:rows, 0:72], hT[:, kc, t0:t0 + rows], wkvb[:, kc, 512:584], start=(kc == 0), stop=(kc == 7), sig=(kc == 7))
            o_ = kvst[tt % 2]
            k.do(k.act, "copy", [pk0], [o_], out=o_[0:rows, 0:512], in_=pk0[0:rows, :])
            k.do(k.dve, "tensor_copy", [pk1], [o_], out=o_[0:rows, 512:584], in_=pk1[0:rows, 0:72])
            k.dma(k_all.t[t0:t0 + rows, :], o_[0:rows, 0:256], [o_], [k_all])
            k.dma(v_all.t[t0:t0 + rows, :], o_[0:rows, 256:512], [o_], [v_all])
            k.dma(ik_all.t[t0:t0 + rows, :], o_[0:rows, 512:576], [o_], [ik_all])
            k.dma(S_iw.t[t0:t0 + rows, :], o_[0:rows, 576:584], [o_], [S_iw])

        wi = [0]
        zi = [0]
        si_ = [0]

        def load_w(src_ap, width):
            i = wi[0]
            wi[0] += 1
            ws = wst[i % 2]
            wb_ = wbf[i % 3]
            k.dma(ws[:, 0:8 * width], src_ap, [], [ws])
            k.do(k.pool, "tensor_copy", [ws], [wb_], out=wb_[:, :, 0:width], in_=ws[:, 0:8 * width].rearrange("p (a b) -> p a b", b=width))
            return wb_

        def proj(wb_, width, g0, n):
            p = pz[zi[0] % 4]
            zi[0] += 1
            for kc in range(8):
                k.do(k.pe, "matmul", [hT, wb_], [p], p[0:width, 0:n], wb_[:, kc, 0:width], hT[:, kc, g0:g0 + n], start=(kc == 0), stop=(kc == 7), sig=(kc == 7))
            return p

        def single(kind, c, func, scale=1.0):
            wb_ = load_w(wfm_d[FM_KINDS.index(kind) * 8 + c], 128)
            for (g0, n) in groups:
                p = proj(wb_, 128, g0, n)
                o_ = stg[si_[0] % 3]
                si_[0] += 1
                k.do(k.act, "activation", [p], [o_], out=o_[:, 0:n], in_=p[:, 0:n], func=func, scale=scale)
                k.dma(S[kind].t[c, :, g0:g0 + n], o_[:, 0:n], [o_], [S[kind]])

        for c in range(8):
            single("q", c, AF.Copy, scale=128.0 ** -0.5)
        for h in range(8):
            wb_ = load_w(wiq_d[h], 64)
            for (g0, n) in groups:
                p = proj(wb_, 64, g0, n)
                o_ = stg[si_[0] % 3]
                si_[0] += 1
                k.do(k.dve, "tensor_copy", [p], [o_], out=o_[0:64, 0:n], in_=p[0:64, 0:n])
                k.dma(S_iq.t[h, :, g0:g0 + n], o_[0:64, 0:n], [o_], [S_iq])
        crow_sets = [(TP - 32, 32, [(0, 2, 32)]), (TP, SROWS, [(1 + s, TS * s + 2, TS * s + TS) for s in range(NS)])]
        ci = 0
        for c in range(8):
            wA = load_w(wfm_d[FM_KINDS.index("ua") * 8 + c], 128)
            wB = load_w(wfm_d[FM_KINDS.index("ub") * 8 + c], 128)
            for (g0, n) in groups:
                pA = proj(wA, 128, g0, n)
                pB = proj(wB, 128, g0, n)
                sg = sig_[si_[0] % 2]
                o_ = stg[si_[0] % 3]
                si_[0] += 1
                k.do(k.act, "activation", [pB], [sg], out=sg[:, 0:n], in_=pB[:, 0:n], func=AF.Sigmoid)
                k.do(k.dve, "tensor_tensor", [pA, sg], [o_], out=o_[:, 0:n], in0=pA[:, 0:n], in1=sg[:, 0:n], op=ALU.mult)
                k.dma(S["ua"].t[c, :, g0:g0 + n], o_[:, 0:n], [o_], [S["ua"]])
            for (r0, M, outs) in crow_sets:
                for kc in range(8):
                    k.do(k.pe, "matmul", [hT, wA], [pc], pc[0:M, 0:128], hT[:, kc, r0:r0 + M], wA[:, kc, :], start=(kc == 0), stop=(kc == 7), sig=False)
                for kc in range(8):
                    k.do(k.pe, "matmul", [hT, wB], [pc], pc[0:M, 128:256], hT[:, kc, r0:r0 + M], wB[:, kc, :], start=(kc == 0), stop=(kc == 7), sig=(kc == 7))
                cs_ = cstg[ci % 2]
                ci += 1
                k.do(k.act, "activation", [pc], [cs_], out=cs_[0:M, 128:256], in_=pc[0:M, 128:256], func=AF.Sigmoid)
                k.do(k.dve, "tensor_tensor", [pc, cs_], [cs_], out=cs_[0:M, 0:128], in0=pc[0:M, 0:128], in1=cs_[0:M, 128:256], op=ALU.mult)
                for (job, ra, rb) in outs:
                    k.dma(conv_o.t[job, :, c * 128:(c + 1) * 128], cs_[ra:rb, 0:128], [cs_], [conv_o])
        for c in range(8):
            single("ga", c, AF.Silu)
        for c in range(8):
            single("gb", c, AF.Silu)
        for c in range(8):
            single("ma", c, AF.Sigmoid)
        for c in range(8):
            single("mb", c, AF.Sigmoid)
        k.barrier()
    P1.close()
    PC.close()

    LMAX, NKT = cfg.LMAX, cfg.NKT
    with ExitStack() as st:
        Vc = sb(st, "Vc", [128, NKT, 256], BF16)
        kT = sb(st, "kT", [128, 2, LMAX], BF16)
        ikT = sb(st, "ikT", [64, LMAX], BF16)
        score = [sb(st, "score%d" % i, [128, LMAX], F32) for i in range(2)]
        mbias = [sb(st, "mbias%d" % i, [128, LMAX], BF16) for i in range(2)]
        pw = sb(st, "pw", [128, NITER + 1], F32)
        for i in range(NITER + 1):
            k.do(k.dve, "memset", [], [pw], pw[:, i:i + 1], 2.0 ** -(i + 1))
        psD = psSc = psS = psO = psN = psT = psI = None
        W = {}

        def build_cache(srcs):
            kst, vst, ist, kbb, ibb = W["kst"], W["vst"], W["ist"], W["kbb"], W["ibb"]
            kpos = 0
            bi = 0
            for (kap, vap, iap, rows, nt, deps) in srcs:
                a = bi % 2
                bi += 1
                ks_, vs_, is_, kb_, ib_ = kst[a], vst[a], ist[a], kbb[a], ibb[a]
                if nt > 1 or rows == 128:
                    k.dma(ks_[:, 0:nt, :], kap.rearrange("(a p) c -> p a c", p=128), deps, [ks_])
                    k.dma(vs_[:, 0:nt, :], vap.rearrange("(a p) c -> p a c", p=128), deps, [vs_])
                    k.dma(is_[:, 0:nt, :], iap.rearrange("(a p) c -> p a c", p=128), deps, [is_])
                else:
                    k.dma(ks_[0:rows, 0, :], kap, deps, [ks_])
                    k.dma(vs_[0:rows, 0, :], vap, deps, [vs_])
                    k.dma(is_[0:rows, 0, :], iap, deps, [is_])
                kt0 = kpos // 128
                k.do(k.pool, "tensor_copy", [ks_], [kb_], out=kb_[0:rows, 0:nt, :], in_=ks_[0:rows, 0:nt, :])
                k.do(k.dve, "tensor_copy", [vs_], [Vc], out=Vc[0:rows, kt0:kt0 + nt, :], in_=vs_[0:rows, 0:nt, :])
                k.do(k.pool, "tensor_copy", [is_], [ib_], out=ib_[0:rows, 0:nt, :], in_=is_[0:rows, 0:nt, :])
                for a_ in range(nt):
                    for g in range(2):
                        k.do(k.pe, "transpose", [kb_, identb], [psT], psT[:, g, a_ * 128:a_ * 128 + rows], kb_[0:rows, a_, g * 128:(g + 1) * 128], identb[0:rows, 0:rows], sig=False)
                    k.do(k.pe, "transpose", [ib_, identb], [psI], psI[0:64, a_ * 128:a_ * 128 + rows], ib_[0:rows, a_, :], identb[0:rows, 0:rows], sig=(a_ == nt - 1))
                n = (nt - 1) * 128 + rows
                k.do(k.act, "copy", [psT], [kT], out=kT[:, :, kpos:kpos + n], in_=psT[:, :, 0:n])
                k.do(k.dve, "tensor_copy", [psI], [ikT], out=ikT[:, kpos:kpos + n], in_=psI[0:64, 0:n])
                kpos += n
            return kpos

        ctr = {"d": 0, "r": 0, "s": 0, "p": 0}

        class QB_:
            def __init__(self, bi, tok0, QB, LK, topk, diag, I4):
                self.bi, self.tok0, self.QB, self.LK, self.topk, self.diag, self.I4 = bi, tok0, QB, LK, topk, diag, I4
                self.q_, self.ga_, self.og_ = W["qT"][bi % 2], W["gaT"][bi % 2], W["ogT"][0]
                self.iq_, self.iw_, self.wd_ = W["iqT"][bi % 2], W["iwt"][bi % 2], W["Wd"][bi % 2]
                self.sc, self.mb = score[bi % 2], mbias[bi % 2]
                self.bs, self.Hh, self.LO, self.mid = W["bs"][bi % 2], W["Hh"][bi % 2], W["LO"][bi % 2], W["mid"][bi % 2]
                self.jD, self.jA, self.SA, self.TA = W["jD"][bi % 2], W["jA"][bi % 2], W["SA"][bi % 2], W["TA"][bi % 2]

            def early_loads(self):
                QB, tok0 = self.QB, self.tok0
                k.dma(self.iq_[:, :, 0:QB], S_iq.t.rearrange("h d t -> d h t")[:, :, tok0:tok0 + QB], [S_iq], [self.iq_])
                k.dma(self.iw_[0:QB, :], S_iw.t[tok0:tok0 + QB, :], [S_iw], [self.iw_])
                for h in range(8):
                    k.do(k.pool, "tensor_scalar", [identf, self.iw_], [self.wd_], out=self.wd_[0:QB, h, 0:QB], in0=identf[0:QB, 0:QB], scalar1=self.iw_[0:QB, h:h + 1], scalar2=None, op0=ALU.mult, sig=(h == 7))

            def late_loads(self):
                QB, tok0 = self.QB, self.tok0
                k.dma(self.q_[:, :, 0:QB], fm_view(S["q"], tok0, QB), [S["q"]], [self.q_])
                k.dma(self.ga_[:, :, 0:QB], fm_view(S["ga"], tok0, QB), [S["ga"]], [self.ga_])

            def gen_I(self):
                QB, LK, iq_, wd_, sc = self.QB, self.LK, self.iq_, self.wd_, self.sc
                Rb = W["Rb"]
                nkt = (LK + 511) // 512
                for kt in range(nkt):
                    k0 = kt * 512
                    wk = min(512, LK - k0)

                    def dmm(h):
                        p = psD[ctr["d"] % 3]
                        ctr["d"] += 1
                        k.do(k.pe, "matmul", [iq_, ikT], [p], p[0:QB, 0:wk], iq_[:, h, 0:QB], ikT[:, k0:k0 + wk], start=True, stop=True)
                        return p
                    pq = [dmm(0), dmm(1)]
                    rq = []
                    for h in range(8):
                        p = pq.pop(0)
                        r_ = Rb[ctr["r"] % 3]
                        ctr["r"] += 1
                        k.do(k.act, "activation", [p], [r_], out=r_[0:QB, 0:wk], in_=p[0:QB, 0:wk], func=AF.Relu)
                        if h + 2 < 8:
                            pq.append(dmm(h + 2))
                        k.do(k.pe, "matmul", [wd_, r_], [psSc], psSc[0:QB, 0:wk], wd_[0:QB, h, 0:QB], r_[0:QB, 0:wk], start=(h == 0), stop=(h == 7), sig=(h == 7))
                        if h < 7:
                            yield
                    k.do(k.act, "copy", [psSc], [sc], out=sc[0:QB, k0:k0 + wk], in_=psSc[0:QB, 0:wk])
                    yield

            def gen_T(self):
                QB, LK, topk, sc, mb = self.QB, self.LK, self.topk, self.sc, self.mb
                bs, Hh, LO, mid = self.bs, self.Hh, self.LO, self.mid
                if LK > topk:
                    k.do(k.dve, "tensor_reduce", [sc], [bs], out=bs[0:QB, 0:1], in_=sc[0:QB, 0:LK], axis=AX.X, op=ALU.min)
                    k.do(k.dve, "tensor_reduce", [sc], [bs], out=bs[0:QB, 1:2], in_=sc[0:QB, 0:LK], axis=AX.X, op=ALU.max)
                    yield
                if self.diag:
                    k.do(k.dve, "memset", [], [sc], sc[0:64, LK - 64:LK], NEG)
                if LK > topk:
                    jD, jA, SA, MID, U, TA = self.jD, self.jA, self.SA, self.mid, self.LO, self.TA
                    k.do(k.dve, "tensor_tensor", [bs], [bs], out=bs[0:QB, 2:3], in0=bs[0:QB, 1:2], in1=bs[0:QB, 0:1], op=ALU.subtract)
                    k.do(k.dve, "tensor_scalar", [pw, bs], [Hh], out=Hh[0:QB, :], in0=pw[0:QB, :], scalar1=bs[0:QB, 2:3], scalar2=None, op0=ALU.mult)
                    k.do(k.dve, "tensor_tensor", [bs, Hh], [MID], out=MID[0:QB, 0:1], in0=bs[0:QB, 0:1], in1=Hh[0:QB, 0:1], op=ALU.add)
                    j0 = jD[0:QB, 0:1]
                    k1 = LK
                    if LK >= 1024:
                        k1 = max(128, int(round(LK * (1.0 - ACT_FRAC) / 128.0)) * 128)
                    nA = LK - k1
                    jb = bass.AP(j0.tensor, j0.offset, [list(j0.ap[0]), [0, k1]])
                    thrc = float(topk) - 0.5 * nA
                    if nA > 0:
                        a0 = jA[0:QB, 0:1]
                        jbA = bass.AP(a0.tensor, a0.offset, [list(a0.ap[0]), [0, nA]])
                        k.do(k.dve, "memset", [], [SA], SA[0:QB, :], 0.0)
                        k.do(k.dve, "memset", [], [TA], TA[0:QB, NITER:NITER + 1], thrc)
                    for it in range(NITER):
                        if nA > 0:
                            k.do(k.act, "activation", [sc, MID], [jA, SA], out=jbA, in_=sc[0:QB, k1:LK], func=AF.Sign, scale=-1.0, bias=MID[0:QB, it:it + 1], accum_out=SA[0:QB, it:it + 1])
                            k.do(k.act, "activation", [SA, TA], [TA], out=TA[0:QB, it:it + 1], in_=SA[0:QB, it:it + 1], func=AF.Identity, scale=0.5, bias=TA[0:QB, NITER:NITER + 1])
                        k.do(k.dve, "tensor_scalar", [sc, MID], [jD, bs], out=jb, in0=sc[0:QB, 0:k1], scalar1=MID[0:QB, it:it + 1], scalar2=None, op0=ALU.is_ge, op1=ALU.add, accum_out=bs[0:QB, 3:4])
                        if nA > 0:
                            k.do(k.dve, "tensor_scalar", [bs, TA, Hh], [U], out=U[0:QB, it:it + 1], in0=bs[0:QB, 3:4], scalar1=TA[0:QB, it:it + 1], scalar2=Hh[0:QB, it:it + 1], op0=ALU.is_ge, op1=ALU.mult)
                        else:
                            k.do(k.dve, "tensor_scalar", [bs, Hh], [U], out=U[0:QB, it:it + 1], in0=bs[0:QB, 3:4], scalar1=thrc, scalar2=Hh[0:QB, it:it + 1], op0=ALU.is_ge, op1=ALU.mult)
                        k.do(k.dve, "tensor_scalar", [U, MID, Hh], [MID], out=MID[0:QB, it + 1:it + 2], in0=U[0:QB, it:it + 1], scalar1=MID[0:QB, it:it + 1], scalar2=Hh[0:QB, it + 1:it + 2], op0=ALU.add, op1=ALU.subtract)
                        yield
                    k.do(k.dve, "tensor_tensor", [MID, Hh], [U], out=U[0:QB, NITER:NITER + 1], in0=MID[0:QB, NITER:NITER + 1], in1=Hh[0:QB, NITER:NITER + 1], op=ALU.subtract)
                    thr = U[0:QB, NITER:NITER + 1]
                    k.do(k.dve, "tensor_scalar", [sc, U], [mb], out=mb[0:QB, 0:LK], in0=sc[0:QB, 0:LK], scalar1=thr, scalar2=MASKV, op0=ALU.is_lt, op1=ALU.mult)
                else:
                    k.do(k.dve, "tensor_scalar", [sc], [mb], out=mb[0:QB, 0:LK], in0=sc[0:QB, 0:LK], scalar1=-1.0e29, scalar2=MASKV, op0=ALU.is_lt, op1=ALU.mult)
                yield

            def gen_A(self):
                QB, LK, q_, ga_, og_, mb, I4 = self.QB, self.LK, self.q_, self.ga_, self.og_, self.mb, self.I4
                PTb, rden = W["PTb"], W["rden"]
                N4 = 4 * QB
                nj = (LK + 127) // 128
                for g in range(2):
                    def qk(j):
                        wk = min(128, LK - j * 128)
                        p = psS[ctr["s"] % 2]
                        ctr["s"] += 1
                        k.do(k.pe, "matmul", [kT, q_], [p], p[0:wk, 0:N4], kT[:, g, j * 128:j * 128 + wk], q_[:, 4 * g:4 * g + 4, 0:QB], start=True, stop=False, sig=False)
                        k.do(k.pe, "matmul", [mb, I4], [p], p[0:wk, 0:N4], mb[0:QB, j * 128:j * 128 + wk], I4[0:QB, 0:N4], start=False, stop=True)
                        return p, wk
                    pend = qk(0)
                    for j in range(nj):
                        p, wk = pend
                        if j + 1 < nj:
                            pend = qk(j + 1)
                        pt = PTb[ctr["p"] % 3]
                        ctr["p"] += 1
                        k.do(k.act, "activation", [p], [pt], out=pt[0:wk, 0:N4], in_=p[0:wk, 0:N4], func=AF.Exp)
                        k.do(k.pe, "matmul", [Vc, pt], [psO], psO[:, 0:N4], Vc[0:wk, j, g * 128:(g + 1) * 128], pt[0:wk, 0:N4], start=(j == 0), stop=(j == nj - 1), sig=False)
                        k.do(k.pe, "matmul", [onesb, pt], [psN], psN[:, 0:N4], onesb[0:wk, :], pt[0:wk, 0:N4], start=(j == 0), stop=(j == nj - 1), sig=True)
                        yield
                    k.do(k.dve, "reciprocal", [psN], [rden], out=rden[:, 0:N4], in_=psN[:, 0:N4])
                    k.do(k.dve, "tensor_tensor", [psO, rden], [rden], out=rden[:, 0:N4], in0=psO[:, 0:N4], in1=rden[:, 0:N4], op=ALU.mult)
                    k.do(k.pool, "tensor_tensor", [rden, ga_], [og_], out=og_[:, 4 * g:4 * g + 4, 0:QB], in0=rden[:, 0:N4].rearrange("p (a b) -> p a b", b=QB), in1=ga_[:, 4 * g:4 * g + 4, 0:QB], op=ALU.mult)
                    yield
                k.dma(fm_view(S["og"], self.tok0, QB), og_[:, :, 0:QB], [og_], [S["og"]])

        def run(gen):
            for _ in gen:
                pass

        def interleave(gens):
            items = []
            for g in gens:
                steps = []
                items.append((g, steps))
            live = [g for g in gens]
            while live:
                for g in list(live):
                    try:
                        next(g)
                    except StopIteration:
                        live.remove(g)

        def interleave_n(gl):
            st_ = [[g, n, 0, False] for (g, n) in gl]
            while True:
                live = [e for e in st_ if not e[3]]
                if not live:
                    break
                e = min(live, key=lambda e: e[2] / max(e[1], 1))
                try:
                    next(e[0])
                    e[2] += 1
                except StopIteration:
                    e[3] = True

        def interleave_ratio(gA, nA, gT, nT):
            ia = it = 0
            doneA = doneT = False
            while not (doneA and doneT):
                fa = ia / max(nA, 1)
                ft = it / max(nT, 1)
                pick_T = (not doneT) and (doneA or ft <= fa)
                if pick_T:
                    try:
                        next(gT)
                        it += 1
                    except StopIteration:
                        doneT = True
                else:
                    try:
                        next(gA)
                        ia += 1
                    except StopIteration:
                        doneA = True

        jobs = []
        srcs = []
        for t0 in range(0, TP, 256):
            nt = min(2, (TP - t0) // 128)
            srcs.append((k_all.t[t0:t0 + nt * 128, :], v_all.t[t0:t0 + nt * 128, :], ik_all.t[t0:t0 + nt * 128, :], 128, nt, [k_all, v_all, ik_all]))
        jobs.append(("p", srcs))
        for s in range(NS):
            srcs = []
            for t0 in range(0, PAST, 256):
                nt = min(2, (PAST - t0) // 128)
                srcs.append((ck_d[s, t0:t0 + nt * 128, :], cv_d[s, t0:t0 + nt * 128, :], cik_d[s, t0:t0 + nt * 128, :], 128, nt, []))
            a0 = TP + s * TS
            srcs.append((k_all.t[a0:a0 + TS, :], v_all.t[a0:a0 + TS, :], ik_all.t[a0:a0 + TS, :], TS, 1, [k_all, v_all, ik_all]))
            jobs.append(("s%d" % s, srcs))
        gbi = 0
        for ji, (jn, srcs) in enumerate(jobs):
            with ExitStack() as st2:
                psT = ps(st2, "psT_" + jn, [128, 2, 512], BF16)
                psI = ps(st2, "psI_" + jn, [128, 512], BF16)
                W["kst"] = [sb(st2, "kst%d" % i, [128, 2, 256], F32) for i in range(2)]
                W["vst"] = [sb(st2, "vst%d" % i, [128, 2, 256], F32) for i in range(2)]
                W["ist"] = [sb(st2, "ist%d" % i, [128, 2, 64], F32) for i in range(2)]
                W["kbb"] = [sb(st2, "kbb%d" % i, [128, 2, 256], BF16) for i in range(2)]
                W["ibb"] = [sb(st2, "ibb%d" % i, [128, 2, 64], BF16) for i in range(2)]
                build_cache(srcs)
                k.barrier()
            with ExitStack() as st2:
                psD = [ps(st2, "psD%d_%s" % (i, jn), [128, 512], F32) for i in range(3)]
                psSc = ps(st2, "psSc_" + jn, [128, 512], F32)
                psS = [ps(st2, "psS%d_%s" % (i, jn), [128, 512], F32) for i in range(2)]
                psO = ps(st2, "psO_" + jn, [128, 512], F32)
                psN = ps(st2, "psN_" + jn, [128, 512], F32)
                W["qT"] = [sb(st2, "qT%d" % i, [128, 8, 128], BF16) for i in range(2)]
                W["gaT"] = [sb(st2, "gaT%d" % i, [128, 8, 128], BF16) for i in range(2)]
                W["ogT"] = [sb(st2, "ogT%d" % i, [128, 8, 128], BF16) for i in range(1)]
                W["iqT"] = [sb(st2, "iqT%d" % i, [64, 8, 128], BF16) for i in range(2)]
                W["iwt"] = [sb(st2, "iwt%d" % i, [128, 8], F32) for i in range(2)]
                W["Wd"] = [sb(st2, "Wd%d" % i, [128, 8, 128], BF16) for i in range(2)]
                W["Rb"] = [sb(st2, "Rb%d" % i, [128, 512], BF16) for i in range(3)]
                W["PTb"] = [sb(st2, "PTb%d" % i, [128, 512], BF16) for i in range(3)]
                W["rden"] = sb(st2, "rden", [128, 512], F32)
                W["bs"] = [sb(st2, "bs%d" % i, [128, 8], F32) for i in range(2)]
                W["Hh"] = [sb(st2, "Hh%d" % i, [128, NITER + 1], F32) for i in range(2)]
                W["LO"] = [sb(st2, "LO%d" % i, [128, NITER + 1], F32) for i in range(2)]
                W["mid"] = [sb(st2, "mid%d" % i, [128, NITER + 2], F32) for i in range(2)]
                W["TA"] = [sb(st2, "TA%d" % i, [128, NITER + 1], F32) for i in range(2)]
                W["jD"] = [sb(st2, "jD%d" % i, [128, 2], F32) for i in range(2)]
                W["jA"] = [sb(st2, "jA%d" % i, [128, 2], F32) for i in range(2)]
                W["SA"] = [sb(st2, "SA%d" % i, [128, NITER], F32) for i in range(2)]
                if ji == 0:
                    NB = TP // 128
                    blocks = [QB_(b, b * 128, 128, (b + 1) * 128, cfg.topk_p, True, I4p) for b in range(NB)]
                else:
                    blocks = [QB_(0, TP + (ji - 1) * TS, TS, cfg.LS, cfg.topk_s, False, I4s)]
                NB = len(blocks)
                blocks[0].early_loads()
                blocks[0].late_loads()
                run(blocks[0].gen_I())
                for b in range(NB):
                    gl = []
                    if b + 1 < NB:
                        blocks[b + 1].early_loads()
                    nT = (NITER + 2) if blocks[b].LK > blocks[b].topk else 1
                    gl.append((blocks[b].gen_T(), nT))
                    if b >= 1:
                        gl.append((blocks[b - 1].gen_A(), 2 * ((blocks[b - 1].LK + 127) // 128 + 1)))
                    if b + 1 < NB:
                        gl.append((blocks[b + 1].gen_I(), 8 * ((blocks[b + 1].LK + 511) // 512)))
                    interleave_n(gl)
                    if b + 1 < NB:
                        blocks[b + 1].late_loads()
                run(blocks[NB - 1].gen_A())
                k.barrier()
        k.barrier()

    TW3 = 512
    with ExitStack() as st:
        wab = sb(st, "wab", [128, 8, D], BF16)
        wbb = sb(st, "wbb", [128, 8, D], BF16)
        wob = sb(st, "wob", [128, 8, D], BF16)
        wdw = sb(st, "wdw", [128, 8, 31], F32)
        vec = sb(st, "vec", [128, 3, 8], F32)
        gfb = sb(st, "gfb", [128, D], F32)
        onesf = sb(st, "onesf", [128, 128], F32)
        gate_p = sb(st, "gate_p", [128, D], F32)
        gate_s = [sb(st, "gate_s%d" % i, [TS, D], F32) for i in range(NS)]
        k.dma(gate_p[:], G_p.t, [G_p], [gate_p])
        for i in range(NS):
            k.dma(gate_s[i][:], G_s[i].t, [G_s[i]], [gate_s[i]])
        k.do(k.dve, "memset", [], [onesf], onesf[:], 1.0 / D)
        k.dma(wdw[:], wdw_d.rearrange("p (a b) -> p a b", b=31), [], [wdw])
        k.dma(vec[:], vec_d.rearrange("p (a b) -> p a b", b=8), [], [vec])
        k.dma(gfb[:], gf_d, [], [gfb])
        with ExitStack() as st2:
            wl = [sb(st2, "wl%d" % i, [128, 2 * D], F32) for i in range(2)]
            li = 0
            for (src, dst) in [(wa_d, wab), (wb_d, wbb), (wo_d, wob)]:
                for q4 in range(4):
                    w_ = wl[li % 2]
                    li += 1
                    k.dma(w_[:], src[:, q4 * 2 * D:(q4 + 1) * 2 * D], [], [w_])
                    k.do(k.pool if li % 2 else k.dve, "tensor_copy", [w_], [dst], out=dst[:, 2 * q4:2 * q4 + 2, :], in_=w_[:].rearrange("p (a b) -> p a b", b=D))
            k.barrier()
        uext = [sb(st, "uext%d" % i, [128, 8, 30 + TW3], BF16) for i in range(2)]
        gbt = [sb(st, "gbt%d" % i, [128, 8, TW3], BF16) for i in range(1)]
        mat = [sb(st, "mat%d" % i, [128, 8, TW3], BF16) for i in range(1)]
        mbt = [sb(st, "mbt%d" % i, [128, 8, TW3], BF16) for i in range(1)]
        ogt = [sb(st, "ogt%d" % i, [128, 8, TW3], BF16) for i in range(1)]
        sstg = sb(st, "sstg", [128, 8, 30], F32)
        acc = [sb(st, "acc%d" % i, [128, TW3], F32) for i in range(8)]
        sq = [sb(st, "sq%d" % i, [128, TW3], F32) for i in range(2)]
        diag = [sb(st, "diag%d" % i, [128, 128], BF16) for i in range(6)]
        mean_sb = sb(st, "mean_sb", [128, TW3], F32)
        rstd_bc = sb(st, "rstd_bc", [128, TW3], F32)
        t1 = [sb(st, "t1_%d" % i, [128, TW3], F32) for i in range(2)]
        t2 = [sb(st, "t2_%d" % i, [128, TW3], F32) for i in range(2)]
        ybin = sb(st, "ybin", [128, 8, TW3], BF16)
        mixed = sb(st, "mixed", [128, 8, TW3], BF16)
        m1 = [sb(st, "m1_%d" % i, [128, TW3], F32) for i in range(2)]
        m2 = [sb(st, "m2_%d" % i, [128, TW3], F32) for i in range(2)]
        xt3 = [sb(st, "xt3_%d" % i, [128, D], F32) for i in range(2)]
        xn = [sb(st, "xn%d" % i, [128, D], F32) for i in range(2)]
        yt = [sb(st, "yt%d" % i, [128, D], F32) for i in range(2)]
        junk3 = sb(st, "junk3", [128, D], F32)
        st3 = [sb(st, "st3_%d" % i, [128, 4], F32) for i in range(2)]
        psM = ps(st, "psM", [128, 512], F32)
        psQ = ps(st, "psQ", [128, 512], F32)
        psA = [ps(st, "psA%d" % i, [128, 512], F32) for i in range(1)]
        psB = [ps(st, "psB%d" % i, [128, 512], F32) for i in range(1)]
        psC = [ps(st, "psC%d" % i, [128, 512], F32) for i in range(2)]
        psY = [ps(st, "psY%d" % i, [128, 512], F32) for i in range(2)]
        cnt = {"t": 0, "c": 0, "y": 0, "m": 0}

        def tile3(tok0, TW, first, sidx, gate):
            i = cnt["t"]
            cnt["t"] += 1
            ue, gb_, ma_, mb_, og_ = uext[i % 2], gbt[0], mat[0], mbt[0], ogt[0]
            if first:
                if sidx is None:
                    k.do(k.pool, "memset", [], [ue], ue[:, :, 0:30], 0.0)
                else:
                    k.dma(sstg[:], sconv_d[sidx].rearrange("p (a b) -> p a b", b=30), [], [sstg])
                    k.do(k.pool, "tensor_copy", [sstg], [ue], out=ue[:, :, 0:30], in_=sstg[:])
                k.dma(ue[:, :, 30:30 + TW], fm_view(S["ua"], tok0, TW), [S["ua"]], [ue])
            else:
                k.dma(ue[:, :, 0:30 + TW], fm_view(S["ua"], tok0 - 30, TW + 30), [S["ua"]], [ue])
            k.dma(gb_[:, :, 0:TW], fm_view(S["gb"], tok0, TW), [S["gb"]], [gb_])
            k.dma(ma_[:, :, 0:TW], fm_view(S["ma"], tok0, TW), [S["ma"]], [ma_])
            k.dma(mb_[:, :, 0:TW], fm_view(S["mb"], tok0, TW), [S["mb"]], [mb_])
            k.dma(og_[:, :, 0:TW], fm_view(S["og"], tok0, TW), [S["og"]], [og_])
            for c in range(8):
                pcv = psC[c % 2]
                for j in range(31):
                    dg = diag[cnt["c"] % 6]
                    cnt["c"] += 1
                    k.do(k.dve, "tensor_scalar", [identf, wdw], [dg], out=dg[:], in0=identf[:], scalar1=wdw[:, c, j:j + 1], scalar2=None, op0=ALU.mult)
                    k.do(k.pe, "matmul", [dg, ue], [pcv], pcv[:, 0:TW], dg[:], ue[:, c, j:j + TW], start=(j == 0), stop=(j == 30))
                k.do(k.act, "activation", [pcv, vec], [acc[c]], out=acc[c][:, 0:TW], in_=pcv[:, 0:TW], func=AF.Identity, bias=vec[:, 0, c:c + 1])
            for c in range(8):
                s_ = sq[c % 2]
                k.do(k.act, "activation", [acc[c]], [s_], out=s_[:, 0:TW], in_=acc[c][:, 0:TW], func=AF.Square)
                k.do(k.pe, "matmul", [onesf, acc[c]], [psM], psM[:, 0:TW], onesf[:], acc[c][:, 0:TW], start=(c == 0), stop=(c == 7), sig=False)
                k.do(k.pe, "matmul", [onesf, s_], [psQ], psQ[:, 0:TW], onesf[:], s_[:, 0:TW], start=(c == 0), stop=(c == 7), sig=True)
            k.do(k.act, "copy", [psM], [mean_sb], out=mean_sb[:, 0:TW], in_=psM[:, 0:TW])
            k.do(k.dve, "tensor_tensor", [mean_sb], [rstd_bc], out=rstd_bc[:, 0:TW], in0=mean_sb[:, 0:TW], in1=mean_sb[:, 0:TW], op=ALU.mult)
            k.do(k.dve, "tensor_tensor", [psQ, rstd_bc], [rstd_bc], out=rstd_bc[:, 0:TW], in0=psQ[:, 0:TW], in1=rstd_bc[:, 0:TW], op=ALU.subtract)
            k.do(k.act, "activation", [rstd_bc, epsT], [rstd_bc], out=rstd_bc[:, 0:TW], in_=rstd_bc[:, 0:TW], func=AF.Sqrt, bias=epsT[:, 0:1])
            k.do(k.dve, "reciprocal", [rstd_bc], [rstd_bc], out=rstd_bc[:, 0:TW], in_=rstd_bc[:, 0:TW])
            for c in range(8):
                a_, b_ = t1[c % 2], t2[c % 2]
                k.do(k.dve, "tensor_tensor", [acc[c], mean_sb], [a_], out=a_[:, 0:TW], in0=acc[c][:, 0:TW], in1=mean_sb[:, 0:TW], op=ALU.subtract)
                k.do(k.dve, "tensor_tensor", [a_, rstd_bc], [a_], out=a_[:, 0:TW], in0=a_[:, 0:TW], in1=rstd_bc[:, 0:TW], op=ALU.mult)
                k.do(k.act, "activation", [a_, vec], [b_], out=b_[:, 0:TW], in_=a_[:, 0:TW], func=AF.Silu, scale=vec[:, 1, c:c + 1], bias=vec[:, 2, c:c + 1])
                k.do(k.dve, "tensor_tensor", [b_, gb_], [ybin], out=ybin[:, c, 0:TW], in0=b_[:, 0:TW], in1=gb_[:, c, 0:TW], op=ALU.mult)
            for cc in range(8):
                pa, pb = psA[0], psB[0]
                for kc in range(8):
                    k.do(k.pe, "matmul", [wab, og_], [pa], pa[:, 0:TW], wab[:, kc, cc * 128:(cc + 1) * 128], og_[:, kc, 0:TW], start=(kc == 0), stop=(kc == 7), sig=(kc == 7))
                for kc in range(8):
                    k.do(k.pe, "matmul", [wbb, ybin], [pb], pb[:, 0:TW], wbb[:, kc, cc * 128:(cc + 1) * 128], ybin[:, kc, 0:TW], start=(kc == 0), stop=(kc == 7), sig=(kc == 7))
                a_, b_ = m1[cc % 2], m2[cc % 2]
                k.do(k.dve, "tensor_tensor", [pa, ma_], [a_], out=a_[:, 0:TW], in0=pa[:, 0:TW], in1=ma_[:, cc, 0:TW], op=ALU.mult)
                k.do(k.dve, "tensor_tensor", [pb, mb_], [b_], out=b_[:, 0:TW], in0=pb[:, 0:TW], in1=mb_[:, cc, 0:TW], op=ALU.mult)
                k.do(k.pool, "tensor_tensor", [a_, b_], [mixed], out=mixed[:, cc, 0:TW], in0=a_[:, 0:TW], in1=b_[:, 0:TW], op=ALU.add)
            for s0 in range(0, TW, 128):
                rows = min(128, TW - s0)
                yi = cnt["y"]
                cnt["y"] += 1
                x_, xn_, y_, s_ = xt3[yi % 2], xn[yi % 2], yt[yi % 2], st3[yi % 2]
                k.dma(x_[0:rows, :], x_all[tok0 + s0:tok0 + s0 + rows, :], [], [x_])
                for half in range(2):
                    py = psY[half]
                    for kc in range(8):
                        k.do(k.pe, "matmul", [mixed, wob], [py], py[0:rows, :], mixed[:, kc, s0:s0 + rows], wob[:, kc, half * 512:(half + 1) * 512], start=(kc == 0), stop=(kc == 7), sig=(kc == 7))
                    hs = slice(half * 512, half * 512 + 512)
                    k.do(k.dve, "tensor_tensor", [py, gate], [xn_], out=xn_[0:rows, hs], in0=py[0:rows, :], in1=gate[0:rows, hs], op=ALU.mult)
                k.do(k.pool, "tensor_tensor", [xn_, x_], [xn_], out=xn_[0:rows, :], in0=xn_[0:rows, :], in1=x_[0:rows, :], op=ALU.add)
                k.do(k.act, "activation", [xn_], [junk3, s_], out=junk3[0:rows, :], in_=xn_[0:rows, :], func=AF.Square, accum_out=s_[0:rows, 0:1])
                k.do(k.act, "activation", [s_, epsT], [s_], out=s_[0:rows, 1:2], in_=s_[0:rows, 0:1], func=AF.Sqrt, scale=1.0 / D, bias=epsT[0:rows, 0:1])
                k.do(k.dve, "reciprocal", [s_], [s_], out=s_[0:rows, 2:3], in_=s_[0:rows, 1:2])
                k.do(k.dve, "scalar_tensor_tensor", [xn_, s_, gfb], [y_], out=y_[0:rows, :], in0=xn_[0:rows, :], scalar=s_[0:rows, 2:3], in1=gfb[0:rows, :], op0=ALU.mult, op1=ALU.mult)
                k.dma(y_all.t[tok0 + s0:tok0 + s0 + rows, :], y_[0:rows, :], [y_], [y_all], E=k.pool)

        for t0 in range(0, TP, TW3):
            tile3(t0, min(TW3, TP - t0), t0 == 0, None, gate_p)
        for s in range(NS):
            tile3(TP + s * TS, TS, True, s, gate_s[s])
        k.barrier()
    k.final_wait()
    P0.close()
    k.stack0.close()
    return nc, k


def _consts():
    c = np.zeros((128, 128 + 192 + 512 + 128), np.float32)
    c[:, 0:128] = np.eye(128, dtype=np.float32)
    c[0, 128:256] = 1.0
    c[1, 256:288] = 1.0
    c[2, 288:320] = 1.0
    for r in range(4):
        c[:, 320 + r * 128:320 + (r + 1) * 128] = np.eye(128, dtype=np.float32)
        c[0:32, 832 + r * 32:832 + (r + 1) * 32] = np.eye(32, dtype=np.float32)
    return c


def _kc_layout(w):
    n = w.shape[1]
    return np.ascontiguousarray(w.reshape(8, 128, n).transpose(1, 0, 2)).reshape(128, 8 * n)


def _fmT(v, n):
    return np.ascontiguousarray(v.reshape(n, 128).T)


def make_in_maps(cfg, x_prompt, x_sample, cache_k, cache_v, cache_idx_k, state_conv, c_prompt, c_sample,
                 w_ada, b_ada, g_norm, w_in, w_a, w_dw, b_dw, g_ln, b_ln, w_b, w_out, g_final, n_cores):
    f = lambda a: np.ascontiguousarray(np.asarray(a, dtype=np.float32))
    w_in0 = f(w_in[0])
    wada = _kc_layout(f(w_ada[0])).reshape(128, 8, 3072)
    bada = np.ascontiguousarray(np.broadcast_to(f(b_ada[0])[None, :], (3, 3072)))
    gn_bc = np.ascontiguousarray(np.broadcast_to(f(g_norm[0])[None, :], (128, D)))
    gf_bc = np.ascontiguousarray(np.broadcast_to(f(g_final)[None, :], (128, D)))
    wfm = np.stack([_kc_layout(w_in0[:, FM_OFF[kd] + c * 128: FM_OFF[kd] + (c + 1) * 128]) for kd in FM_KINDS for c in range(8)])
    wiq = np.stack([_kc_layout(w_in0[:, O_IQ + h * 64: O_IQ + (h + 1) * 64]) for h in range(8)])
    wkv = _kc_layout(np.concatenate([w_in0[:, O_K:O_K + 512], w_in0[:, O_IK:O_IK + 64], w_in0[:, O_IW:O_IW + 8]], axis=1))
    wa, wb, wo = _kc_layout(f(w_a[0])), _kc_layout(f(w_b[0])), _kc_layout(f(w_out[0]))
    wdwT = np.ascontiguousarray(f(w_dw[0]).T.reshape(8, 128, 31).transpose(1, 0, 2)).reshape(128, 8 * 31)
    vecT = np.concatenate([_fmT(f(b_dw[0]), 8), _fmT(f(g_ln[0]), 8), _fmT(f(b_ln[0]), 8)], axis=1)
    consts = _consts()
    NS, TS = cfg.NS, cfg.TS
    maps = []
    for i in range(n_cores):
        ss = slice(i * NS, (i + 1) * NS)
        xs = f(x_sample[ss]).reshape(NS * TS, D)
        x_all = np.concatenate([f(x_prompt[i]), xs], axis=0)
        cc = np.concatenate([f(c_prompt[i])[None], f(c_sample[ss])], axis=0)
        cT = np.ascontiguousarray(cc.reshape(3, 8, 128).transpose(2, 1, 0))
        sconv = f(state_conv[0][ss])
        sconvT = np.ascontiguousarray(sconv.reshape(NS, 30, 8, 128).transpose(0, 3, 2, 1)).reshape(NS, 128, 240)
        maps.append({
            "x_all": x_all, "cT": cT, "wada": wada, "bada": bada, "gn_bc": gn_bc, "wfm": wfm, "wiq": wiq, "wkv": wkv,
            "wa": wa, "wb": wb, "wo": wo, "wdwT": wdwT, "vecT": vecT, "gf_bc": gf_bc,
            "ck": f(cache_k[0][ss]).reshape(NS, cfg.PAST, 256), "cv": f(cache_v[0][ss]).reshape(NS, cfg.PAST, 256),
            "cik": f(cache_idx_k[0][ss]), "sconvT": sconvT, "consts": consts,
        })
    return maps


def assemble(cfg, results, n_cores):
    TP, NS, TS = cfg.TP, cfg.NS, cfg.TS
    g = lambda name: [np.asarray(r[name]) for r in results]
    y, kk, vv, ik, cv = g("y_all"), g("k_all"), g("v_all"), g("ik_all"), g("conv_o")
    y_p = np.stack([a[:TP] for a in y])
    y_s = np.concatenate([a[TP:].reshape(NS, TS, D) for a in y])
    k_p = np.stack([a[:TP].reshape(TP, 2, 128) for a in kk])[None]
    v_p = np.stack([a[:TP].reshape(TP, 2, 128) for a in vv])[None]
    ik_p = np.stack([a[:TP] for a in ik])[None]
    c_p = np.stack([a[0] for a in cv])[None]
    k_s = np.concatenate([a[TP:].reshape(NS, TS, 2, 128) for a in kk])[None]
    v_s = np.concatenate([a[TP:].reshape(NS, TS, 2, 128) for a in vv])[None]
    ik_s = np.concatenate([a[TP:].reshape(NS, TS, 64) for a in ik])[None]
    c_s = np.concatenate([a[1:] for a in cv])[None]
    return tuple(np.ascontiguousarray(a, dtype=np.float32) for a in (y_p, y_s, k_p, v_p, ik_p, c_p, k_s, v_s, ik_s, c_s))


def kernel(**inputs):
    cfg = Cfg()
    n = 8
    nc, _ = build(cfg)
    maps = make_in_maps(cfg, n_cores=n, **inputs)
    res = run_bass_kernel_spmd(nc, maps, core_ids=list(range(n)))
    return assemble(cfg, res.results, n)
```

```python
import numpy as np
from contextlib import ExitStack
import concourse.bass as bass
import concourse.mybir as mybir
from concourse.bass_utils import run_bass_kernel_spmd

F32 = mybir.dt.float32
BF16 = mybir.dt.bfloat16
AF = mybir.ActivationFunctionType
ALU = mybir.AluOpType
AX = mybir.AxisListType

D = 1024
NIN = 8264
CHUNK = 64
NITER = 22
import os
ACT_FRAC = float(os.environ.get("ACT_FRAC", "0.0"))
NEG = -1.0e30
MASKV = -30000.0

O_Q, O_K, O_V, O_GA, O_IQ, O_IW, O_IK, O_GLU, O_GB, O_M = 0, 1024, 1280, 1536, 2560, 3072, 3080, 3144, 5192, 6216
FM_KINDS = ["q", "ga", "ua", "ub", "gb", "ma", "mb"]
FM_OFF = {"q": O_Q, "ga": O_GA, "ua": O_GLU, "ub": O_GLU + 1024, "gb": O_GB, "ma": O_M, "mb": O_M + 1024}


class Sem:
    def __init__(self, k, name):
        self.h = k.stack0.enter_context(k.nc.semaphore(name))
        self.v = 0


class Buf:
    def __init__(self, t, disjoint=False):
        self.t = t
        self.w = {}
        self.r = {}
        self.disjoint = disjoint

    def __getitem__(self, idx):
        return self.t[idx]


def _merge(d, tok):
    if tok is None:
        return
    s, v = tok
    if id(s) not in d or d[id(s)][1] < v:
        d[id(s)] = (s, v)


class Eng:
    def __init__(self, k, e, name, same=True):
        self.k = k
        self.e = e
        self.name = name
        self.sem = Sem(k, "e_" + name)
        self.seen = {}
        self.same = same
        self.pending = []

    def wait_tok(self, s, v):
        if s is self.sem and not self.same:
            return
        if self.seen.get(id(s), 0) < v:
            self.e.wait_ge(s.h, v)
            self.seen[id(s)] = v

    def wait_bufs(self, reads, writes):
        for b in reads:
            for s, v in b.w.values():
                self.wait_tok(s, v)
        for b in writes:
            if b.disjoint:
                continue
            for s, v in b.w.values():
                self.wait_tok(s, v)
            for s, v in b.r.values():
                self.wait_tok(s, v)


class K:
    def __init__(self, nc):
        self.nc = nc
        self.stack0 = ExitStack()
        self.pe = Eng(self, nc.tensor, "pe", same=False)
        self.act = Eng(self, nc.scalar, "act")
        self.dve = Eng(self, nc.vector, "dve")
        self.pool = Eng(self, nc.gpsimd, "pool")
        self.sp = Eng(self, nc.sync, "sp")
        self.engs = [self.pe, self.act, self.dve, self.pool, self.sp]
        self.dsems = [Sem(self, "d%d" % i) for i in range(24)]
        self.di = 0
        self.nins = 0

    def _post(self, E, tok, reads, writes):
        for b in writes:
            if not b.disjoint:
                b.w = {}
                b.r = {}
            _merge(b.w, tok)
        for b in reads:
            _merge(b.r, tok)

    def do(self, E, fn, reads, writes, *a, sig=True, **kw):
        E.wait_bufs(reads, writes)
        ins = getattr(E.e, fn)(*a, **kw)
        self.nins += 1
        if sig:
            E.sem.v += 1
            ins.then_inc(E.sem.h, 1)
            tok = (E.sem, E.sem.v)
            for (rr, ww) in E.pending:
                self._post(E, tok, rr, ww)
            E.pending = []
            self._post(E, tok, reads, writes)
            return tok
        E.pending.append((reads, writes))
        return None

    def dma(self, out, in_, reads, writes, E=None, **kw):
        E = E or self.sp
        E.wait_bufs(reads, writes)
        s = self.dsems[self.di % len(self.dsems)]
        self.di += 1
        if s.v > 0:
            E.wait_tok(s, s.v)
        s.v += 16
        E.e.dma_start(out=out, in_=in_, **kw).then_inc(s.h, 16)
        self.nins += 1
        tok = (s, s.v)
        for b in writes:
            _merge(b.w, tok)
        for b in reads:
            _merge(b.r, tok)
        return tok

    def barrier(self):
        sems = [e.sem for e in self.engs] + self.dsems
        for E in [self.pe, self.act, self.dve, self.pool, self.sp]:
            for s in sems:
                if s.v > 0 and s is not E.sem:
                    E.wait_tok(s, s.v)

    def final_wait(self):
        for s in self.dsems:
            if s.v > 0:
                self.sp.wait_tok(s, s.v)
        for e in self.engs:
            if e is not self.sp and e.sem.v > 0:
                self.sp.wait_tok(e.sem, e.sem.v)


class Cfg:
    def __init__(self, TP=8192, PAST=2048, TS=32, NS=2, topk_max=256, debug=False):
        self.TP, self.PAST, self.TS, self.NS = TP, PAST, TS, NS
        self.debug = debug
        self.NTOK = TP + NS * TS
        self.topk_p = min(topk_max, TP // 4)
        self.topk_s = min(topk_max, (PAST + TS) // 4)
        self.LS = PAST + TS
        self.LMAX = max(TP, ((self.LS + 127) // 128) * 128)
        self.NKT = self.LMAX // 128


def build(cfg):
    TP, NTOK, TS, NS, PAST = cfg.TP, cfg.NTOK, cfg.TS, cfg.NS, cfg.PAST
    SROWS = NS * TS
    nc = bass.Bass("TRN2", target_bir_lowering=False)
    k = K(nc)

    def din(name, shape, dt=F32):
        return nc.dram_tensor(name, list(shape), dt, kind="ExternalInput").ap()

    def dout(name, shape, dt=F32):
        return nc.dram_tensor(name, list(shape), dt, kind="ExternalOutput").ap()

    def dscr(name, shape, dt=BF16):
        return Buf(nc.dram_tensor(name, list(shape), dt, kind="Internal").ap(), disjoint=True)

    x_all = din("x_all", [NTOK, D])
    cT_d = din("cT", [128, 8, 3])
    wada_d = din("wada", [128, 8, 3072])
    bada_d = din("bada", [3, 3072])
    gn_d = din("gn_bc", [128, D])
    wfm_d = din("wfm", [56, 128, 8 * 128])
    wiq_d = din("wiq", [8, 128, 8 * 64])
    wkv_d = din("wkv", [128, 8 * 584])
    wa_d = din("wa", [128, 8 * D])
    wb_d = din("wb", [128, 8 * D])
    wo_d = din("wo", [128, 8 * D])
    wdw_d = din("wdwT", [128, 8 * 31])
    vec_d = din("vecT", [128, 3 * 8])
    gf_d = din("gf_bc", [128, D])
    ck_d = din("ck", [NS, PAST, 256])
    cv_d = din("cv", [NS, PAST, 256])
    cik_d = din("cik", [NS, PAST, 64])
    sconv_d = din("sconvT", [NS, 128, 8 * 30])
    const_d = din("consts", [128, 128 + 192 + 512 + 128])

    y_all = Buf(disjoint=True, t=dout("y_all", [NTOK, D]))
    k_all = Buf(disjoint=True, t=dout("k_all", [NTOK, 256]))
    v_all = Buf(disjoint=True, t=dout("v_all", [NTOK, 256]))
    ik_all = Buf(disjoint=True, t=dout("ik_all", [NTOK, 64]))
    conv_o = Buf(disjoint=True, t=dout("conv_o", [1 + NS, 30, D]))

    S = {kind: dscr("s_" + kind, [8, 128, NTOK]) for kind in FM_KINDS}
    if cfg.debug:
        S["og"] = Buf(nc.dram_tensor("s_og", [8, 128, NTOK], BF16, kind="ExternalOutput").ap(), disjoint=True)
    else:
        S["og"] = dscr("s_og", [8, 128, NTOK])
    S_iq = dscr("s_iq", [8, 64, NTOK])
    S_iw = dscr("s_iw", [NTOK, 8], F32)

    def fm_view(b, t0, n):
        return b.t.rearrange("h d t -> d h t")[:, :, t0:t0 + n]

    P0 = ExitStack()

    uniq = [0]

    def sb(st, name, shape, dt):
        uniq[0] += 1
        return Buf(st.enter_context(nc.sbuf_tensor("%s_u%d" % (name, uniq[0]), list(shape), dt)))

    def ps(st, name, shape, dt):
        uniq[0] += 1
        return Buf(st.enter_context(nc.psum_tensor("%s_u%d" % (name, uniq[0]), list(shape), dt)))

    identf = sb(P0, "identf", [128, 128], F32)
    identb = sb(P0, "identb", [128, 128], BF16)
    G_p = Buf(nc.dram_tensor("g_p", [128, D], F32, kind="Internal").ap(), disjoint=True)
    G_s = [Buf(nc.dram_tensor("g_s%d" % i, [TS, D], F32, kind="Internal").ap(), disjoint=True) for i in range(NS)]
    epsT = sb(P0, "epsT", [128, 1], F32)
    onesb = sb(P0, "onesb", [128, 128], BF16)
    I4p = sb(P0, "I4p", [128, 512], BF16)
    I4s = sb(P0, "I4s", [32, 128], BF16)
    PC = ExitStack()
    cst = sb(PC, "cst", [128, 128 + 192 + 512 + 128], F32)

    k.dma(cst[:], const_d, [], [cst])
    k.do(k.dve, "tensor_copy", [cst], [identf], out=identf[:], in_=cst[:, 0:128])
    k.do(k.dve, "tensor_copy", [cst], [identb], out=identb[:], in_=cst[:, 0:128])
    k.do(k.dve, "memset", [], [epsT], epsT[:], 1e-6)
    k.do(k.dve, "memset", [], [onesb], onesb[:], 1.0)
    k.do(k.dve, "tensor_copy", [cst], [I4p], out=I4p[:], in_=cst[:, 320:832])
    k.do(k.dve, "tensor_copy", [cst], [I4s], out=I4s[:], in_=cst[0:32, 832:960])
    sel = lambda a, b_: cst[0:3, 128 + a:128 + b_]

    P1 = ExitStack()
    hT = sb(P1, "hT", [128, 8, NTOK], BF16)
    P1a = ExitStack()
    AB_p = sb(P1a, "AB_p", [128, 2 * D], F32)
    AB_s = sb(P1a, "AB_s", [SROWS, 2 * D], F32)
    with ExitStack() as st:
        cT = sb(st, "cT_t", [128, 8, 3], F32)
        sc = sb(st, "sc_t", [128, 8, 3], F32)
        modrows = sb(st, "modrows", [3, 3072], F32)
        gn = sb(st, "gn", [128, D], F32)
        wat = [sb(st, "wat%d" % i, [128, 8, 256], F32) for i in range(2)]
        pm = ps(st, "pm", [128, 512], F32)
        pbp = [ps(st, "pbp%d" % i, [128, 512], F32) for i in range(2)]
        k.dma(cT[:], cT_d, [], [cT])
        k.dma(modrows[:], bada_d, [], [modrows])
        k.dma(gn[:], gn_d, [], [gn])
        k.do(k.act, "activation", [cT], [sc], out=sc[:], in_=cT[:], func=AF.Silu)
        wview = wada_d
        for i in range(12):
            w = wat[i % 2]
            k.dma(w[:], wview[:, :, i * 256:(i + 1) * 256], [], [w])
            for kc in range(8):
                k.do(k.pe, "matmul", [sc, w], [pm], pm[0:3, 0:256], sc[:, kc, :], w[:, kc, :], start=(kc == 0), stop=(kc == 7), sig=(kc == 7))
            k.do(k.dve, "tensor_tensor", [pm, modrows], [modrows], out=modrows[:, i * 256:(i + 1) * 256], in0=pm[0:3, 0:256], in1=modrows[:, i * 256:(i + 1) * 256], op=ALU.add)
        gstg = [sb(st, "gstg%d" % i, [128, 512], F32) for i in range(2)]
        targets = [((0, 128), 128, AB_p, G_p), ((128, 128 + SROWS), SROWS, AB_s, None)]
        for si in range(NS):
            targets.append(((128 + si * 32, 128 + si * 32 + TS), TS, None, G_s[si]))
        bi = 0
        for (sc0, sc1), M, AB, gt in targets:
            for part in range(3):
                if part < 2 and AB is None:
                    continue
                if part == 2 and gt is None:
                    continue
                for half in range(2):
                    pb = pbp[bi % 2]
                    bi += 1
                    k.do(k.pe, "matmul", [cst, modrows], [pb], pb[0:M, :], sel(sc0, sc1), modrows[:, part * 1024 + half * 512: part * 1024 + half * 512 + 512], start=True, stop=True)
                    cs = slice(half * 512, half * 512 + 512)
                    if part == 1:
                        k.do(k.dve, "scalar_tensor_tensor", [pb, gn], [AB], out=AB[0:M, cs], in0=pb[0:M, :], scalar=1.0, in1=gn[0:M, cs], op0=ALU.add, op1=ALU.mult)
                    elif part == 0:
                        k.do(k.act, "copy", [pb], [AB], out=AB[0:M, D + half * 512: D + half * 512 + 512], in_=pb[0:M, :])
                    else:
                        g_ = gstg[bi % 2]
                        k.do(k.act, "copy", [pb], [g_], out=g_[0:M, :], in_=pb[0:M, :])
                        k.dma(gt.t[0:M, cs], g_[0:M, :], [g_], [gt])
        k.barrier()

    NT = (NTOK + 127) // 128
    with ExitStack() as st:
        xb = [sb(st, "xb%d" % i, [128, D], F32) for i in range(3)]
        h1 = [sb(st, "h1_%d" % i, [128, D], F32) for i in range(2)]
        hb = [sb(st, "hb%d" % i, [128, D], BF16) for i in range(2)]
        junk = sb(st, "junk1", [128, D], F32)
        stats = [sb(st, "stats%d" % i, [128, 4], F32) for i in range(3)]
        pT = [ps(st, "pT%d" % i, [128, 8, 128], BF16) for i in range(2)]
        def stage1(tt):
            t0 = tt * 128
            rows = min(128, NTOK - t0)
            x = xb[tt % 3]
            s_ = stats[tt % 3]
            k.dma(x[0:rows, :], x_all[t0:t0 + rows, :], [], [x])
            k.do(k.act, "activation", [x], [junk, s_], out=junk[0:rows, :], in_=x[0:rows, :], func=AF.Square, accum_out=s_[0:rows, 0:1])
            k.do(k.act, "activation", [s_, epsT], [s_], out=s_[0:rows, 1:2], in_=s_[0:rows, 0:1], func=AF.Sqrt, scale=1.0 / D, bias=epsT[0:rows, 0:1])
        stage1(0)
        for tt in range(NT):
            t0 = tt * 128
            rows = min(128, NTOK - t0)
            AB = AB_p if t0 < TP else AB_s
            x = xb[tt % 3]
            s_ = stats[tt % 3]
            if tt + 1 < NT:
                stage1(tt + 1)
            k.do(k.dve, "reciprocal", [s_], [s_], out=s_[0:rows, 2:3], in_=s_[0:rows, 1:2])
            h_ = h1[tt % 2]
            k.do(k.dve, "scalar_tensor_tensor", [x, s_, AB], [h_], out=h_[0:rows, :], in0=x[0:rows, :], scalar=s_[0:rows, 2:3], in1=AB[0:rows, 0:D], op0=ALU.mult, op1=ALU.mult)
            hb_ = hb[tt % 2]
            k.do(k.dve, "tensor_tensor", [h_, AB], [hb_], out=hb_[0:rows, :], in0=h_[0:rows, :], in1=AB[0:rows, D:2 * D], op=ALU.add)
            p = pT[tt % 2]
            for c in range(8):
                k.do(k.pe, "transpose", [hb_, identb], [p], p[:, c, 0:rows], hb_[0:rows, c * 128:(c + 1) * 128], identb[0:rows, 0:rows], sig=(c == 7))
            k.do(k.act, "copy", [p], [hT], out=hT[:, :, t0:t0 + rows], in_=p[:, :, 0:rows])
        k.barrier()
    P1a.close()

    groups = [(g0, min(512, NTOK - g0)) for g0 in range(0, NTOK, 512)]
    with ExitStack() as st:
        wkvb = sb(st, "wkvb", [128, 8, 584], BF16)
        kvst = [sb(st, "kvst%d" % i, [128, 584], F32) for i in range(2)]
        wst = [sb(st, "wst%d" % i, [128, 8 * 128], F32) for i in range(2)]
        wbf = [sb(st, "wbf%d" % i, [128, 8, 128], BF16) for i in range(4)]
        stg = [sb(st, "stg%d" % i, [128, 512], BF16) for i in range(3)]
        sig_ = [sb(st, "sig%d" % i, [128, 512], F32) for i in range(2)]
        cstg = [sb(st, "cstg%d" % i, [64, 256], F32) for i in range(2)]
        pz = [ps(st, "pz%d" % i, [128, 512], F32) for i in range(4)]
        pk0 = ps(st, "pk0", [128, 512], F32)
        pk1 = ps(st, "pk1", [128, 512], F32)
        pc = ps(st, "pcv", [128, 512], F32)
        for kc in range(8):
            b_ = kvst[kc % 2]
            k.dma(b_[:], wkv_d[:, kc * 584:(kc + 1) * 584], [], [b_])
            k.do(k.pool, "tensor_copy", [b_], [wkvb], out=wkvb[:, kc, :], in_=b_[:])
        for tt in range(NT):
            t0 = tt * 128
            rows = min(128, NTOK - t0)
            for kc in range(8):
                k.do(k.pe, "matmul", [hT, wkvb], [pk0], pk0[0:rows, :], hT[:, kc, t0:t0 + rows], wkvb[:, kc, 0:512], start=(kc == 0), stop=(kc == 7), sig=False)
            for kc in range(8):
                k.do(k.pe, "matmul", [hT, wkvb], [pk1], pk1[0:rows, 0:72], hT[:, kc, t0:t0 + rows], wkvb[:, kc, 512:584], start=(kc == 0), stop=(kc == 7), sig=(kc == 7))
            o_ = kvst[tt % 2]
            k.do(k.act, "copy", [pk0], [o_], out=o_[0:rows, 0:512], in_=pk0[0:rows, :])
            k.do(k.dve, "tensor_copy", [pk1], [o_], out=o_[0:rows, 512:584], in_=pk1[0:rows, 0:72])
            k.dma(k_all.t[t0:t0 + rows, :], o_[0:rows, 0:256], [o_], [k_all])
            k.dma(v_all.t[t0:t0 + rows, :], o_[0:rows, 256:512], [o_], [v_all])
            k.dma(ik_all.t[t0:t0 + rows, :], o_[0:rows, 512:576], [o_], [ik_all])
            k.dma(S_iw.t[t0:t0 + rows, :], o_[0:rows, 576:584], [o_], [S_iw])

        wi = [0]
        zi = [0]
        si_ = [0]

        def load_w(src_ap, width):
            i = wi[0]
            wi[0] += 1
            ws = wst[i % 2]
            wb_ = wbf[i % 4]
            k.dma(ws[:, 0:8 * width], src_ap, [], [ws])
            k.do(k.pool, "tensor_copy", [ws], [wb_], out=wb_[:, :, 0:width], in_=ws[:, 0:8 * width].rearrange("p (a b) -> p a b", b=width))
            return wb_

        order = [("q", c) for c in range(8)] + [("iq", h) for h in range(8)]
        for c in range(8):
            order += [("ua", c), ("ub", c)]
        for kd in ("ga", "gb", "ma", "mb"):
            order += [(kd, c) for c in range(8)]
        pos = {key: i for i, key in enumerate(order)}
        pre = {}

        def src_of(key):
            kd, c = key
            return (wiq_d[c], 64) if kd == "iq" else (wfm_d[FM_KINDS.index(kd) * 8 + c], 128)

        def want(key):
            if key in pre:
                return pre.pop(key)
            return load_w(*src_of(key))

        def prefetch_after(key, n=1):
            i = pos[key]
            for j in range(i + 1, min(i + 1 + n, len(order))):
                if order[j] not in pre:
                    pre[order[j]] = load_w(*src_of(order[j]))

        def proj(wb_, width, g0, n):
            p = pz[zi[0] % 4]
            zi[0] += 1
            for kc in range(8):
                k.do(k.pe, "matmul", [hT, wb_], [p], p[0:width, 0:n], wb_[:, kc, 0:width], hT[:, kc, g0:g0 + n], start=(kc == 0), stop=(kc == 7), sig=(kc == 7))
            return p

        def single(kind, c, func, scale=1.0):
            wb_ = want((kind, c))
            prefetch_after((kind, c), 1)
            for (g0, n) in groups:
                p = proj(wb_, 128, g0, n)
                o_ = stg[si_[0] % 3]
                si_[0] += 1
                k.do(k.act, "activation", [p], [o_], out=o_[:, 0:n], in_=p[:, 0:n], func=func, scale=scale)
                k.dma(S[kind].t[c, :, g0:g0 + n], o_[:, 0:n], [o_], [S[kind]])

        for c in range(8):
            single("q", c, AF.Copy, scale=128.0 ** -0.5)
        for h in range(8):
            wb_ = want(("iq", h))
            prefetch_after(("iq", h), 1)
            for (g0, n) in groups:
                p = proj(wb_, 64, g0, n)
                o_ = stg[si_[0] % 3]
                si_[0] += 1
                k.do(k.dve, "tensor_copy", [p], [o_], out=o_[0:64, 0:n], in_=p[0:64, 0:n])
                k.dma(S_iq.t[h, :, g0:g0 + n], o_[0:64, 0:n], [o_], [S_iq])
        crow_sets = [(TP - 32, 32, [(0, 2, 32)]), (TP, SROWS, [(1 + s, TS * s + 2, TS * s + TS) for s in range(NS)])]
        ci = 0
        for c in range(8):
            wA = want(("ua", c))
            wB = want(("ub", c))
            prefetch_after(("ub", c), 2)
            for (g0, n) in groups:
                pA = proj(wA, 128, g0, n)
                pB = proj(wB, 128, g0, n)
                sg = sig_[si_[0] % 2]
                o_ = stg[si_[0] % 3]
                si_[0] += 1
                k.do(k.act, "activation", [pB], [sg], out=sg[:, 0:n], in_=pB[:, 0:n], func=AF.Sigmoid)
                k.do(k.dve, "tensor_tensor", [pA, sg], [o_], out=o_[:, 0:n], in0=pA[:, 0:n], in1=sg[:, 0:n], op=ALU.mult)
                k.dma(S["ua"].t[c, :, g0:g0 + n], o_[:, 0:n], [o_], [S["ua"]])
            for (r0, M, outs) in crow_sets:
                for kc in range(8):
                    k.do(k.pe, "matmul", [hT, wA], [pc], pc[0:M, 0:128], hT[:, kc, r0:r0 + M], wA[:, kc, :], start=(kc == 0), stop=(kc == 7), sig=False)
                for kc in range(8):
                    k.do(k.pe, "matmul", [hT, wB], [pc], pc[0:M, 128:256], hT[:, kc, r0:r0 + M], wB[:, kc, :], start=(kc == 0), stop=(kc == 7), sig=(kc == 7))
                cs_ = cstg[ci % 2]
                ci += 1
                k.do(k.act, "activation", [pc], [cs_], out=cs_[0:M, 128:256], in_=pc[0:M, 128:256], func=AF.Sigmoid)
                k.do(k.dve, "tensor_tensor", [pc, cs_], [cs_], out=cs_[0:M, 0:128], in0=pc[0:M, 0:128], in1=cs_[0:M, 128:256], op=ALU.mult)
                for (job, ra, rb) in outs:
                    k.dma(conv_o.t[job, :, c * 128:(c + 1) * 128], cs_[ra:rb, 0:128], [cs_], [conv_o])
        for c in range(8):
            single("ga", c, AF.Silu)
        for c in range(8):
            single("gb", c, AF.Silu)
        for c in range(8):
            single("ma", c, AF.Sigmoid)
        for c in range(8):
            single("mb", c, AF.Sigmoid)
        k.barrier()
    P1.close()
    PC.close()

    LMAX, NKT = cfg.LMAX, cfg.NKT
    with ExitStack() as st:
        Vc = sb(st, "Vc", [128, NKT, 256], BF16)
        kT = sb(st, "kT", [128, 2, LMAX], BF16)
        ikT = sb(st, "ikT", [64, LMAX], BF16)
        score = [sb(st, "score%d" % i, [128, LMAX], F32) for i in range(2)]
        mbias = [sb(st, "mbias%d" % i, [128, LMAX], BF16) for i in range(2)]
        pw = sb(st, "pw", [128, NITER + 1], F32)
        for i in range(NITER + 1):
            k.do(k.dve, "memset", [], [pw], pw[:, i:i + 1], 2.0 ** -(i + 1))
        psD = psSc = psS = psO = psN = psT = psI = None
        W = {}

        def build_cache(srcs):
            kst, vst, ist, kbb, ibb = W["kst"], W["vst"], W["ist"], W["kbb"], W["ibb"]
            kpos = 0
            bi = 0
            for (kap, vap, iap, rows, nt, deps) in srcs:
                a = bi % 2
                bi += 1
                ks_, vs_, is_, kb_, ib_ = kst[a], vst[a], ist[a], kbb[a], ibb[a]
                if nt > 1 or rows == 128:
                    k.dma(ks_[:, 0:nt, :], kap.rearrange("(a p) c -> p a c", p=128), deps, [ks_])
                    k.dma(vs_[:, 0:nt, :], vap.rearrange("(a p) c -> p a c", p=128), deps, [vs_])
                    k.dma(is_[:, 0:nt, :], iap.rearrange("(a p) c -> p a c", p=128), deps, [is_])
                else:
                    k.dma(ks_[0:rows, 0, :], kap, deps, [ks_])
                    k.dma(vs_[0:rows, 0, :], vap, deps, [vs_])
                    k.dma(is_[0:rows, 0, :], iap, deps, [is_])
                kt0 = kpos // 128
                k.do(k.pool, "tensor_copy", [ks_], [kb_], out=kb_[0:rows, 0:nt, :], in_=ks_[0:rows, 0:nt, :])
                k.do(k.dve, "tensor_copy", [vs_], [Vc], out=Vc[0:rows, kt0:kt0 + nt, :], in_=vs_[0:rows, 0:nt, :])
                k.do(k.pool, "tensor_copy", [is_], [ib_], out=ib_[0:rows, 0:nt, :], in_=is_[0:rows, 0:nt, :])
                for a_ in range(nt):
                    for g in range(2):
                        k.do(k.pe, "transpose", [kb_, identb], [psT], psT[:, g, a_ * 128:a_ * 128 + rows], kb_[0:rows, a_, g * 128:(g + 1) * 128], identb[0:rows, 0:rows], sig=False)
                    k.do(k.pe, "transpose", [ib_, identb], [psI], psI[0:64, a_ * 128:a_ * 128 + rows], ib_[0:rows, a_, :], identb[0:rows, 0:rows], sig=(a_ == nt - 1))
                n = (nt - 1) * 128 + rows
                k.do(k.act, "copy", [psT], [kT], out=kT[:, :, kpos:kpos + n], in_=psT[:, :, 0:n])
                k.do(k.dve, "tensor_copy", [psI], [ikT], out=ikT[:, kpos:kpos + n], in_=psI[0:64, 0:n])
                kpos += n
            return kpos

        ctr = {"d": 0, "r": 0, "s": 0, "p": 0}

        class QB_:
            def __init__(self, bi, tok0, QB, LK, topk, diag, I4):
                self.bi, self.tok0, self.QB, self.LK, self.topk, self.diag, self.I4 = bi, tok0, QB, LK, topk, diag, I4
                self.q_, self.ga_, self.og_ = W["qT"][bi % 2], W["gaT"][bi % 2], W["ogT"][0]
                self.iq_, self.iw_, self.wd_ = W["iqT"][bi % 2], W["iwt"][bi % 2], W["Wd"][bi % 2]
                self.sc, self.mb = score[bi % 2], mbias[bi % 2]
                self.bs, self.Hh, self.LO, self.mid = W["bs"][bi % 2], W["Hh"][bi % 2], W["LO"][bi % 2], W["mid"][bi % 2]
                self.jD, self.jA, self.SA, self.TA = W["jD"][bi % 2], W["jA"][bi % 2], W["SA"][bi % 2], W["TA"][bi % 2]

            def early_loads(self):
                QB, tok0 = self.QB, self.tok0
                k.dma(self.iq_[:, :, 0:QB], S_iq.t.rearrange("h d t -> d h t")[:, :, tok0:tok0 + QB], [S_iq], [self.iq_])
                k.dma(self.iw_[0:QB, :], S_iw.t[tok0:tok0 + QB, :], [S_iw], [self.iw_])
                for h in range(8):
                    k.do(k.pool, "tensor_scalar", [identf, self.iw_], [self.wd_], out=self.wd_[0:QB, h, 0:QB], in0=identf[0:QB, 0:QB], scalar1=self.iw_[0:QB, h:h + 1], scalar2=None, op0=ALU.mult, sig=(h == 7))

            def late_loads(self):
                QB, tok0 = self.QB, self.tok0
                k.dma(self.q_[:, :, 0:QB], fm_view(S["q"], tok0, QB), [S["q"]], [self.q_])
                k.dma(self.ga_[:, :, 0:QB], fm_view(S["ga"], tok0, QB), [S["ga"]], [self.ga_])

            def gen_I(self):
                QB, LK, iq_, wd_, sc = self.QB, self.LK, self.iq_, self.wd_, self.sc
                Rb = W["Rb"]
                nkt = (LK + 511) // 512
                for kt in range(nkt):
                    k0 = kt * 512
                    wk = min(512, LK - k0)

                    def dmm(h):
                        p = psD[ctr["d"] % 3]
                        ctr["d"] += 1
                        k.do(k.pe, "matmul", [iq_, ikT], [p], p[0:QB, 0:wk], iq_[:, h, 0:QB], ikT[:, k0:k0 + wk], start=True, stop=True)
                        return p
                    pq = [dmm(0), dmm(1)]
                    rq = []
                    for h in range(8):
                        p = pq.pop(0)
                        r_ = Rb[ctr["r"] % 3]
                        ctr["r"] += 1
                        k.do(k.act, "activation", [p], [r_], out=r_[0:QB, 0:wk], in_=p[0:QB, 0:wk], func=AF.Relu)
                        if h + 2 < 8:
                            pq.append(dmm(h + 2))
                        k.do(k.pe, "matmul", [wd_, r_], [psSc], psSc[0:QB, 0:wk], wd_[0:QB, h, 0:QB], r_[0:QB, 0:wk], start=(h == 0), stop=(h == 7), sig=(h == 7))
                        if h < 7:
                            yield
                    k.do(k.act, "copy", [psSc], [sc], out=sc[0:QB, k0:k0 + wk], in_=psSc[0:QB, 0:wk])
                    yield

            def gen_T(self):
                QB, LK, topk, sc, mb = self.QB, self.LK, self.topk, self.sc, self.mb
                bs, Hh, LO, mid = self.bs, self.Hh, self.LO, self.mid
                if LK > topk:
                    k.do(k.dve, "tensor_reduce", [sc], [bs], out=bs[0:QB, 0:1], in_=sc[0:QB, 0:LK], axis=AX.X, op=ALU.min)
                    k.do(k.dve, "tensor_reduce", [sc], [bs], out=bs[0:QB, 1:2], in_=sc[0:QB, 0:LK], axis=AX.X, op=ALU.max)
                    yield
                if self.diag:
                    k.do(k.dve, "memset", [], [sc], sc[0:64, LK - 64:LK], NEG)
                if LK > topk:
                    jD, jA, SA, MID, U, TA = self.jD, self.jA, self.SA, self.mid, self.LO, self.TA
                    k.do(k.dve, "tensor_tensor", [bs], [bs], out=bs[0:QB, 2:3], in0=bs[0:QB, 1:2], in1=bs[0:QB, 0:1], op=ALU.subtract)
                    k.do(k.dve, "tensor_scalar", [pw, bs], [Hh], out=Hh[0:QB, :], in0=pw[0:QB, :], scalar1=bs[0:QB, 2:3], scalar2=None, op0=ALU.mult)
                    k.do(k.dve, "tensor_tensor", [bs, Hh], [MID], out=MID[0:QB, 0:1], in0=bs[0:QB, 0:1], in1=Hh[0:QB, 0:1], op=ALU.add)
                    j0 = jD[0:QB, 0:1]
                    k1 = LK
                    if LK >= 1024:
                        k1 = max(128, int(round(LK * (1.0 - ACT_FRAC) / 128.0)) * 128)
                    nA = LK - k1
                    jb = bass.AP(j0.tensor, j0.offset, [list(j0.ap[0]), [0, k1]])
                    thrc = float(topk) - 0.5 * nA
                    if nA > 0:
                        a0 = jA[0:QB, 0:1]
                        jbA = bass.AP(a0.tensor, a0.offset, [list(a0.ap[0]), [0, nA]])
                        k.do(k.dve, "memset", [], [SA], SA[0:QB, :], 0.0)
                        k.do(k.dve, "memset", [], [TA], TA[0:QB, NITER:NITER + 1], thrc)
                    for it in range(NITER):
                        if nA > 0:
                            k.do(k.act, "activation", [sc, MID], [jA, SA], out=jbA, in_=sc[0:QB, k1:LK], func=AF.Sign, scale=-1.0, bias=MID[0:QB, it:it + 1], accum_out=SA[0:QB, it:it + 1])
                            k.do(k.act, "activation", [SA, TA], [TA], out=TA[0:QB, it:it + 1], in_=SA[0:QB, it:it + 1], func=AF.Identity, scale=0.5, bias=TA[0:QB, NITER:NITER + 1])
                        k.do(k.dve, "tensor_scalar", [sc, MID], [jD, bs], out=jb, in0=sc[0:QB, 0:k1], scalar1=MID[0:QB, it:it + 1], scalar2=None, op0=ALU.is_ge, op1=ALU.add, accum_out=bs[0:QB, 3:4])
                        if nA > 0:
                            k.do(k.dve, "tensor_scalar", [bs, TA, Hh], [U], out=U[0:QB, it:it + 1], in0=bs[0:QB, 3:4], scalar1=TA[0:QB, it:it + 1], scalar2=Hh[0:QB, it:it + 1], op0=ALU.is_ge, op1=ALU.mult)
                        else:
                            k.do(k.dve, "tensor_scalar", [bs, Hh], [U], out=U[0:QB, it:it + 1], in0=bs[0:QB, 3:4], scalar1=thrc, scalar2=Hh[0:QB, it:it + 1], op0=ALU.is_ge, op1=ALU.mult)
                        k.do(k.dve, "tensor_scalar", [U, MID, Hh], [MID], out=MID[0:QB, it + 1:it + 2], in0=U[0:QB, it:it + 1], scalar1=MID[0:QB, it:it + 1], scalar2=Hh[0:QB, it + 1:it + 2], op0=ALU.add, op1=ALU.subtract)
                        yield
                    k.do(k.dve, "tensor_tensor", [MID, Hh], [U], out=U[0:QB, NITER:NITER + 1], in0=MID[0:QB, NITER:NITER + 1], in1=Hh[0:QB, NITER:NITER + 1], op=ALU.subtract)
                    thr = U[0:QB, NITER:NITER + 1]
                    k.do(k.dve, "tensor_scalar", [sc, U], [mb], out=mb[0:QB, 0:LK], in0=sc[0:QB, 0:LK], scalar1=thr, scalar2=MASKV, op0=ALU.is_lt, op1=ALU.mult)
                else:
                    k.do(k.dve, "tensor_scalar", [sc], [mb], out=mb[0:QB, 0:LK], in0=sc[0:QB, 0:LK], scalar1=-1.0e29, scalar2=MASKV, op0=ALU.is_lt, op1=ALU.mult)
                yield

            def gen_A(self):
                QB, LK, q_, ga_, og_, mb, I4 = self.QB, self.LK, self.q_, self.ga_, self.og_, self.mb, self.I4
                PTb, rden = W["PTb"], W["rden"]
                N4 = 4 * QB
                nj = (LK + 127) // 128
                for g in range(2):
                    def qk(j):
                        wk = min(128, LK - j * 128)
                        p = psS[ctr["s"] % 2]
                        ctr["s"] += 1
                        k.do(k.pe, "matmul", [kT, q_], [p], p[0:wk, 0:N4], kT[:, g, j * 128:j * 128 + wk], q_[:, 4 * g:4 * g + 4, 0:QB], start=True, stop=False, sig=False)
                        k.do(k.pe, "matmul", [mb, I4], [p], p[0:wk, 0:N4], mb[0:QB, j * 128:j * 128 + wk], I4[0:QB, 0:N4], start=False, stop=True)
                        return p, wk
                    pend = qk(0)
                    for j in range(nj):
                        p, wk = pend
                        if j + 1 < nj:
                            pend = qk(j + 1)
                        pt = PTb[ctr["p"] % 3]
                        ctr["p"] += 1
                        k.do(k.act, "activation", [p], [pt], out=pt[0:wk, 0:N4], in_=p[0:wk, 0:N4], func=AF.Exp)
                        k.do(k.pe, "matmul", [Vc, pt], [psO], psO[:, 0:N4], Vc[0:wk, j, g * 128:(g + 1) * 128], pt[0:wk, 0:N4], start=(j == 0), stop=(j == nj - 1), sig=False)
                        k.do(k.pe, "matmul", [onesb, pt], [psN], psN[:, 0:N4], onesb[0:wk, :], pt[0:wk, 0:N4], start=(j == 0), stop=(j == nj - 1), sig=True)
                        yield
                    k.do(k.dve, "reciprocal", [psN], [rden], out=rden[:, 0:N4], in_=psN[:, 0:N4])
                    k.do(k.dve, "tensor_tensor", [psO, rden], [rden], out=rden[:, 0:N4], in0=psO[:, 0:N4], in1=rden[:, 0:N4], op=ALU.mult)
                    k.do(k.pool, "tensor_tensor", [rden, ga_], [og_], out=og_[:, 4 * g:4 * g + 4, 0:QB], in0=rden[:, 0:N4].rearrange("p (a b) -> p a b", b=QB), in1=ga_[:, 4 * g:4 * g + 4, 0:QB], op=ALU.mult)
                    yield
                k.dma(fm_view(S["og"], self.tok0, QB), og_[:, :, 0:QB], [og_], [S["og"]], E=k.pool)

        def run(gen):
            for _ in gen:
                pass

        def interleave(gens):
            items = []
            for g in gens:
                steps = []
                items.append((g, steps))
            live = [g for g in gens]
            while live:
                for g in list(live):
                    try:
                        next(g)
                    except StopIteration:
                        live.remove(g)

        def interleave_n(gl):
            st_ = [[g, n, 0, False] for (g, n) in gl]
            while True:
                live = [e for e in st_ if not e[3]]
                if not live:
                    break
                e = min(live, key=lambda e: e[2] / max(e[1], 1))
                try:
                    next(e[0])
                    e[2] += 1
                except StopIteration:
                    e[3] = True

        def interleave_ratio(gA, nA, gT, nT):
            ia = it = 0
            doneA = doneT = False
            while not (doneA and doneT):
                fa = ia / max(nA, 1)
                ft = it / max(nT, 1)
                pick_T = (not doneT) and (doneA or ft <= fa)
                if pick_T:
                    try:
                        next(gT)
                        it += 1
                    except StopIteration:
                        doneT = True
                else:
                    try:
                        next(gA)
                        ia += 1
                    except StopIteration:
                        doneA = True

        jobs = []
        srcs = []
        for t0 in range(0, TP, 256):
            nt = min(2, (TP - t0) // 128)
            srcs.append((k_all.t[t0:t0 + nt * 128, :], v_all.t[t0:t0 + nt * 128, :], ik_all.t[t0:t0 + nt * 128, :], 128, nt, [k_all, v_all, ik_all]))
        jobs.append(("p", srcs))
        for s in range(NS):
            srcs = []
            for t0 in range(0, PAST, 256):
                nt = min(2, (PAST - t0) // 128)
                srcs.append((ck_d[s, t0:t0 + nt * 128, :], cv_d[s, t0:t0 + nt * 128, :], cik_d[s, t0:t0 + nt * 128, :], 128, nt, []))
            a0 = TP + s * TS
            srcs.append((k_all.t[a0:a0 + TS, :], v_all.t[a0:a0 + TS, :], ik_all.t[a0:a0 + TS, :], TS, 1, [k_all, v_all, ik_all]))
            jobs.append(("s%d" % s, srcs))
        gbi = 0
        for ji, (jn, srcs) in enumerate(jobs):
            with ExitStack() as st2:
                psT = ps(st2, "psT_" + jn, [128, 2, 512], BF16)
                psI = ps(st2, "psI_" + jn, [128, 512], BF16)
                W["kst"] = [sb(st2, "kst%d" % i, [128, 2, 256], F32) for i in range(2)]
                W["vst"] = [sb(st2, "vst%d" % i, [128, 2, 256], F32) for i in range(2)]
                W["ist"] = [sb(st2, "ist%d" % i, [128, 2, 64], F32) for i in range(2)]
                W["kbb"] = [sb(st2, "kbb%d" % i, [128, 2, 256], BF16) for i in range(2)]
                W["ibb"] = [sb(st2, "ibb%d" % i, [128, 2, 64], BF16) for i in range(2)]
                build_cache(srcs)
                k.barrier()
            with ExitStack() as st2:
                psD = [ps(st2, "psD%d_%s" % (i, jn), [128, 512], F32) for i in range(3)]
                psSc = ps(st2, "psSc_" + jn, [128, 512], F32)
                psS = [ps(st2, "psS%d_%s" % (i, jn), [128, 512], F32) for i in range(2)]
                psO = ps(st2, "psO_" + jn, [128, 512], F32)
                psN = ps(st2, "psN_" + jn, [128, 512], F32)
                W["qT"] = [sb(st2, "qT%d" % i, [128, 8, 128], BF16) for i in range(2)]
                W["gaT"] = [sb(st2, "gaT%d" % i, [128, 8, 128], BF16) for i in range(2)]
                W["ogT"] = [sb(st2, "ogT%d" % i, [128, 8, 128], BF16) for i in range(1)]
                W["iqT"] = [sb(st2, "iqT%d" % i, [64, 8, 128], BF16) for i in range(2)]
                W["iwt"] = [sb(st2, "iwt%d" % i, [128, 8], F32) for i in range(2)]
                W["Wd"] = [sb(st2, "Wd%d" % i, [128, 8, 128], BF16) for i in range(2)]
                W["Rb"] = [sb(st2, "Rb%d" % i, [128, 512], BF16) for i in range(3)]
                W["PTb"] = [sb(st2, "PTb%d" % i, [128, 512], BF16) for i in range(3)]
                W["rden"] = sb(st2, "rden", [128, 512], F32)
                W["bs"] = [sb(st2, "bs%d" % i, [128, 8], F32) for i in range(2)]
                W["Hh"] = [sb(st2, "Hh%d" % i, [128, NITER + 1], F32) for i in range(2)]
                W["LO"] = [sb(st2, "LO%d" % i, [128, NITER + 1], F32) for i in range(2)]
                W["mid"] = [sb(st2, "mid%d" % i, [128, NITER + 2], F32) for i in range(2)]
                W["TA"] = [sb(st2, "TA%d" % i, [128, NITER + 1], F32) for i in range(2)]
                W["jD"] = [sb(st2, "jD%d" % i, [128, 2], F32) for i in range(2)]
                W["jA"] = [sb(st2, "jA%d" % i, [128, 2], F32) for i in range(2)]
                W["SA"] = [sb(st2, "SA%d" % i, [128, NITER], F32) for i in range(2)]
                if ji == 0:
                    NB = TP // 128
                    blocks = [QB_(b, b * 128, 128, (b + 1) * 128, cfg.topk_p, True, I4p) for b in range(NB)]
                else:
                    blocks = [QB_(0, TP + (ji - 1) * TS, TS, cfg.LS, cfg.topk_s, False, I4s)]
                NB = len(blocks)
                blocks[0].early_loads()
                blocks[0].late_loads()
                run(blocks[0].gen_I())
                for b in range(NB):
                    gl = []
                    if b + 1 < NB:
                        blocks[b + 1].early_loads()
                    nT = (NITER + 2) if blocks[b].LK > blocks[b].topk else 1
                    gl.append((blocks[b].gen_T(), nT))
                    if b >= 1:
                        gl.append((blocks[b - 1].gen_A(), 2 * ((blocks[b - 1].LK + 127) // 128 + 1)))
                    if b + 1 < NB:
                        gl.append((blocks[b + 1].gen_I(), 8 * ((blocks[b + 1].LK + 511) // 512)))
                    interleave_n(gl)
                    if b + 1 < NB:
                        blocks[b + 1].late_loads()
                run(blocks[NB - 1].gen_A())
                k.barrier()
        k.barrier()

    TW3 = 512
    with ExitStack() as st:
        wab = sb(st, "wab", [128, 8, D], BF16)
        wbb = sb(st, "wbb", [128, 8, D], BF16)
        wob = sb(st, "wob", [128, 8, D], BF16)
        wdw = sb(st, "wdw", [128, 8, 31], F32)
        vec = sb(st, "vec", [128, 3, 8], F32)
        gfb = sb(st, "gfb", [128, D], F32)
        onesf = sb(st, "onesf", [128, 128], F32)
        gate_p = sb(st, "gate_p", [128, D], F32)
        gate_s = [sb(st, "gate_s%d" % i, [TS, D], F32) for i in range(NS)]
        k.dma(gate_p[:], G_p.t, [G_p], [gate_p])
        for i in range(NS):
            k.dma(gate_s[i][:], G_s[i].t, [G_s[i]], [gate_s[i]])
        k.do(k.dve, "memset", [], [onesf], onesf[:], 1.0 / D)
        k.dma(wdw[:], wdw_d.rearrange("p (a b) -> p a b", b=31), [], [wdw])
        k.dma(vec[:], vec_d.rearrange("p (a b) -> p a b", b=8), [], [vec])
        k.dma(gfb[:], gf_d, [], [gfb])
        with ExitStack() as st2:
            wl = [sb(st2, "wl%d" % i, [128, 2 * D], F32) for i in range(2)]
            li = 0
            for (src, dst) in [(wa_d, wab), (wb_d, wbb), (wo_d, wob)]:
                for q4 in range(4):
                    w_ = wl[li % 2]
                    li += 1
                    k.dma(w_[:], src[:, q4 * 2 * D:(q4 + 1) * 2 * D], [], [w_])
                    k.do(k.pool if li % 2 else k.dve, "tensor_copy", [w_], [dst], out=dst[:, 2 * q4:2 * q4 + 2, :], in_=w_[:].rearrange("p (a b) -> p a b", b=D))
            k.barrier()
        uext = [sb(st, "uext%d" % i, [128, 8, 30 + TW3], BF16) for i in range(2)]
        gbt = [sb(st, "gbt%d" % i, [128, 8, TW3], BF16) for i in range(1)]
        mat = [sb(st, "mat%d" % i, [128, 8, TW3], BF16) for i in range(1)]
        mbt = [sb(st, "mbt%d" % i, [128, 8, TW3], BF16) for i in range(1)]
        ogt = [sb(st, "ogt%d" % i, [128, 8, TW3], BF16) for i in range(1)]
        sstg = sb(st, "sstg", [128, 8, 30], F32)
        acc = [sb(st, "acc%d" % i, [128, TW3], F32) for i in range(8)]
        sq = [sb(st, "sq%d" % i, [128, TW3], F32) for i in range(2)]
        diag = [sb(st, "diag%d" % i, [128, 128], BF16) for i in range(6)]
        mean_sb = sb(st, "mean_sb", [128, TW3], F32)
        rstd_bc = sb(st, "rstd_bc", [128, TW3], F32)
        t1 = [sb(st, "t1_%d" % i, [128, TW3], F32) for i in range(2)]
        t2 = [sb(st, "t2_%d" % i, [128, TW3], F32) for i in range(2)]
        ybin = sb(st, "ybin", [128, 8, TW3], BF16)
        mixed = sb(st, "mixed", [128, 8, TW3], BF16)
        m1 = [sb(st, "m1_%d" % i, [128, TW3], F32) for i in range(2)]
        m2 = [sb(st, "m2_%d" % i, [128, TW3], F32) for i in range(2)]
        xt3 = [sb(st, "xt3_%d" % i, [128, D], F32) for i in range(2)]
        xn = [sb(st, "xn%d" % i, [128, D], F32) for i in range(2)]
        yt = [sb(st, "yt%d" % i, [128, D], F32) for i in range(2)]
        junk3 = sb(st, "junk3", [128, D], F32)
        st3 = [sb(st, "st3_%d" % i, [128, 4], F32) for i in range(2)]
        psM = ps(st, "psM", [128, 512], F32)
        psQ = ps(st, "psQ", [128, 512], F32)
        psA = [ps(st, "psA%d" % i, [128, 512], F32) for i in range(1)]
        psB = [ps(st, "psB%d" % i, [128, 512], F32) for i in range(1)]
        psC = [ps(st, "psC%d" % i, [128, 512], F32) for i in range(2)]
        psY = [ps(st, "psY%d" % i, [128, 512], F32) for i in range(2)]
        cnt = {"t": 0, "c": 0, "y": 0, "m": 0}

        def tile3(tok0, TW, first, sidx, gate):
            i = cnt["t"]
            cnt["t"] += 1
            ue, gb_, ma_, mb_, og_ = uext[i % 2], gbt[0], mat[0], mbt[0], ogt[0]
            if first:
                if sidx is None:
                    k.do(k.pool, "memset", [], [ue], ue[:, :, 0:30], 0.0)
                else:
                    k.dma(sstg[:], sconv_d[sidx].rearrange("p (a b) -> p a b", b=30), [], [sstg])
                    k.do(k.pool, "tensor_copy", [sstg], [ue], out=ue[:, :, 0:30], in_=sstg[:])
                k.dma(ue[:, :, 30:30 + TW], fm_view(S["ua"], tok0, TW), [S["ua"]], [ue])
            else:
                k.dma(ue[:, :, 0:30 + TW], fm_view(S["ua"], tok0 - 30, TW + 30), [S["ua"]], [ue])
            k.dma(gb_[:, :, 0:TW], fm_view(S["gb"], tok0, TW), [S["gb"]], [gb_])
            k.dma(ma_[:, :, 0:TW], fm_view(S["ma"], tok0, TW), [S["ma"]], [ma_])
            k.dma(mb_[:, :, 0:TW], fm_view(S["mb"], tok0, TW), [S["mb"]], [mb_])
            k.dma(og_[:, :, 0:TW], fm_view(S["og"], tok0, TW), [S["og"]], [og_])
            for c in range(8):
                pcv = psC[c % 2]
                for j in range(31):
                    dg = diag[cnt["c"] % 6]
                    cnt["c"] += 1
                    k.do(k.dve, "tensor_scalar", [identf, wdw], [dg], out=dg[:], in0=identf[:], scalar1=wdw[:, c, j:j + 1], scalar2=None, op0=ALU.mult)
                    k.do(k.pe, "matmul", [dg, ue], [pcv], pcv[:, 0:TW], dg[:], ue[:, c, j:j + TW], start=(j == 0), stop=(j == 30))
                k.do(k.act, "activation", [pcv, vec], [acc[c]], out=acc[c][:, 0:TW], in_=pcv[:, 0:TW], func=AF.Identity, bias=vec[:, 0, c:c + 1])
            for c in range(8):
                s_ = sq[c % 2]
                k.do(k.act, "activation", [acc[c]], [s_], out=s_[:, 0:TW], in_=acc[c][:, 0:TW], func=AF.Square)
                k.do(k.pe, "matmul", [onesf, acc[c]], [psM], psM[:, 0:TW], onesf[:], acc[c][:, 0:TW], start=(c == 0), stop=(c == 7), sig=False)
                k.do(k.pe, "matmul", [onesf, s_], [psQ], psQ[:, 0:TW], onesf[:], s_[:, 0:TW], start=(c == 0), stop=(c == 7), sig=True)
            k.do(k.act, "copy", [psM], [mean_sb], out=mean_sb[:, 0:TW], in_=psM[:, 0:TW])
            k.do(k.dve, "tensor_tensor", [mean_sb], [rstd_bc], out=rstd_bc[:, 0:TW], in0=mean_sb[:, 0:TW], in1=mean_sb[:, 0:TW], op=ALU.mult)
            k.do(k.dve, "tensor_tensor", [psQ, rstd_bc], [rstd_bc], out=rstd_bc[:, 0:TW], in0=psQ[:, 0:TW], in1=rstd_bc[:, 0:TW], op=ALU.subtract)
            k.do(k.act, "activation", [rstd_bc, epsT], [rstd_bc], out=rstd_bc[:, 0:TW], in_=rstd_bc[:, 0:TW], func=AF.Sqrt, bias=epsT[:, 0:1])
            k.do(k.dve, "reciprocal", [rstd_bc], [rstd_bc], out=rstd_bc[:, 0:TW], in_=rstd_bc[:, 0:TW])
            for c in range(8):
                a_, b_ = t1[c % 2], t2[c % 2]
                k.do(k.dve, "tensor_tensor", [acc[c], mean_sb], [a_], out=a_[:, 0:TW], in0=acc[c][:, 0:TW], in1=mean_sb[:, 0:TW], op=ALU.subtract)
                k.do(k.dve, "tensor_tensor", [a_, rstd_bc], [a_], out=a_[:, 0:TW], in0=a_[:, 0:TW], in1=rstd_bc[:, 0:TW], op=ALU.mult)
                k.do(k.act, "activation", [a_, vec], [b_], out=b_[:, 0:TW], in_=a_[:, 0:TW], func=AF.Silu, scale=vec[:, 1, c:c + 1], bias=vec[:, 2, c:c + 1])
                k.do(k.dve, "tensor_tensor", [b_, gb_], [ybin], out=ybin[:, c, 0:TW], in0=b_[:, 0:TW], in1=gb_[:, c, 0:TW], op=ALU.mult)
            for cc in range(8):
                pa, pb = psA[0], psB[0]
                for kc in range(8):
                    k.do(k.pe, "matmul", [wab, og_], [pa], pa[:, 0:TW], wab[:, kc, cc * 128:(cc + 1) * 128], og_[:, kc, 0:TW], start=(kc == 0), stop=(kc == 7), sig=(kc == 7))
                for kc in range(8):
                    k.do(k.pe, "matmul", [wbb, ybin], [pb], pb[:, 0:TW], wbb[:, kc, cc * 128:(cc + 1) * 128], ybin[:, kc, 0:TW], start=(kc == 0), stop=(kc == 7), sig=(kc == 7))
                a_, b_ = m1[cc % 2], m2[cc % 2]
                k.do(k.dve, "tensor_tensor", [pa, ma_], [a_], out=a_[:, 0:TW], in0=pa[:, 0:TW], in1=ma_[:, cc, 0:TW], op=ALU.mult)
                k.do(k.dve, "tensor_tensor", [pb, mb_], [b_], out=b_[:, 0:TW], in0=pb[:, 0:TW], in1=mb_[:, cc, 0:TW], op=ALU.mult)
                k.do(k.pool, "tensor_tensor", [a_, b_], [mixed], out=mixed[:, cc, 0:TW], in0=a_[:, 0:TW], in1=b_[:, 0:TW], op=ALU.add)
            for s0 in range(0, TW, 128):
                rows = min(128, TW - s0)
                yi = cnt["y"]
                cnt["y"] += 1
                x_, xn_, y_, s_ = xt3[yi % 2], xn[yi % 2], yt[yi % 2], st3[yi % 2]
                k.dma(x_[0:rows, :], x_all[tok0 + s0:tok0 + s0 + rows, :], [], [x_])
                for half in range(2):
                    py = psY[half]
                    for kc in range(8):
                        k.do(k.pe, "matmul", [mixed, wob], [py], py[0:rows, :], mixed[:, kc, s0:s0 + rows], wob[:, kc, half * 512:(half + 1) * 512], start=(kc == 0), stop=(kc == 7), sig=(kc == 7))
                    hs = slice(half * 512, half * 512 + 512)
                    k.do(k.dve, "tensor_tensor", [py, gate], [xn_], out=xn_[0:rows, hs], in0=py[0:rows, :], in1=gate[0:rows, hs], op=ALU.mult)
                k.do(k.pool, "tensor_tensor", [xn_, x_], [xn_], out=xn_[0:rows, :], in0=xn_[0:rows, :], in1=x_[0:rows, :], op=ALU.add)
                k.do(k.act, "activation", [xn_], [junk3, s_], out=junk3[0:rows, :], in_=xn_[0:rows, :], func=AF.Square, accum_out=s_[0:rows, 0:1])
                k.do(k.act, "activation", [s_, epsT], [s_], out=s_[0:rows, 1:2], in_=s_[0:rows, 0:1], func=AF.Sqrt, scale=1.0 / D, bias=epsT[0:rows, 0:1])
                k.do(k.dve, "reciprocal", [s_], [s_], out=s_[0:rows, 2:3], in_=s_[0:rows, 1:2])
                k.do(k.dve, "scalar_tensor_tensor", [xn_, s_, gfb], [y_], out=y_[0:rows, :], in0=xn_[0:rows, :], scalar=s_[0:rows, 2:3], in1=gfb[0:rows, :], op0=ALU.mult, op1=ALU.mult)
                k.dma(y_all.t[tok0 + s0:tok0 + s0 + rows, :], y_[0:rows, :], [y_], [y_all], E=k.pool)

        for t0 in range(0, TP, TW3):
            tile3(t0, min(TW3, TP - t0), t0 == 0, None, gate_p)
        for s in range(NS):
            tile3(TP + s * TS, TS, True, s, gate_s[s])
        k.barrier()
    k.final_wait()
    P0.close()
    k.stack0.close()
    return nc, k


def _consts():
    c = np.zeros((128, 128 + 192 + 512 + 128), np.float32)
    c[:, 0:128] = np.eye(128, dtype=np.float32)
    c[0, 128:256] = 1.0
    c[1, 256:288] = 1.0
    c[2, 288:320] = 1.0
    for r in range(4):
        c[:, 320 + r * 128:320 + (r + 1) * 128] = np.eye(128, dtype=np.float32)
        c[0:32, 832 + r * 32:832 + (r + 1) * 32] = np.eye(32, dtype=np.float32)
    return c


def _kc_layout(w):
    n = w.shape[1]
    return np.ascontiguousarray(w.reshape(8, 128, n).transpose(1, 0, 2)).reshape(128, 8 * n)


def _fmT(v, n):
    return np.ascontiguousarray(v.reshape(n, 128).T)


def make_in_maps(cfg, x_prompt, x_sample, cache_k, cache_v, cache_idx_k, state_conv, c_prompt, c_sample,
                 w_ada, b_ada, g_norm, w_in, w_a, w_dw, b_dw, g_ln, b_ln, w_b, w_out, g_final, n_cores):
    f = lambda a: np.ascontiguousarray(np.asarray(a, dtype=np.float32))
    w_in0 = f(w_in[0])
    wada = _kc_layout(f(w_ada[0])).reshape(128, 8, 3072)
    bada = np.ascontiguousarray(np.broadcast_to(f(b_ada[0])[None, :], (3, 3072)))
    gn_bc = np.ascontiguousarray(np.broadcast_to(f(g_norm[0])[None, :], (128, D)))
    gf_bc = np.ascontiguousarray(np.broadcast_to(f(g_final)[None, :], (128, D)))
    wfm = np.stack([_kc_layout(w_in0[:, FM_OFF[kd] + c * 128: FM_OFF[kd] + (c + 1) * 128]) for kd in FM_KINDS for c in range(8)])
    wiq = np.stack([_kc_layout(w_in0[:, O_IQ + h * 64: O_IQ + (h + 1) * 64]) for h in range(8)])
    wkv = _kc_layout(np.concatenate([w_in0[:, O_K:O_K + 512], w_in0[:, O_IK:O_IK + 64], w_in0[:, O_IW:O_IW + 8]], axis=1))
    wa, wb, wo = _kc_layout(f(w_a[0])), _kc_layout(f(w_b[0])), _kc_layout(f(w_out[0]))
    wdwT = np.ascontiguousarray(f(w_dw[0]).T.reshape(8, 128, 31).transpose(1, 0, 2)).reshape(128, 8 * 31)
    vecT = np.concatenate([_fmT(f(b_dw[0]), 8), _fmT(f(g_ln[0]), 8), _fmT(f(b_ln[0]), 8)], axis=1)
    consts = _consts()
    NS, TS = cfg.NS, cfg.TS
    maps = []
    for i in range(n_cores):
        ss = slice(i * NS, (i + 1) * NS)
        xs = f(x_sample[ss]).reshape(NS * TS, D)
        x_all = np.concatenate([f(x_prompt[i]), xs], axis=0)
        cc = np.concatenate([f(c_prompt[i])[None], f(c_sample[ss])], axis=0)
        cT = np.ascontiguousarray(cc.reshape(3, 8, 128).transpose(2, 1, 0))
        sconv = f(state_conv[0][ss])
        sconvT = np.ascontiguousarray(sconv.reshape(NS, 30, 8, 128).transpose(0, 3, 2, 1)).reshape(NS, 128, 240)
        maps.append({
            "x_all": x_all, "cT": cT, "wada": wada, "bada": bada, "gn_bc": gn_bc, "wfm": wfm, "wiq": wiq, "wkv": wkv,
            "wa": wa, "wb": wb, "wo": wo, "wdwT": wdwT, "vecT": vecT, "gf_bc": gf_bc,
            "ck": f(cache_k[0][ss]).reshape(NS, cfg.PAST, 256), "cv": f(cache_v[0][ss]).reshape(NS, cfg.PAST, 256),
            "cik": f(cache_idx_k[0][ss]), "sconvT": sconvT, "consts": consts,
        })
    return maps


def assemble(cfg, results, n_cores):
    TP, NS, TS = cfg.TP, cfg.NS, cfg.TS
    g = lambda name: [np.asarray(r[name]) for r in results]
    y, kk, vv, ik, cv = g("y_all"), g("k_all"), g("v_all"), g("ik_all"), g("conv_o")
    y_p = np.stack([a[:TP] for a in y])
    y_s = np.concatenate([a[TP:].reshape(NS, TS, D) for a in y])
    k_p = np.stack([a[:TP].reshape(TP, 2, 128) for a in kk])[None]
    v_p = np.stack([a[:TP].reshape(TP, 2, 128) for a in vv])[None]
    ik_p = np.stack([a[:TP] for a in ik])[None]
    c_p = np.stack([a[0] for a in cv])[None]
    k_s = np.concatenate([a[TP:].reshape(NS, TS, 2, 128) for a in kk])[None]
    v_s = np.concatenate([a[TP:].reshape(NS, TS, 2, 128) for a in vv])[None]
    ik_s = np.concatenate([a[TP:].reshape(NS, TS, 64) for a in ik])[None]
    c_s = np.concatenate([a[1:] for a in cv])[None]
    return tuple(np.ascontiguousarray(a, dtype=np.float32) for a in (y_p, y_s, k_p, v_p, ik_p, c_p, k_s, v_s, ik_s, c_s))


def kernel(**inputs):
    cfg = Cfg()
    n = 8
    nc, _ = build(cfg)
    maps = make_in_maps(cfg, n_cores=n, **inputs)
    res = run_bass_kernel_spmd(nc, maps, core_ids=list(range(n)))
    return assemble(cfg, res.results, n)
```
